# Optimizing a Trainium2 kernel written in Bass

```python
import jax, jax.numpy as jnp
from jax import lax
import numpy as np

D_MODEL = 2048
BATCH = 4
SEQ = 2048
DEPTH = 1
DEC_BATCH = 128
DEC_SEQ = 8
PAST_LEN = 16384
PAGE_SIZE = 128

N_META = 16
W_A = D_MODEL // 2
HEAD_A = 64
H_A = W_A // HEAD_A
LORA_W = 64
LORA_A = 64
W_B = D_MODEL - W_A
H_B = 8
DK_B = W_B // H_B
CHUNK = 128
N_A_COLS = 4 * W_A + LORA_W + LORA_A
N_B_COLS = 4 * W_B
N_IN = N_A_COLS + N_B_COLS
ALPHA = (2.0 * DEPTH) ** 0.25
BETA = (8.0 * DEPTH) ** -0.25
ROPE_BASE = 10000.0
LN_EPS = 1e-5
GN_EPS_A = 64e-5
GN_EPS_B = 1e-5

kernel_name = "hymba_rwkv7_retention_deepnorm_step"


def _layer_norm(x, g, b):
    xf = x.astype(jnp.float32)
    mu = jnp.mean(xf, -1, keepdims=True)
    var = jnp.mean(jnp.square(xf - mu), -1, keepdims=True)
    return ((xf - mu) * lax.rsqrt(var + LN_EPS) * g.astype(jnp.float32) + b.astype(jnp.float32)).astype(x.dtype)


def _group_norm(x, n_heads, g, b, eps):
    shp = x.shape
    xh = x.astype(jnp.float32).reshape(shp[:-1] + (n_heads, shp[-1] // n_heads))
    mu = jnp.mean(xh, -1, keepdims=True)
    var = jnp.mean(jnp.square(xh - mu), -1, keepdims=True)
    xh = ((xh - mu) * lax.rsqrt(var + eps)).reshape(shp)
    return xh * g.astype(jnp.float32) + b.astype(jnp.float32)


def _rotary(x, pos):
    half = x.shape[-1] // 2
    inv = ROPE_BASE ** (-jnp.arange(half, dtype=jnp.float32) / half)
    ang = pos.astype(jnp.float32)[:, None] * inv[None, :]
    cos = jnp.cos(ang)[None, :, None, :]
    sin = jnp.sin(ang)[None, :, None, :]
    x1, x2 = x[..., :half], x[..., half:]
    return jnp.concatenate([x1 * cos - x2 * sin, x1 * sin + x2 * cos], axis=-1)


def _rwkv7_mixer(za, shift_in, s0, shift_mix, w0, w_up, a0, a_up, k_k, k_a, r_k, gn_g, gn_b):
    B, T, _ = za.shape
    f32 = jnp.float32
    za = za.astype(f32)
    prev = jnp.concatenate([shift_in.astype(f32)[:, None, :], za[:, :-1]], axis=1)
    zs = za + (prev - za) * shift_mix.astype(f32)
    r, k, v, g, zw, zx = jnp.split(zs, [W_A, 2 * W_A, 3 * W_A, 4 * W_A, 4 * W_A + LORA_W], axis=-1)
    w_logit = w0.astype(f32) + jnp.tanh(zw) @ w_up.astype(f32)
    log_w = -jnp.exp(-jax.nn.softplus(-w_logit) - 0.5)
    a = jax.nn.sigmoid(a0.astype(f32) + zx @ a_up.astype(f32))
    hd = lambda t: t.reshape(B, T, H_A, HEAD_A)
    kk = hd(k * k_k.astype(f32))
    kk = kk / jnp.maximum(jnp.sqrt(jnp.sum(kk * kk, -1, keepdims=True)), 1e-12)
    k = k * (1.0 + (a - 1.0) * k_a.astype(f32))
    rh, kh, vh, ah, wh = hd(r), hd(k), hd(v), hd(a), jnp.exp(hd(log_w))

    def step(S, inp):
        r_t, w_t, k_t, v_t, kk_t, b_t = inp
        sa = jnp.einsum('bhvk,bhk->bhv', S, -kk_t)
        S = S * w_t[:, :, None, :] + sa[..., None] * b_t[:, :, None, :] + v_t[..., None] * k_t[:, :, None, :]
        return S, jnp.einsum('bhvk,bhk->bhv', S, r_t)

    xs = tuple(jnp.swapaxes(t, 0, 1) for t in (rh, wh, kh, vh, kk, kk * ah))
    s_T, o = lax.scan(step, s0.astype(f32), xs)
    o = _group_norm(jnp.swapaxes(o, 0, 1).reshape(B, T, W_A), H_A, gn_g, gn_b, GN_EPS_A)
    bonus = (jnp.sum(rh * kh * r_k.astype(f32), -1, keepdims=True) * vh).reshape(B, T, W_A)
    return (o + bonus) * jax.nn.silu(g), s_T, za[:, -1]


def _retention_block(S, q, k, v, log_g):
    C = q.shape[1]
    i = jnp.arange(C, dtype=jnp.float32)
    diff = i[:, None] - i[None, :]
    dmask = jnp.where(diff >= 0, jnp.exp(jnp.maximum(diff, 0.0)[None] * log_g[:, None, None]), 0.0)
    scores = jnp.einsum('bihd,bjhd->bhij', q, k) * dmask[None]
    inner = jnp.einsum('bhij,bjhe->bihe', scores, v)
    q_dec = q * jnp.exp((i + 1.0)[:, None] * log_g[None, :])[None, :, :, None]
    cross = jnp.einsum('bihd,bhde->bihe', q_dec, S)
    k_dec = k * jnp.exp((C - 1.0 - i)[:, None] * log_g[None, :])[None, :, :, None]
    S = jnp.exp(C * log_g)[None, :, None, None] * S + jnp.einsum('bjhd,bjhe->bhde', k_dec, v)
    return S, inner + cross


def _retention_blocks(S, q, k, v, log_g, block):
    B, T, H, _ = q.shape
    n = T // block
    xs = tuple(jnp.swapaxes(t.reshape(B, n, block, H, t.shape[-1]), 0, 1) for t in (q, k, v))
    S, o = lax.scan(lambda s, qkv: _retention_block(s, qkv[0], qkv[1], qkv[2], log_g), S, xs)
    return S, jnp.swapaxes(o, 0, 1).reshape(B, T, H, -1)


def _retention_mixer(zb, pos, s0, lead, block, gn_g, gn_b):
    B, T, _ = zb.shape
    f32 = jnp.float32
    q, k, v, g = jnp.split(zb.astype(f32), 4, axis=-1)
    hd = lambda t: t.reshape(B, T, H_B, DK_B)
    q = _rotary(hd(q), pos)
    k = _rotary(hd(k), pos) * (DK_B ** -0.5)
    v = hd(v)
    log_g = jnp.log1p(-jnp.exp2(-5.0 - jnp.arange(H_B, dtype=f32)))
    S = s0.astype(f32)
    outs = []
    if lead > 0:
        S, o_lead = _retention_block(S, q[:, :lead], k[:, :lead], v[:, :lead], log_g)
        outs.append(o_lead)
    S, o_rest = _retention_blocks(S, q[:, lead:], k[:, lead:], v[:, lead:], log_g, block)
    outs.append(o_rest)
    o = jnp.concatenate(outs, axis=1).reshape(B, T, W_B)
    o = _group_norm(o, H_B, gn_g, gn_b, GN_EPS_B)
    return o * jax.nn.silu(g), S


def _layer(x, pos, shift_in, s_a, s_b, lead, block, w_in, w_out, shift_mix, w0, w_up, a0, a_up,
           k_k, k_a, r_k, gn_a_g, gn_a_b, gn_b_g, gn_b_b, ln_g, ln_b):
    z = jnp.einsum('btd,dn->btn', x, w_in)
    o_a, s_a, shift = _rwkv7_mixer(z[..., :N_A_COLS], shift_in, s_a, shift_mix, w0, w_up, a0, a_up,
                                   k_k, k_a, r_k, gn_a_g, gn_a_b)
    o_b, s_b = _retention_mixer(z[..., N_A_COLS:], pos, s_b, lead, block, gn_b_g, gn_b_b)
    h = jnp.einsum('btc,cd->btd', jnp.concatenate([o_a, o_b], axis=-1).astype(x.dtype), w_out)
    y = _layer_norm(ALPHA * x + h, ln_g, ln_b)
    return y, s_a, shift, s_b


def _run_trunk(x, pos, s_rwkv, s_shift, s_ret, lead, block, w_in, w_out, shift_mix, w0, w_up, a0, a_up,
               k_k, k_a, r_k, gn_a_g, gn_a_b, gn_b_g, gn_b_b, ln_g, ln_b):
    new_rwkv, new_shift, new_ret = [], [], []
    for l in range(DEPTH):
        x, sa, sh, sb = _layer(x, pos, s_shift[l], s_rwkv[l], s_ret[l], lead, block, w_in[l], w_out[l],
                               shift_mix[l], w0[l], w_up[l], a0[l], a_up[l], k_k[l], k_a[l], r_k[l],
                               gn_a_g[l], gn_a_b[l], gn_b_g[l], gn_b_b[l], ln_g[l], ln_b[l])
        new_rwkv.append(sa.astype(x.dtype))
        new_shift.append(sh.astype(x.dtype))
        new_ret.append(sb.astype(x.dtype))
    return x, jnp.stack(new_rwkv), jnp.stack(new_shift), jnp.stack(new_ret)


def setup_inputs(seed: int = 0) -> dict:
    key = jax.random.key(seed)
    ks = jax.random.split(key, 24)
    f32 = jnp.float32
    nrm = lambda kk, shape, s: s * jax.random.normal(kk, shape, f32)
    return {
        'x_prompt': nrm(ks[0], (BATCH, SEQ, D_MODEL), 1.0),
        'x_sample': nrm(ks[1], (DEC_BATCH, DEC_SEQ, D_MODEL), 1.0),
        'state_rwkv': nrm(ks[2], (DEPTH, DEC_BATCH, H_A, HEAD_A, HEAD_A), 0.5),
        'state_shift': nrm(ks[3], (DEPTH, DEC_BATCH, N_A_COLS), 1.0),
        'state_ret': nrm(ks[4], (DEPTH, DEC_BATCH, H_B, DK_B, DK_B), 1.0),
        'meta_tokens': nrm(ks[5], (N_META, D_MODEL), 1.0),
        'w_in': nrm(ks[6], (DEPTH, D_MODEL, N_IN), D_MODEL ** -0.5),
        'w_out': nrm(ks[7], (DEPTH, W_A + W_B, D_MODEL), BETA * (W_A + W_B) ** -0.5),
        'shift_mix': jax.random.uniform(ks[8], (DEPTH, N_A_COLS), f32),
        'w0': jax.random.uniform(ks[9], (DEPTH, W_A), f32, -6.5, -1.5),
        'w_up': nrm(ks[10], (DEPTH, LORA_W, W_A), 0.5 * LORA_W ** -0.5),
        'a0': nrm(ks[11], (DEPTH, W_A), 0.1),
        'a_up': nrm(ks[12], (DEPTH, LORA_A, W_A), 0.5 * LORA_A ** -0.5),
        'k_k': 0.85 + nrm(ks[13], (DEPTH, W_A), 0.02),
        'k_a': 1.0 + nrm(ks[14], (DEPTH, W_A), 0.02),
        'r_k': nrm(ks[15], (DEPTH, H_A, HEAD_A), 0.1),
        'gn_a_g': 1.0 + nrm(ks[16], (DEPTH, W_A), 0.01),
        'gn_a_b': nrm(ks[17], (DEPTH, W_A), 0.01),
        'gn_b_g': 1.0 + nrm(ks[18], (DEPTH, W_B), 0.01),
        'gn_b_b': nrm(ks[19], (DEPTH, W_B), 0.01),
        'ln_g': 1.0 + nrm(ks[20], (DEPTH, D_MODEL), 0.01),
        'ln_b': nrm(ks[21], (DEPTH, D_MODEL), 0.01),
    }


def reference(x_prompt, x_sample, state_rwkv, state_shift, state_ret, meta_tokens, w_in, w_out, shift_mix,
              w0, w_up, a0, a_up, k_k, k_a, r_k, gn_a_g, gn_a_b, gn_b_g, gn_b_b, ln_g, ln_b):
    weights = (w_in, w_out, shift_mix, w0, w_up, a0, a_up, k_k, k_a, r_k, gn_a_g, gn_a_b, gn_b_g, gn_b_b, ln_g, ln_b)
    dt = x_prompt.dtype
    bp = x_prompt.shape[0]
    meta = jnp.broadcast_to(meta_tokens.astype(dt)[None], (bp, N_META, D_MODEL))
    xp = jnp.concatenate([meta, x_prompt], axis=1)
    pos_p = jnp.arange(xp.shape[1], dtype=jnp.int32)
    z_rwkv = jnp.zeros((DEPTH, bp, H_A, HEAD_A, HEAD_A), dt)
    z_shift = jnp.zeros((DEPTH, bp, N_A_COLS), dt)
    z_ret = jnp.zeros((DEPTH, bp, H_B, DK_B, DK_B), dt)
    yp, rwkv_p, shift_p, ret_p = _run_trunk(xp, pos_p, z_rwkv, z_shift, z_ret, N_META, CHUNK, *weights)
    y_prompt = yp[:, N_META:]
    pos_s = PAST_LEN + jnp.arange(x_sample.shape[1], dtype=jnp.int32)
    y_sample, rwkv_s, shift_s, ret_s = _run_trunk(x_sample, pos_s, state_rwkv, state_shift, state_ret,
                                                  0, x_sample.shape[1], *weights)
    return (y_prompt, y_sample, rwkv_p, shift_p, ret_p, rwkv_s, shift_s, ret_s)
```

```python
import contextlib
import numpy as np
import concourse.bass as bass
import concourse.mybir as mybir
from concourse.bass_utils import run_bass_kernel_spmd

F32 = mybir.dt.float32
BF16 = mybir.dt.bfloat16
AF = mybir.ActivationFunctionType
ALU = mybir.AluOpType

D = 2048
NT = 18
TPS = 3
NST = NT // TPS
STN = TPS * 128
NA = 4224
NB = 4096
NIN = NA + NB
NBLK_A = 33
SAME_ENGINE_SYNC = True


class Buf:
    def __init__(self, t, name):
        self.t = t
        self.name = name
        self.w = None
        self.r = {}
        self.dsem = None
        self.dcnt = 0
        self.psum = False

    def __getitem__(self, idx):
        return V(self, self.t[idx])


class V:
    def __init__(self, buf, ap):
        self.buf = buf
        self.ap = ap


def _ap(x):
    return x.ap if isinstance(x, V) else x


class Ctx:
    ENG = ["pe", "act", "dve", "pool", "sp"]

    def __init__(self, nc, stack):
        self.nc = nc
        self.stack = stack
        self.prog = {e: [] for e in self.ENG}
        self.cnt = {e: 0 for e in self.ENG}
        self.sem = {e: stack.enter_context(nc.semaphore("sem_" + e)) for e in self.ENG}
        self.seen = {e: {} for e in self.ENG}
        self.nbuf = 0
        self.dma_sems = []

    def sbuf(self, shape, dt, name=None):
        self.nbuf += 1
        name = name or ("sb%d" % self.nbuf)
        t = self.stack.enter_context(self.nc.sbuf_tensor(name, list(shape), dt))
        return Buf(t, name)

    def psum(self, shape, dt, name=None):
        self.nbuf += 1
        name = name or ("ps%d" % self.nbuf)
        t = self.stack.enter_context(self.nc.psum_tensor(name, list(shape), dt))
        bf = Buf(t, name)
        bf.psum = True
        return bf

    def _need(self, e, dep, waits):
        if dep is None:
            return
        kind, key, val, sem = dep
        if kind == "eng" and key == e:
            if not SAME_ENGINE_SYNC or e in ("pe", "sp"):
                return
        if self.seen[e].get(key, 0) >= val:
            return
        waits[key] = (sem, max(val, waits.get(key, (None, 0))[1]))

    def _deps(self, e, reads, writes, skip_same_pe=True):
        waits = {}
        for b in reads:
            self._need(e, b.w, waits)
            if b.psum:
                for k_, d in b.r.items():
                    if k_ != e:
                        self._need(e, d, waits)
        for b in writes:
            self._need(e, b.w, waits)
            for d in b.r.values():
                self._need(e, d, waits)
        out = []
        for key, (sem, val) in waits.items():
            self.seen[e][key] = val
            out.append((sem, val))
        return out

    def op(self, e, fn, reads, writes):
        reads = [x.buf for x in reads if isinstance(x, V)]
        writes = [x.buf for x in writes if isinstance(x, V)]
        waits = self._deps(e, reads, writes)
        self.cnt[e] += 1
        n = self.cnt[e]
        sem = self.sem[e]

        def emit(eng):
            for (s, v) in waits:
                eng.wait_ge(s, v)
            fn(eng).then_inc(sem, 1)
        self.prog[e].append(emit)
        dep = ("eng", e, n, sem)
        for b in reads:
            b.r[e] = dep
        for b in writes:
            b.w = dep
            b.r = {}

    def dma(self, q, out, in_, extra_waits=(), **kw):
        reads = [in_.buf] if isinstance(in_, V) else []
        writes = [out.buf] if isinstance(out, V) else []
        waits = self._deps(q, reads, writes) + list(extra_waits)
        tb = (writes + reads)[0]
        if tb.dsem is None:
            tb.dsem = self.stack.enter_context(self.nc.semaphore("dsem_" + tb.name))
            self.dma_sems.append(tb)
        tb.dcnt += 16
        val = tb.dcnt
        sem = tb.dsem
        o, i = _ap(out), _ap(in_)

        def emit(eng):
            for (s, v) in waits:
                eng.wait_ge(s, v)
            eng.dma_start(out=o, in_=i, **kw).then_inc(sem, 16)
        self.prog[q].append(emit)
        dep = ("dma", "d_" + tb.name, val, sem)
        for b in reads:
            b.r["d_" + tb.name] = dep
        for b in writes:
            b.w = dep
            b.r = {}

    def barrier(self):
        targets = [(self.sem[e], self.cnt[e], e) for e in self.ENG if self.cnt[e] > 0]
        dmas = [(b.dsem, b.dcnt, "d_" + b.name) for b in self.dma_sems]
        for e in self.ENG:
            ws = [(s, v) for (s, v, k) in targets if k != e] + [(s, v) for (s, v, k) in dmas]

            def emit(eng, ws=ws):
                for (s, v) in ws:
                    eng.wait_ge(s, v)
            self.prog[e].append(emit)
            for (s, v, k) in targets + dmas:
                if k != e:
                    self.seen[e][k] = max(self.seen[e].get(k, 0), v)

    def mm(self, out, lhsT, rhs, start=True, stop=True):
        o, l, r = out.ap, lhsT.ap, rhs.ap
        self.op("pe", lambda eng: eng.matmul(o, l, r, start=start, stop=stop), [lhsT, rhs], [out])

    def tr(self, out, in_, ident):
        o, i, d = out.ap, in_.ap, ident.ap
        self.op("pe", lambda eng: eng.transpose(o, i, d), [in_, ident], [out])

    def act(self, out, in_, func, bias=None, scale=1.0):
        o, i = out.ap, in_.ap
        rd = [in_]
        kw = {}
        if bias is not None:
            kw["bias"] = _ap(bias)
            if isinstance(bias, V):
                rd.append(bias)
        if isinstance(scale, V):
            rd.append(scale)
        sc = _ap(scale)
        self.op("act", lambda eng: eng.activation(o, i, func, scale=sc, **kw), rd, [out])

    def tt(self, e, out, a, b, op):
        o, x, y = out.ap, a.ap, b.ap
        self.op(e, lambda eng: eng.tensor_tensor(o, x, y, op), [a, b], [out])

    def ts(self, e, out, a, s1, s2, op0, op1=None):
        o, x = out.ap, a.ap
        rd = [a] + [s for s in (s1, s2) if isinstance(s, V)]
        a1, a2 = _ap(s1), _ap(s2)
        if op1 is None:
            self.op(e, lambda eng: eng.tensor_scalar(o, x, a1, None, op0), rd, [out])
        else:
            self.op(e, lambda eng: eng.tensor_scalar(o, x, a1, a2, op0, op1), rd, [out])

    def stt(self, out, in0, scalar, in1, op0, op1):
        o, x, y = out.ap, in0.ap, in1.ap
        rd = [in0, in1] + ([scalar] if isinstance(scalar, V) else [])
        s = _ap(scalar)
        self.op("dve", lambda eng: eng.scalar_tensor_tensor(o, x, s, y, op0, op1), rd, [out])

    def cp(self, e, out, in_):
        o, i = out.ap, in_.ap
        if e == "act":
            self.op(e, lambda eng: eng.copy(o, i), [in_], [out])
        else:
            self.op(e, lambda eng: eng.tensor_copy(o, i), [in_], [out])

    def memset(self, e, out, val):
        o = out.ap
        self.op(e, lambda eng: eng.memset(o, val), [], [out])

    def finish(self):
        finals = [(b.dsem, b.dcnt) for b in self.dma_sems]
        engs = [(e, self.sem[e], self.cnt[e]) for e in self.ENG if e != "sp" and self.cnt[e] > 0]

        def emit(eng):
            for (s, v) in finals:
                eng.wait_ge(s, v)
            for (_, s, v) in engs:
                eng.wait_ge(s, v)
        self.prog["sp"].append(emit)
        prog = self.prog
        with self.nc.Block() as block:
            @block.tensor
            def _(eng):
                for f in prog["pe"]:
                    f(eng)

            @block.scalar
            def _(eng):
                for f in prog["act"]:
                    f(eng)

            @block.vector
            def _(eng):
                for f in prog["dve"]:
                    f(eng)

            @block.gpsimd
            def _(eng):
                for f in prog["pool"]:
                    f(eng)

            @block.sync
            def _(eng):
                for f in prog["sp"]:
                    f(eng)


ALPHA = 2.0 ** 0.25
LN_EPS = 1e-5
GN_EPS_A = 64e-5
GN_EPS_B = 1e-5
LOG_G = [float(np.log1p(-np.exp2(np.float32(-5.0 - h)))) for h in range(8)]
import os
RW_LEVEL = int(os.environ.get("RW_LEVEL", "4"))
SL = int(os.environ.get("SL", "4"))
PIPE = int(os.environ.get("PIPE", "1"))

PV_MIX = 0
PV_W0 = 33
PV_A0 = 41
PV_KK = 49
PV_KA = 57
PV_RK = 65
PV_GAG = 73
PV_GAB = 81
PV_GBG = 89
PV_GBB = 97
PV_DECK = 105
PV_BM = 121
NPV = 137


def build():
    nc = bass.Bass("TRN2", target_bir_lowering=False)

    def din(name, shape, dt=F32):
        return nc.dram_tensor(name, list(shape), dt, kind="ExternalInput").ap()

    def dout(name, shape, dt=F32):
        return nc.dram_tensor(name, list(shape), dt, kind="ExternalOutput").ap()

    xall = din("xall", [NT * 128, D])
    w_in = din("w_in", [D, NIN])
    w_out = din("w_out", [D, D])
    ident_f = din("ident_f", [128, 128])
    prot_f = din("prot_f", [128, 128])
    cos_d = din("cos_d", [128, NT * 128])
    sin_d = din("sin_d", [128, NT * 128])
    dmask_d = din("dmask_d", [128, 2 * 8 * 128])
    decq_d = din("decq_d", [128, 2 * 8 * 128])
    pv_d = din("pv_d", [128, NPV])
    lng_d = din("lng_d", [1, D])
    lnb_d = din("lnb_d", [1, D])
    sret_d = din("sret_d", [16, 8, 128, 128])
    bones_d = din("bones_d", [128, 128])
    maskT4_d = din("maskT4_d", [128, 1024])
    maskN_d = din("maskN_d", [128, 256])
    scanm_d = din("scanm_d", [128, 256])
    lora_d = din("lora_d", [128, 1024])
    rwkv_p = dout("rwkv_p", [16, 64, 64])
    srwkv_d = din("srwkv_d", [16, 16, 64, 64])
    sshift_d = din("sshift_d", [16, NA])
    rwkv_s = dout("rwkv_s", [16, 16, 64, 64])
    wbf = nc.dram_tensor("wbf", [33, 128, 16, 256], BF16, kind="Internal").ap()
    wobf = nc.dram_tensor("wobf", [8, 128, 16, 256], BF16, kind="Internal").ap()
    shift_p = dout("shift_p", [1, NA])
    shift_s = dout("shift_s", [16, NA])
    ret_p = dout("ret_p", [8, 128, 128])
    ret_s = dout("ret_s", [16, 8, 128, 128])
    yp_d = dout("yp", [2048, D])
    ys_d = dout("ys", [128, D])

    with contextlib.ExitStack() as stack:
        cx = Ctx(nc, stack)
        CDEC = float(np.exp(-0.5))
        xf = cx.sbuf([128, D], F32, "xf")
        xb = cx.sbuf([128, D], BF16, "xb")
        xT = cx.sbuf([128, 16, STN], BF16, "xT")
        ZW = 17 * STN
        zA = cx.sbuf([128, ZW], F32, "zA")
        zv = zA.t[:, :].rearrange("p (b t) -> p b t", b=17)
        hv = zA.t[:, 0:3 * D].rearrange("p (j d) -> p j d", j=3)
        wsl = [cx.sbuf([128, 16, 256], BF16, "wsl%d" % i) for i in range(2)]
        oT = cx.sbuf([128, 16, STN], BF16, "oT")
        ps = [cx.psum([128, 512], F32, "ps%d" % i) for i in range(7)]
        ps_tb = cx.psum([128, 1024], BF16, "pstb")

        idf = cx.sbuf([128, 128], F32, "idf")
        idb = cx.sbuf([128, 128], BF16, "idb")
        protb = cx.sbuf([128, 128], BF16, "protb")
        onesf = cx.sbuf([128, 128], F32, "onesf")
        bonesf = cx.sbuf([128, 128], F32, "bonesf")
        bonesb = cx.sbuf([128, 128], BF16, "bonesb")
        pv = cx.sbuf([128, NPV], F32, "pv")
        omm = cx.sbuf([128, 33], F32, "omm")
        dmask = cx.sbuf([128, 2, 4, 128], F32, "dmask")
        decq = cx.sbuf([128, 2, 4, 128], F32, "decq")
        maskT4 = cx.sbuf([128, 2, 512], BF16, "maskT4")
        maskN = cx.sbuf([128, 2, 128], BF16, "maskN")
        scanm = cx.sbuf([128, 2, 128], F32, "scanm")
        lorab = cx.sbuf([128, 1024], BF16, "lorab")
        epsb = cx.sbuf([128, 4], F32, "epsb")
        cx.dma("sp", idf[:, :], ident_f)
        cx.cp("dve", idb[:, :], idf[:, :])
        cx.dma("sp", xf[:, 0:128], prot_f)
        cx.cp("dve", protb[:, :], xf[:, 0:128])
        cx.dma("sp", bonesf[:, :], bones_d)
        cx.cp("dve", bonesb[:, :], bonesf[:, :])
        cx.memset("dve", onesf[:, :], 1.0)
        cx.memset("dve", epsb[:, 0:1], GN_EPS_B)
        cx.memset("dve", epsb[:, 1:2], LN_EPS)
        cx.memset("dve", epsb[:, 2:3], GN_EPS_A)
        cx.memset("dve", epsb[:, 3:4], 1e-18)
        cx.dma("sp", pv[:, :], pv_d)
        cx.ts("dve", omm[:, :], pv[:, PV_MIX:PV_MIX + 33], -1.0, 1.0, ALU.mult, ALU.add)
        cx.dma("sp", xf[:, 0:1024], maskT4_d)
        cx.cp("dve", V(maskT4, maskT4.t[:, :, :].rearrange("p a t -> p (a t)")), xf[:, 0:1024])
        cx.dma("sp", xf[:, 0:256], maskN_d)
        cx.cp("dve", V(maskN, maskN.t[:, :, :].rearrange("p a t -> p (a t)")), xf[:, 0:256])
        cx.dma("sp", V(scanm, scanm.t[:, :, :].rearrange("p a t -> p (a t)")), scanm_d)
        cx.dma("sp", xf[:, 0:1024], lora_d)
        cx.cp("dve", lorab[:, :], xf[:, 0:1024])

        vtB = cx.sbuf([128, 3, 512], BF16, "vtB")
        cosb = cx.sbuf([128, STN], F32, "cosb")
        sinb = cx.sbuf([128, STN], F32, "sinb")
        qr = cx.sbuf([128, 4, STN], BF16, "qr")
        kr = cx.sbuf([128, 4, STN], BF16, "kr")
        zb16 = cx.sbuf([128, STN], BF16, "zb16")
        t2 = cx.sbuf([128, STN], F32, "t2")
        ktok = cx.sbuf([128, 512], BF16, "ktok")
        scm = cx.sbuf([128, 512], BF16, "scm")
        qdec = cx.sbuf([128, 4, 128], BF16, "qdec")
        vdec = cx.sbuf([128, 4, 128], BF16, "vdec")
        Sret = cx.sbuf([128, 8, 128], F32, "Sret")
        Sretb = cx.sbuf([128, 8, 128], BF16, "Sretb")
        Rg = cx.sbuf([128, 5120], F32, "Rg").t

        def rview(a, n, dt, name, pat=None, **kw):
            ap = Rg[:, a:a + n]
            if dt == BF16:
                ap = ap.bitcast(BF16)
            if pat is not None:
                ap = ap.rearrange(pat, **kw)
            return Buf(ap, name)
        SS = rview(0, 1024, F32, "SS", "p (s e) -> p s e", s=8)
        SSn = rview(1024, 1024, F32, "SSn", "p (s e) -> p s e", s=8)
        Nat = rview(2048, 1024, F32, "Nat", "p (s e) -> p s e", s=8)
        SSb = rview(3072, 1024, BF16, "SSb", "p (s e) -> p s e", s=16)
        vexp = rview(4096, 512, BF16, "vexp", "p (s e) -> p s e", s=8)
        Vexp2 = rview(4608, 512, BF16, "Vexp2", "p (s e) -> p s e", s=8)
        Of = cx.sbuf([128, 512], F32, "Of")
        Osq = cx.sbuf([128, 512], F32, "Osq")
        gm = cx.sbuf([128, 512], F32, "gm")
        gq = cx.sbuf([128, 512], F32, "gq")
        gs = cx.sbuf([128, 512], F32, "gs")
        lnst = cx.sbuf([128, 8], F32, "lnst")
        carry = cx.sbuf([128, 33], F32, "carry")
        rawp = cx.sbuf([128, 33], F32, "rawp")
        raws = cx.sbuf([128, 33, 16], F32, "raws")
        zc1 = cx.sbuf([128, STN], F32, "zc1")
        t1 = zc1
        lnbuf = xf
        rowb = Of
        rowp = gm
        twzx = cx.sbuf([128, STN], BF16, "twzx")
        AT = cx.sbuf([128, 4, 128], BF16, "AT")
        BT = cx.sbuf([128, 4, 128], BF16, "BT")
        KT = cx.sbuf([128, 4, 128], BF16, "KT")
        RT = cx.sbuf([128, 4, 128], BF16, "RT")
        vb = cx.sbuf([128, 4, 128], BF16, "vb")
        bonus = cx.sbuf([128, 512], F32, "bonus")
        S1 = cx.sbuf([128, 128], F32, "S1")
        A1 = cx.sbuf([128, 128], F32, "A1")
        CN = cx.sbuf([128, 128], F32, "CN")
        E1 = cx.sbuf([128, 128], F32, "E1")
        E2 = cx.sbuf([128, 128], F32, "E2")
        E3 = cx.sbuf([128, 128], F32, "E3")
        KK = cx.sbuf([128, 128], F32, "KK")
        RN = cx.sbuf([128, 128], F32, "RN")
        T1 = cx.sbuf([128, 128], F32, "T1")
        KP = cx.sbuf([128, 128], F32, "KP")
        KK2b = cx.sbuf([128, 128], BF16, "KK2b")
        RKb = cx.sbuf([128, 128], BF16, "RKb")
        Gam = cx.sbuf([128, 4], F32, "Gam")
        Btok = cx.sbuf([128, 512], BF16, "Btok")
        Ktok = cx.sbuf([128, 512], BF16, "Ktok")
        Vtok2 = cx.sbuf([128, 512], BF16, "Vtok2")
        Vpad = cx.sbuf([128, 8, 128], BF16, "Vpad")
        Xpad = cx.sbuf([128, 8, 128], BF16, "Xpad")
        Upad = cx.sbuf([128, 8, 128], BF16, "Upad")
        Upair = cx.sbuf([128, 4, 128], BF16, "Upair")
        LT = [cx.sbuf([128, 512], BF16, "LT%d" % i) for i in range(8)]
        AA0 = [cx.sbuf([128, 128], BF16, "AA0_%d" % i) for i in range(8)]
        AAM = [[cx.sbuf([128, 384], BF16, "AAMx%d_%d" % (p_, i)) for i in range(8)] for p_ in range(2)]
        XTb = cx.sbuf([128, 128], BF16, "XTb")
        prevS = cx.sbuf([128, 128], F32, "prevS")
        GamS = cx.sbuf([128, 4, 16], F32, "GamS")
        stT = cx.sbuf([128, 33, 16], F32, "stT")
        Hf = cx.sbuf([128, 8, 128], F32, "Hf")
        Hb = cx.sbuf([128, 8, 128], BF16, "Hb")
        bdm = bonesf

        cx.memset("dve", Sret[:, :, :], 0.0)
        cx.memset("dve", Sretb[:, :, :], 0.0)
        cx.memset("pool", oT[:, :, :], 0.0)
        cx.memset("dve", carry[:, :], 0.0)
        cx.memset("pool", Vpad[:, :, :], 0.0)
        cx.memset("pool", Xpad[:, :, :], 0.0)
        cx.memset("pool", Upad[:, :, :], 0.0)
        cx.memset("dve", Hf[:, :, :], 0.0)
        cx.memset("dve", Hb[:, :, :], 0.0)
        SETS = [(AT, BT, KT, RT, vb, bonus, Gam, Btok, Ktok, Vtok2, Vpad)]
        if PIPE:
            SETS.append((
                rview(0, 256, BF16, "AT1", "p (b t) -> p b t", b=4),
                rview(256, 256, BF16, "BT1", "p (b t) -> p b t", b=4),
                rview(512, 256, BF16, "KT1", "p (b t) -> p b t", b=4),
                rview(768, 256, BF16, "RT1", "p (b t) -> p b t", b=4),
                rview(1024, 256, BF16, "vb1", "p (b t) -> p b t", b=4),
                rview(1280, 512, F32, "bonus1"),
                rview(1792, 4, F32, "Gam1"),
                rview(1800, 256, BF16, "Btok1"),
                rview(2056, 256, BF16, "Ktok1"),
                rview(2312, 256, BF16, "Vtok21"),
                rview(2568, 512, BF16, "Vpad1", "p (h c) -> p h c", h=8),
            ))
            cx.memset("pool", SETS[1][10][:, :, :], 0.0)
        else:
            cx.memset("pool", Nat[:, :, :], 0.0)
        for grp in range(9):
            nb = 4 if grp < 8 else 1
            cx.dma("sp", Of[0:16, 0:nb * 128], sshift_d[:, grp * 512:grp * 512 + nb * 128])
            for k in range(nb):
                cx.tr(ps[0][:, k * 16:(k + 1) * 16], Of[0:16, k * 128:(k + 1) * 128], idf[0:16, 0:16])
            cx.cp("dve", V(stT, stT.t[:, grp * 4:grp * 4 + nb, :].rearrange("p b s -> p (b s)")), ps[0][:, 0:nb * 16])

        w_view = w_in.rearrange("(c p) n -> p c n", p=128)
        wo_view = w_out.rearrange("(c p) n -> p c n", p=128)
        dmask_v = dmask_d.rearrange("p (a h t) -> p a h t", a=2, h=8)
        decq_v = decq_d.rearrange("p (a h t) -> p a h t", a=2, h=8)
        state = {"w": 0, "ev": 0, "ps": 0}

        wgran = {}

        def gran_id(which, c0):
            if which == "out":
                return 100 + c0 // 256
            if c0 < 4096:
                return c0 // 256
            if c0 == 4096:
                return 16
            return 17 + (c0 - NA) // 256

        def convert(which, c0, ncol, extra=()):
            gid = gran_id(which, c0)
            pb = Buf(None, "wg%d" % gid)
            if which == "out":
                dst = wobf[c0 // 256]
                src = wo_view[:, :, c0:c0 + ncol]
            else:
                dst = wbf[gid]
                src = w_view[:, :, c0:c0 + ncol]
            cx.dma("pool", V(pb, dst[:, :, 0:ncol]), src, extra_waits=list(extra))
            wgran[gid] = (pb, dst)

        conv_order = []
        for hg in range(2):
            for typ in range(4):
                for q in range(2):
                    conv_order.append(("in", typ * 1024 + hg * 512 + q * 256, 256))
            if hg == 0:
                conv_order.append(("in", 4096, 128))
            for off in (0, 1024, 3072, 2048):
                for q in range(2):
                    conv_order.append(("in", NA + hg * 512 + off + q * 256, 256))
        for g8i in range(8):
            conv_order.append(("out", g8i * 256, 256))
        conv_idx = {gran_id(w_, c_): i for i, (w_, c_, n_) in enumerate(conv_order)}
        conv_state = {"done": 0}

        def ensure_converted(upto):
            upto = min(upto, len(conv_order) - 1)
            while conv_state["done"] <= upto:
                w_, c_, n_ = conv_order[conv_state["done"]]
                extra = []
                if conv_state["done"] >= 4 and cx.cnt["pe"] > 0:
                    extra = [(cx.sem["pe"], cx.cnt["pe"])]
                convert(w_, c_, n_, extra)
                conv_state["done"] += 1

        ensure_converted(3)

        def load_w(which, c0, ncol):
            slot = wsl[state["w"] % 2]
            state["w"] += 1
            ensure_converted(conv_idx[gran_id(which, c0)] + 3)
            pb, src = wgran[gran_id(which, c0)]
            cx.dma("sp", slot[:, :, 0:ncol], V(pb, src[:, :, 0:ncol]))
            return slot

        def evac_eng():
            state["ev"] += 1
            return "act" if state["ev"] % 2 == 0 else "dve"

        def next_ps():
            state["ps"] += 1
            return ps[state["ps"] % 2]

        def proj_blocks(c0, nblk, evac_fn):
            done = 0
            while done < nblk:
                nb = min(2, nblk - done)
                slot = load_w("in", c0 + done * 128, nb * 128)
                for cb in range(nb):
                    p = next_ps()
                    for c in range(16):
                        cx.mm(p[:, 0:STN], slot[:, c, cb * 128:(cb + 1) * 128], xT[:, c, :],
                              start=(c == 0), stop=(c == 15))
                    evac_fn(done + cb, p)
                done += nb

        def mix_evac(st, gblk, dblk, p):
            mixc = pv[:, PV_MIX + gblk:PV_MIX + gblk + 1]
            d = zv[:, dblk, :]
            cx.act(zc1[:, :], p[:, 0:STN], AF.Identity, scale=omm[:, gblk:gblk + 1])
            npr = STN if st < NST - 1 else 256
            cx.stt(V(zA, d[:, 1:npr]), p[:, 0:npr - 1], mixc, zc1[:, 1:npr], ALU.mult, ALU.add)
            cx.stt(V(zA, d[:, 0:1]), carry[:, gblk:gblk + 1], mixc, zc1[:, 0:1], ALU.mult, ALU.add)
            if st == NST - 1 and SL >= 1:
                d3 = d[:, 256:384].rearrange("p (s t) -> p s t", t=8)
                p3 = p.t[:, 256:384].rearrange("p (s t) -> p s t", t=8)
                z3 = zc1.t[:, 256:384].rearrange("p (s t) -> p s t", t=8)
                pS3 = prevS.t[:, :].rearrange("p (s t) -> p s t", t=8)
                cx.cp("act", V(prevS, pS3[:, :, 1:8]), V(p, p3[:, :, 0:7]))
                cx.cp("dve", V(prevS, pS3[:, :, 0]), stT[:, gblk, :])
                cx.stt(V(zA, d[:, 256:384]), prevS[:, :], mixc, zc1[:, 256:384], ALU.mult, ALU.add)
            cx.cp("act", carry[:, gblk:gblk + 1], p[:, STN - 1:STN])
            if st == NST - 1:
                cx.cp("act", rawp[:, gblk:gblk + 1], p[:, 255:256])
                cx.cp("act", raws[:, gblk, :], p[:, 263:384:8])

        def load_H(sh, gb, want_b):
            for e in range(2):
                cx.dma("sp", Nat[e * 64:(e + 1) * 64, :, e * 64:(e + 1) * 64],
                       srwkv_d[sh * 8:(sh + 1) * 8, 2 * gb + e].rearrange("s v k -> v s k"))
            for q in range(2):
                pT = ps[4 + q]
                for i4 in range(4):
                    cx.tr(pT[:, i4 * 128:(i4 + 1) * 128], Nat[:, q * 4 + i4, :], idf[:, :])
                if want_b:
                    o0 = sh * 8 + q * 4
                    cx.cp("act" if q == 0 else "dve",
                          V(SSb, SSb.t[:, o0:o0 + 4, :].rearrange("p s e -> p (s e)")), pT[:, :])
                else:
                    cx.cp("act" if q == 0 else "dve",
                          V(SS, SS.t[:, q * 4:(q + 1) * 4, :].rearrange("p s e -> p (s e)")), pT[:, :])

        def chain_sample(hg, b, cur):
            gb = hg * 4 + b
            he, ho = 2 * b, 2 * b + 1
            bs = slice(b * 128, (b + 1) * 128)
            pXT = ps[2]
            load_H(0, gb, True)
            load_H(1, gb, True)
            cx.mm(pXT[:, 0:128], Vpad[:, he, :], LT[he][:, 256:384], start=True, stop=False)
            cx.mm(pXT[:, 0:128], Vpad[:, ho, :], LT[ho][:, 256:384], start=False, stop=False)
            for i in range(16):
                cx.mm(pXT[:, 8 * i:8 * i + 8], SSb[:, i, :], AT[:, b, 8 * i:8 * i + 8],
                      start=False, stop=(i == 15))
            cx.cp("act", XTb[:, :], pXT[:, 0:128])
            if SL < 4:
                return
            cx.tr(ps_tb[:, 0:128], XTb[:, :], idb[:, :])
            cx.cp("act", Xpad[:, he, 0:64], ps_tb[:, 0:64])
            cx.cp("dve", Xpad[:, ho, 64:128], ps_tb[:, 64:128])
            pU = ps[3]
            cx.mm(pU[:, 0:128], idb[:, :], Xpad[:, he, :], start=True, stop=False)
            cx.mm(pU[:, 0:128], idb[:, :], Xpad[:, ho, :], start=False, stop=False)
            cx.mm(pU[:, 0:128], cur[he][2], Xpad[:, he, :], start=False, stop=False)
            cx.mm(pU[:, 0:128], cur[ho][2], Xpad[:, ho, :], start=False, stop=True)
            cx.cp("act", Upad[:, he, 0:64], pU[:, 0:64])
            cx.cp("dve", Upad[:, ho, 64:128], pU[:, 64:128])
            cx.cp("act", Upair[:, b, :], pU[:, 0:128])
            pO = ps[6][:, bs]
            cx.mm(pO, Upad[:, he, :], LT[he][:, 128:256], start=True, stop=False)
            cx.mm(pO, Upad[:, ho, :], LT[ho][:, 128:256], start=False, stop=False)
            cx.mm(pO, Vpad[:, he, :], LT[he][:, 384:512], start=False, stop=False)
            cx.mm(pO, Vpad[:, ho, :], LT[ho][:, 384:512], start=False, stop=False)
            for i in range(16):
                cx.mm(ps[6][:, b * 128 + 8 * i:b * 128 + 8 * i + 8], SSb[:, i, :], RT[:, b, 8 * i:8 * i + 8],
                      start=False, stop=(i == 15))
            for sh in range(2):
                load_H(sh, gb, False)
                bm = pv.t[:, PV_BM + sh * 8:PV_BM + sh * 8 + 8]
                bmb = V(pv, bm.unsqueeze(2).to_broadcast([128, 8, 128]))
                cx.tt("dve", vexp[:, :, :],
                      V(Upair, Upair.t[:, b, :].unsqueeze(1).to_broadcast([128, 8, 128])), bmb, ALU.mult)
                cx.tt("dve", Vexp2[:, :, :],
                      V(Vtok2, Vtok2.t[:, bs].unsqueeze(1).to_broadcast([128, 8, 128])), bmb, ALU.mult)
                for q in range(2):
                    pp = ps[2 + q]
                    qs = slice(q * 4, (q + 1) * 4)
                    cx.mm(pp[:, :], Btok[:, bs], V(vexp, vexp.t[:, qs, :].rearrange("p s e -> p (s e)")),
                          start=True, stop=False)
                    cx.mm(pp[:, :], Ktok[:, bs], V(Vexp2, Vexp2.t[:, qs, :].rearrange("p s e -> p (s e)")),
                          start=False, stop=True)
                    sn = V(SSn, SSn.t[:, qs, :])
                    cx.tt("dve", sn, V(pp, pp.t[:, :].rearrange("p (s e) -> p s e", s=4)),
                          V(bonesf, bonesf.t[:, :].unsqueeze(1).to_broadcast([128, 4, 128])), ALU.mult)
                    cx.tt("dve", sn, sn, V(SS, SS.t[:, qs, :]), ALU.add)
                    gsl = GamS.t[:, b, sh * 8 + q * 4:sh * 8 + q * 4 + 4]
                    cx.tt("dve", sn, sn, V(GamS, gsl.unsqueeze(2).to_broadcast([128, 4, 128])), ALU.mult)
                for q in range(2):
                    pT = ps[4 + q]
                    for i4 in range(4):
                        cx.tr(pT[:, i4 * 128:(i4 + 1) * 128], SSn[:, q * 4 + i4, :], idf[:, :])
                    cx.cp("act" if q == 0 else "dve",
                          V(SS, SS.t[:, q * 4:(q + 1) * 4, :].rearrange("p s e -> p (s e)")), pT[:, :])
                for e in range(2):
                    cx.dma("sp", rwkv_s[sh * 8:(sh + 1) * 8, 2 * gb + e].rearrange("s v k -> v s k"),
                           SS[e * 64:(e + 1) * 64, :, e * 64:(e + 1) * 64])

        def rw_prep(st, hg, j, s):
            t = st * TPS + j
            samp = 1 if t == NT - 1 else 0
            if RW_LEVEL < 1 or (samp and SL < 2):
                return
            cs = slice(j * 128, (j + 1) * 128)
            AT, BT, KT, RT, vb, bonus, Gam, Btok, Ktok, Vtok2, Vpad = SETS[s]
            for b in range(4):
                gb = hg * 4 + b
                pc = lambda off: pv[:, off + gb:off + gb + 1]
                cx.mm(ps[0][:, 0:128], lorab[0:64, gb * 128:(gb + 1) * 128], twzx[0:64, cs])
                cx.mm(ps[1][:, 256:384], lorab[64:128, gb * 128:(gb + 1) * 128], twzx[64:128, cs])
                cx.act(Of[:, b * 128:(b + 1) * 128], ps[0][:, 0:128], AF.Sigmoid, bias=pc(PV_W0))
                cx.act(Osq[:, b * 128:(b + 1) * 128], ps[1][:, 256:384], AF.Sigmoid, bias=pc(PV_A0))
            yield
            for b in range(4):
                gb = hg * 4 + b
                r_ = V(zA, zv[:, b, cs])
                k_ = V(zA, zv[:, 4 + b, cs])
                v_ = V(zA, zv[:, 8 + b, cs])
                pc = lambda off: pv[:, off + gb:off + gb + 1]
                S1 = V(Of, Of.t[:, b * 128:(b + 1) * 128])
                A1 = V(Osq, Osq.t[:, b * 128:(b + 1) * 128])
                cx.op("dve", lambda eng, o=CN.t[:, :], d0=scanm.t[:, samp, :], d1=S1.ap:
                      eng.tensor_tensor_scan(o, d0, d1, 0.0, ALU.mult, ALU.add),
                      [scanm[:, samp, :], S1], [CN[:, :]])
                cx.act(E1[:, :], CN[:, :], AF.Exp, scale=-CDEC)
                cx.act(E2[:, :], CN[:, :], AF.Exp, scale=CDEC)
                cx.tt("dve", RN[:, :], CN[:, :], S1, ALU.subtract)
                cx.act(E3[:, :], RN[:, :], AF.Exp, scale=-CDEC)
                if samp:
                    cx.cp("act", GamS[:, b, :], E1[:, 7:128:8])
                else:
                    cx.cp("act", Gam[:, b:b + 1], E1[:, 127:128])
                cx.ts("dve", KK[:, :], k_, pc(PV_KK), None, ALU.mult)
                cx.tt("dve", KK2b[:, :], KK[:, :], KK[:, :], ALU.mult)
                cx.mm(ps[1][:, 0:128], bonesb[:, :], KK2b[:, :])
                cx.act(RN[:, :], ps[1][:, 0:128], AF.Ln, bias=epsb[:, 3:4])
                cx.act(RN[:, :], RN[:, :], AF.Exp, scale=-0.5)
                cx.tt("dve", KK[:, :], KK[:, :], RN[:, :], ALU.mult)
                cx.ts("dve", T1[:, :], A1, 1.0, pc(PV_KA), ALU.subtract, ALU.mult)
                cx.stt(KP[:, :], T1[:, :], 1.0, k_, ALU.add, ALU.mult)
                cx.stt(AT[:, b, :], KK[:, :], -1.0, E3[:, :], ALU.mult, ALU.mult)
                cx.tt("dve", T1[:, :], KK[:, :], A1, ALU.mult)
                cx.tt("dve", BT[:, b, :], T1[:, :], E2[:, :], ALU.mult)
                cx.tt("dve", KT[:, b, :], KP[:, :], E2[:, :], ALU.mult)
                cx.tt("dve", RT[:, b, :], r_, E1[:, :], ALU.mult)
                cx.stt(RKb[:, :], r_, pc(PV_RK), KP[:, :], ALU.mult, ALU.mult)
                cx.mm(ps[1][:, 128:256], bonesb[:, :], RKb[:, :])
                cx.cp("act", vb[:, b, :], v_)
                cx.tt("dve", bonus[:, b * 128:(b + 1) * 128], ps[1][:, 128:256], v_, ALU.mult)
                yield
            yield
            for (src, dst) in ((BT, Btok), (KT, Ktok), (vb, Vtok2)):
                for b in range(4):
                    cx.tr(ps_tb[:, b * 128:(b + 1) * 128], src[:, b, :], idb[:, :])
                cx.cp("act", dst[:, :], ps_tb[:, 0:512])
            vt4 = Vtok2.t[:, :].rearrange("p (b c) -> p b c", b=4)
            vp4 = Vpad.t[:, :, :].rearrange("p (b e) c -> p b e c", e=2)
            cx.cp("dve", V(Vpad, vp4[:, :, 0, 0:64]), V(Vtok2, vt4[:, :, 0:64]))
            cx.cp("dve", V(Vpad, vp4[:, :, 1, 64:128]), V(Vtok2, vt4[:, :, 64:128]))
            yield

        def rw_core(st, hg, j, s):
            t = st * TPS + j
            samp = 1 if t == NT - 1 else 0
            if RW_LEVEL < 2 or (samp and SL < 2):
                return
            cs = slice(j * 128, (j + 1) * 128)
            AT, BT, KT, RT, vb, bonus, Gam, Btok, Ktok, Vtok2, Vpad = SETS[s]
            for hl in range(8):
                b = hl // 2
                pr = slice((hl % 2) * 64, (hl % 2) * 64 + 64)
                pL = ps[4 + (hl % 2)]
                cx.mm(pL[:, 0:128], BT[pr, b, :], AT[pr, b, :])
                cx.mm(pL[:, 128:256], BT[pr, b, :], RT[pr, b, :])
                cx.mm(pL[:, 256:384], KT[pr, b, :], AT[pr, b, :])
                cx.mm(pL[:, 384:512], KT[pr, b, :], RT[pr, b, :])
                cx.tt("dve", LT[hl][:, :], pL[:, :], maskT4[:, samp, :], ALU.mult)
                pN = ps[2 + (hl % 2)]
                cx.mm(pN[:, 0:128], AT[pr, b, :], BT[pr, b, :])
                cx.tt("dve", AA0[hl][:, :], pN[:, 0:128], maskN[:, samp, :], ALU.mult)
                yield
            if RW_LEVEL < 3:
                return
            cur = [(AA0[hl][:, :], LT[hl][:, 0:128], LT[hl][:, 0:128]) for hl in range(8)]
            nl = 2 if samp else 6
            for rnd in range(1, nl + 2):
                for hl in range(8):
                    Ac, ATc, Mc = cur[hl]
                    pA = ps[2 + ((hl + 3 * rnd) % 5)]
                    pM = pA
                    do_sq = rnd <= nl
                    do_m = rnd >= 2
                    if do_sq:
                        cx.mm(pA[:, 0:128], ATc, Ac)
                        cx.mm(pA[:, 128:256], Ac, ATc)
                    if do_m:
                        cx.mm(pM[:, 256:384], idb[:, :], ATc, start=True, stop=False)
                        cx.mm(pM[:, 256:384], Ac, Mc, start=False, stop=True)
                    nA, nAT, nM = Ac, ATc, Mc
                    if do_sq:
                        AAn = AAM[rnd % 2][hl]
                        cx.cp("act", AAn[:, 0:256], pA[:, 0:256])
                        nA, nAT = AAn[:, 0:128], AAn[:, 128:256]
                    if do_m:
                        Mn = AAM[rnd % 2][hl]
                        cx.tt("dve", Mn[:, 256:384], pM[:, 256:384], Mc, ALU.add)
                        nM = Mn[:, 256:384]
                    cur[hl] = (nA, nAT, nM)
                    yield
            if RW_LEVEL < 4:
                return
            if samp:
                for b in range(4):
                    chain_sample(hg, b, cur)
            else:
                g4 = slice(hg * 4, hg * 4 + 4)
                pXa, pUa, pHa = ps[2], ps[3], ps[4]
                x4 = Xpad.t[:, :, :].rearrange("p (b e) c -> p b e c", e=2)
                u4 = Upad.t[:, :, :].rearrange("p (b e) c -> p b e c", e=2)
                for b in range(4):
                    gb = hg * 4 + b
                    he, ho = 2 * b, 2 * b + 1
                    o = pXa[:, b * 128:(b + 1) * 128]
                    cx.mm(o, AT[:, b, :], Hb[:, gb, :], start=True, stop=False)
                    cx.mm(o, LT[he][:, 256:384], Vpad[:, he, :], start=False, stop=False)
                    cx.mm(o, LT[ho][:, 256:384], Vpad[:, ho, :], start=False, stop=True)
                yield
                px4 = pXa.t[:, :].rearrange("p (b c) -> p b c", b=4)
                cx.cp("act", V(Xpad, x4[:, :, 0, 0:64]), V(pXa, px4[:, :, 0:64]))
                cx.cp("act", V(Xpad, x4[:, :, 1, 64:128]), V(pXa, px4[:, :, 64:128]))
                for b in range(4):
                    he, ho = 2 * b, 2 * b + 1
                    o = pUa[:, b * 128:(b + 1) * 128]
                    cx.mm(o, idb[:, :], Xpad[:, he, :], start=True, stop=False)
                    cx.mm(o, idb[:, :], Xpad[:, ho, :], start=False, stop=False)
                    cx.mm(o, cur[he][2], Xpad[:, he, :], start=False, stop=False)
                    cx.mm(o, cur[ho][2], Xpad[:, ho, :], start=False, stop=True)
                yield
                pu4 = pUa.t[:, :].rearrange("p (b c) -> p b c", b=4)
                cx.cp("dve", V(Upad, u4[:, :, 0, 0:64]), V(pUa, pu4[:, :, 0:64]))
                cx.cp("dve", V(Upad, u4[:, :, 1, 64:128]), V(pUa, pu4[:, :, 64:128]))
                cx.cp("dve", V(Upair, Upair.t[:, :, :].rearrange("p b c -> p (b c)")), pUa[:, :])
                for b in range(4):
                    gb = hg * 4 + b
                    he, ho = 2 * b, 2 * b + 1
                    pO = ps[6][:, b * 128:(b + 1) * 128]
                    cx.mm(pO, Hb[:, gb, :], RT[:, b, :], start=True, stop=False)
                    cx.mm(pO, Upad[:, he, :], LT[he][:, 128:256], start=False, stop=False)
                    cx.mm(pO, Upad[:, ho, :], LT[ho][:, 128:256], start=False, stop=False)
                    cx.mm(pO, Vpad[:, he, :], LT[he][:, 384:512], start=False, stop=False)
                    cx.mm(pO, Vpad[:, ho, :], LT[ho][:, 384:512], start=False, stop=True)
                    pH = pHa[:, b * 128:(b + 1) * 128]
                    cx.mm(pH, Btok[:, b * 128:(b + 1) * 128], Upair[:, b, :], start=True, stop=False)
                    cx.mm(pH, Ktok[:, b * 128:(b + 1) * 128], Vtok2[:, b * 128:(b + 1) * 128],
                          start=False, stop=True)
                yield
                tq = V(gq, gq.t[:, :].rearrange("p (b c) -> p b c", b=4))
                cx.tt("dve", tq, V(pHa, pHa.t[:, :].rearrange("p (b c) -> p b c", b=4)),
                      V(bonesf, bonesf.t[:, :].unsqueeze(1).to_broadcast([128, 4, 128])), ALU.mult)
                cx.tt("dve", tq, tq, Hf[:, g4, :], ALU.add)
                cx.tt("dve", Hf[:, g4, :], tq,
                      V(Gam, Gam.t[:, 0:4].unsqueeze(2).to_broadcast([128, 4, 128])), ALU.mult)
                cx.cp("act", Hb[:, g4, :], Hf[:, g4, :])
            yield
            cx.cp("act", Of[:, :], ps[6][:, :])
            cx.act(Osq[:, :], ps[6][:, :], AF.Square)
            cx.mm(ps[5][:, :], bonesf[:, :], Of[:, :])
            cx.ts("dve", gm[:, :], ps[5][:, :], 1.0 / 64.0, None, ALU.mult)
            cx.mm(ps[5][:, :], bonesf[:, :], Osq[:, :])
            cx.tt("dve", gq[:, :], gm[:, :], gm[:, :], ALU.mult)
            cx.stt(gq[:, :], ps[5][:, :], 1.0 / 64.0, gq[:, :], ALU.mult, ALU.subtract)
            cx.act(gq[:, :], gq[:, :], AF.Ln, bias=epsb[:, 2:3])
            cx.act(gq[:, :], gq[:, :], AF.Exp, scale=-0.5)
            cx.tt("dve", Of[:, :], Of[:, :], gm[:, :], ALU.subtract)
            cx.tt("dve", Of[:, :], Of[:, :], gq[:, :], ALU.mult)
            for b in range(4):
                gb = hg * 4 + b
                cx.ts("dve", Of[:, b * 128:(b + 1) * 128], Of[:, b * 128:(b + 1) * 128],
                      pv[:, PV_GAG + gb:PV_GAG + gb + 1], pv[:, PV_GAB + gb:PV_GAB + gb + 1],
                      ALU.mult, ALU.add)
            cx.tt("dve", Of[:, :], Of[:, :], bonus[:, :], ALU.add)
            cx.act(V(gs, gs.t[:, :].rearrange("p (h t) -> p h t", h=4)),
                   V(zA, zv[:, 12:16, cs]), AF.Sigmoid)
            cx.tt("dve", V(gs, gs.t[:, :].rearrange("p (h t) -> p h t", h=4)),
                  V(gs, gs.t[:, :].rearrange("p (h t) -> p h t", h=4)), V(zA, zv[:, 12:16, cs]), ALU.mult)
            cx.tt("dve", oT[:, hg * 4:hg * 4 + 4, cs],
                  V(Of, Of.t[:, :].rearrange("p (h t) -> p h t", h=4)),
                  V(gs, gs.t[:, :].rearrange("p (h t) -> p h t", h=4)), ALU.mult)

            yield

        def run_gens(main, side=None, ratio=6):
            k = 0
            while main is not None or side is not None:
                if main is not None:
                    try:
                        next(main)
                    except StopIteration:
                        main = None
                k += 1
                if side is not None and (main is None or k % ratio == 0):
                    try:
                        next(side)
                    except StopIteration:
                        side = None


        for st in range(NST):
            if st == NST - 1 and PIPE:
                cx.barrier()
                cx.memset("dve", Nat[:, :, :], 0.0)
            for j in range(TPS):
                t = st * TPS + j
                cx.dma("sp", xf[:, :], xall[t * 128:(t + 1) * 128, :])
                cx.cp("act", xb[:, :], xf[:, :])
                for half in range(2):
                    for c in range(8):
                        cc = half * 8 + c
                        cx.tr(ps_tb[:, c * 128:(c + 1) * 128], xb[:, cc * 128:(cc + 1) * 128], idb[:, :])
                    cx.cp("dve" if half == 0 else "act",
                          xT[:, half * 8:(half + 1) * 8, j * 128:(j + 1) * 128],
                          V(ps_tb, ps_tb.t[:, :].rearrange("p (c t) -> p c t", c=8)))
            cx.dma("sp", cosb[:, :], cos_d[:, st * STN:(st + 1) * STN])
            cx.dma("sp", sinb[:, :], sin_d[:, st * STN:(st + 1) * STN])

            for hg in range(2):
                for typ in range(4):
                    proj_blocks(typ * 1024 + hg * 512, 4,
                                lambda i, p, typ=typ: mix_evac(st, typ * 8 + hg * 4 + i, typ * 4 + i, p))
                if hg == 0:
                    proj_blocks(4096, 1, lambda i, p: mix_evac(st, 32, 16, p))
                    cx.act(twzx[0:64, :], V(zA, zv[0:64, 16, :]), AF.Tanh)
                    cx.cp("dve", twzx[64:128, :], V(zA, zv[64:128, 16, :]))
                if st < NST - 1 and PIPE:
                    run_gens(rw_prep(st, hg, 0, 0))
                    run_gens(rw_core(st, hg, 0, 0), rw_prep(st, hg, 1, 1))
                    run_gens(rw_core(st, hg, 1, 1), rw_prep(st, hg, 2, 0))
                    run_gens(rw_core(st, hg, 2, 0))
                else:
                    for j in range(TPS):
                        run_gens(rw_prep(st, hg, j, 0))
                        run_gens(rw_core(st, hg, j, 0))

                hs = slice(hg * 4, hg * 4 + 4)
                cx.dma("sp", dmask[:, :, :, :], dmask_v[:, :, hs, :])
                cx.dma("sp", decq[:, :, :, :], decq_v[:, :, hs, :])
                base = NA + hg * 512
                proj_blocks(base, 4, lambda i, p: cx.cp(evac_eng(), V(zA, zv[:, i, :]), p[:, 0:STN]))
                proj_blocks(base + 1024, 4, lambda i, p: cx.act(V(zA, zv[:, 4 + i, :]), p[:, 0:STN], AF.Copy,
                                                                scale=float(128.0 ** -0.5)))
                proj_blocks(base + 3072, 4, lambda i, p: cx.cp(evac_eng(), V(zA, zv[:, 8 + i, :]), p[:, 0:STN]))
                for half in range(2):
                    slot = load_w("in", base + 2048 + half * 256, 256)
                    for j in range(TPS):
                        p = next_ps()
                        for c in range(16):
                            cx.mm(p[:, 0:256], xT[:, c, j * 128:(j + 1) * 128], slot[:, c, :],
                                  start=(c == 0), stop=(c == 15))
                        cx.cp(evac_eng(), vtB[:, j, half * 256:(half + 1) * 256], p[:, 0:256])
                for hh in range(4):
                    for (sb, dst) in ((0, qr), (4, kr)):
                        src = V(zA, zv[:, sb + hh, :])
                        cx.cp("act", zb16[:, :], src)
                        cx.mm(ps[2][:, 0:STN], protb[:, :], zb16[:, :])
                        cx.tt("dve", t1[:, :], src, cosb[:, :], ALU.mult)
                        cx.tt("dve", t2[:, :], ps[2][:, 0:STN], sinb[:, :], ALU.mult)
                        cx.tt("dve", dst[:, hh, :], t1[:, :], t2[:, :], ALU.add)
                for j in range(TPS):
                    t = st * TPS + j
                    samp = 1 if t == NT - 1 else 0
                    cs = slice(j * 128, (j + 1) * 128)
                    for hh in range(4):
                        cx.tr(ps_tb[:, hh * 128:(hh + 1) * 128], kr[:, hh, cs], idb[:, :])
                    cx.cp("act", ktok[:, :], ps_tb[:, 0:512])
                    for hh in range(4):
                        cx.mm(ps[3][:, hh * 128:(hh + 1) * 128], kr[:, hh, cs], qr[:, hh, cs])
                    cx.tt("dve", scm[:, :], ps[3][:, :],
                          V(dmask, dmask.t[:, samp, :, :].rearrange("p h t -> p (h t)")), ALU.mult)
                    cx.tt("dve", qdec[:, :, :], qr[:, :, cs], decq[:, samp, :, :], ALU.mult)
                    dk = pv.t[:, PV_DECK + samp * 8 + hg * 4:PV_DECK + samp * 8 + hg * 4 + 4]
                    cx.tt("dve", vdec[:, :, :],
                          V(vtB, vtB.t[:, j, :].rearrange("p (h e) -> p h e", h=4)),
                          V(pv, dk.unsqueeze(2).to_broadcast([128, 4, 128])), ALU.mult)
                    for hh in range(4):
                        h = hg * 4 + hh
                        o = ps[4][:, hh * 128:(hh + 1) * 128]
                        if not samp:
                            cx.mm(o, vtB[:, j, hh * 128:(hh + 1) * 128], scm[:, hh * 128:(hh + 1) * 128],
                                  start=True, stop=False)
                            cx.mm(o, Sretb[:, h, :], qdec[:, hh, :], start=False, stop=True)
                            cx.mm(ps[5][:, hh * 128:(hh + 1) * 128], ktok[:, hh * 128:(hh + 1) * 128],
                                  vdec[:, hh, :])
                        else:
                            cx.mm(o, vtB[:, j, hh * 128:(hh + 1) * 128], scm[:, hh * 128:(hh + 1) * 128],
                                  start=True, stop=False)
                            g8 = float(np.exp(np.float32(8.0 * LOG_G[h])))
                            for sh in range(2):
                                cx.dma("sp", SS[:, :, :],
                                       sret_d[sh * 8:(sh + 1) * 8, h].rearrange("s d e -> d s e"))
                                cx.cp("act", SSb[:, 0:8, :], SS[:, :, :])
                                for i8 in range(8):
                                    i = sh * 8 + i8
                                    cx.mm(ps[4][:, hh * 128 + 8 * i:hh * 128 + 8 * i + 8], SSb[:, i8, :],
                                          qdec[:, hh, 8 * i:8 * i + 8], start=False, stop=(i == 15))
                                bm = pv.t[:, PV_BM + sh * 8:PV_BM + sh * 8 + 8]
                                cx.tt("dve", vexp[:, :, :],
                                      V(vdec, vdec.t[:, hh, :].unsqueeze(1).to_broadcast([128, 8, 128])),
                                      V(pv, bm.unsqueeze(2).to_broadcast([128, 8, 128])), ALU.mult)
                                for q4 in range(2):
                                    pp = ps[5 + (q4 % 2)]
                                    cx.mm(pp[:, :], ktok[:, hh * 128:(hh + 1) * 128],
                                          V(vexp, vexp.t[:, q4 * 4:(q4 + 1) * 4, :].rearrange("p s e -> p (s e)")))
                                    cx.stt(V(SSn, SSn.t[:, q4 * 4:(q4 + 1) * 4, :].rearrange("p s e -> p (s e)")),
                                           V(SS, SS.t[:, q4 * 4:(q4 + 1) * 4, :].rearrange("p s e -> p (s e)")),
                                           g8, pp[:, :], ALU.mult, ALU.add)
                                cx.dma("sp", ret_s[sh * 8:(sh + 1) * 8, h].rearrange("s d e -> d s e"), SSn[:, :, :])
                    if not samp:
                        for hh in range(4):
                            h = hg * 4 + hh
                            gC = float(np.exp(np.float32(128.0 * LOG_G[h])))
                            cx.stt(Sret[:, h, :], Sret[:, h, :], gC, ps[5][:, hh * 128:(hh + 1) * 128],
                                   ALU.mult, ALU.add)
                        cx.cp("act", Sretb[:, hs, :], Sret[:, hs, :])
                    cx.cp("act", Of[:, :], ps[4][:, :])
                    cx.act(Osq[:, :], ps[4][:, :], AF.Square)
                    cx.mm(ps[6][:, :], onesf[:, :], Of[:, :])
                    cx.ts("dve", gm[:, :], ps[6][:, :], 1.0 / 128.0, None, ALU.mult)
                    cx.mm(ps[6][:, :], onesf[:, :], Osq[:, :])
                    cx.tt("dve", gq[:, :], gm[:, :], gm[:, :], ALU.mult)
                    cx.stt(gq[:, :], ps[6][:, :], 1.0 / 128.0, gq[:, :], ALU.mult, ALU.subtract)
                    cx.act(gq[:, :], gq[:, :], AF.Ln, bias=epsb[:, 0:1])
                    cx.act(gq[:, :], gq[:, :], AF.Exp, scale=-0.5)
                    cx.tt("dve", Of[:, :], Of[:, :], gm[:, :], ALU.subtract)
                    cx.tt("dve", Of[:, :], Of[:, :], gq[:, :], ALU.mult)
                    for hh in range(4):
                        h = hg * 4 + hh
                        cx.ts("dve", Of[:, hh * 128:(hh + 1) * 128], Of[:, hh * 128:(hh + 1) * 128],
                              pv[:, PV_GBG + h:PV_GBG + h + 1], pv[:, PV_GBB + h:PV_GBB + h + 1],
                              ALU.mult, ALU.add)
                    cx.act(V(gs, gs.t[:, :].rearrange("p (h t) -> p h t", h=4)),
                           V(zA, zv[:, 8:12, cs]), AF.Sigmoid)
                    cx.tt("dve", V(gs, gs.t[:, :].rearrange("p (h t) -> p h t", h=4)),
                          V(gs, gs.t[:, :].rearrange("p (h t) -> p h t", h=4)), V(zA, zv[:, 8:12, cs]), ALU.mult)
                    cx.tt("dve", oT[:, 8 + hg * 4:8 + hg * 4 + 4, cs],
                          V(Of, Of.t[:, :].rearrange("p (h t) -> p h t", h=4)),
                          V(gs, gs.t[:, :].rearrange("p (h t) -> p h t", h=4)), ALU.mult)

            for j in range(TPS):
                t = st * TPS + j
                cx.dma("sp", V(zA, hv[:, j, :]), xall[t * 128:(t + 1) * 128, :])
            cx.act(V(zA, zA.t[:, 0:3 * D]), V(zA, zA.t[:, 0:3 * D]), AF.Copy, scale=float(ALPHA))
            for g8i in range(8):
                slot = load_w("out", g8i * 256, 256)
                for j in range(TPS):
                    t = st * TPS + j
                    if t == 0:
                        continue
                    p = next_ps()
                    for c in range(16):
                        cx.mm(p[:, 0:256], oT[:, c, j * 128:(j + 1) * 128], slot[:, c, :],
                              start=(c == 0), stop=(c == 15))
                    hsl = V(zA, hv[:, j, g8i * 256:(g8i + 1) * 256])
                    cx.tt("dve", hsl, hsl, p[:, 0:256], ALU.add)
            tiles_ln = [j for j in range(TPS) if st * TPS + j != 0]
            for j in tiles_ln:
                hj = V(zA, hv[:, j, :])
                cx.op("dve", lambda eng, o=lnst.t[:, 0:1], i=hv[:, j, :]: eng.tensor_reduce(
                    o, i, mybir.AxisListType.X, ALU.add), [hj], [lnst[:, 0:1]])
                cx.act(xf[:, :], hj, AF.Square)
                cx.op("dve", lambda eng, o=lnst.t[:, 1:2], i=xf.t[:, :]: eng.tensor_reduce(
                    o, i, mybir.AxisListType.X, ALU.add), [xf[:, :]], [lnst[:, 1:2]])
                cx.ts("dve", lnst[:, 2:3], lnst[:, 0:1], 1.0 / D, None, ALU.mult)
                cx.tt("dve", lnst[:, 3:4], lnst[:, 2:3], lnst[:, 2:3], ALU.mult)
                cx.stt(lnst[:, 4:5], lnst[:, 1:2], 1.0 / D, lnst[:, 3:4], ALU.mult, ALU.subtract)
                cx.act(lnst[:, 5:6], lnst[:, 4:5], AF.Ln, bias=epsb[:, 1:2])
                cx.act(lnst[:, 5:6], lnst[:, 5:6], AF.Exp, scale=-0.5)
                cx.ts("dve", hj, hj, lnst[:, 2:3], lnst[:, 5:6], ALU.subtract, ALU.mult)
            cx.dma("sp", lnbuf[:, :], lng_d[0, :].partition_broadcast(128))
            for j in tiles_ln:
                hj = V(zA, hv[:, j, :])
                cx.tt("dve", hj, hj, lnbuf[:, :], ALU.mult)
            cx.dma("sp", lnbuf[:, :], lnb_d[0, :].partition_broadcast(128))
            for j in tiles_ln:
                t = st * TPS + j
                hj = V(zA, hv[:, j, :])
                cx.tt("dve", hj, hj, lnbuf[:, :], ALU.add)
                if t == NT - 1:
                    cx.dma("sp", ys_d, hj)
                else:
                    cx.dma("sp", yp_d[(t - 1) * 128:t * 128, :], hj)

        cx.dma("sp", ret_p.rearrange("h d e -> d h e"), Sret[:, :, :])
        for gb in range(8):
            pT = ps[gb % 2]
            cx.tr(pT[:, 0:128], Hf[:, gb, :], idf[:, :])
            cx.cp("act", Of[:, 0:128], pT[:, 0:128])
            for e in range(2):
                cx.dma("sp", rwkv_p[2 * gb + e], Of[e * 64:(e + 1) * 64, e * 64:(e + 1) * 64])
        for grp in range(9):
            nb = 4 if grp < 8 else 1
            pa, pb = ps[0], ps[1]
            for k in range(nb):
                blk = grp * 4 + k
                cx.tr(pa[0:16, k * 128:(k + 1) * 128], raws[:, blk, :], idf[:, :])
                cx.tr(pb[0:1, k * 128:(k + 1) * 128], rawp[:, blk:blk + 1], idf[:, :])
            cx.cp("dve", rowb[0:16, 0:nb * 128], pa[0:16, 0:nb * 128])
            cx.cp("act", rowp[0:1, 0:nb * 128], pb[0:1, 0:nb * 128])
            cx.dma("sp", shift_s[:, grp * 512:grp * 512 + nb * 128], rowb[0:16, 0:nb * 128])
            cx.dma("sp", shift_p[:, grp * 512:grp * 512 + nb * 128], rowp[0:1, 0:nb * 128])
        cx.finish()
    return nc


def dmask_or(decq, samp, hs):
    return V(decq, decq.t[:, samp, hs, :])


_NC_CACHE = {}


def _consts():
    f32 = np.float32
    c = {}
    c["ident_f"] = np.eye(128, dtype=f32)
    P = np.zeros((128, 128), f32)
    for m in range(64):
        P[m + 64, m] = -1.0
    for m in range(64, 128):
        P[m - 64, m] = 1.0
    c["prot_f"] = P
    half = 64
    inv = (np.float32(10000.0) ** (-(np.arange(half, dtype=f32) / np.float32(half)))).astype(f32)
    pos = np.zeros(NT * 128, f32)
    pos[112:128] = np.arange(16)
    pos[128:17 * 128] = 16 + np.arange(2048)
    pos[17 * 128:] = 16384 + (np.arange(128) % 8)
    ang = (pos[None, :] * inv[:, None]).astype(f32)
    cos = np.cos(ang).astype(f32)
    sin = np.sin(ang).astype(f32)
    c["cos_d"] = np.ascontiguousarray(np.concatenate([cos, cos], axis=0))
    c["sin_d"] = np.ascontiguousarray(np.concatenate([sin, sin], axis=0))
    lg = np.array(LOG_G, f32)
    j = np.arange(128)[:, None]
    i = np.arange(128)[None, :]
    dm = np.zeros((128, 2, 8, 128), f32)
    dq = np.zeros((128, 2, 8, 128), f32)
    deck = np.zeros((128, 16), f32)
    for h in range(8):
        diff = (i - j).astype(f32)
        dm[:, 0, h, :] = np.where(diff >= 0, np.exp(np.maximum(diff, 0) * lg[h]), 0.0)
        same = (i // 8) == (j // 8)
        dm[:, 1, h, :] = np.where((diff >= 0) & same, np.exp(np.maximum(diff, 0) * lg[h]), 0.0)
        dq[:, 0, h, :] = np.exp((np.arange(128, dtype=f32) + 1.0) * lg[h])[None, :]
        dq[:, 1, h, :] = np.exp(((np.arange(128) % 8).astype(f32) + 1.0) * lg[h])[None, :]
        deck[:, h] = np.exp((127.0 - np.arange(128, dtype=f32)) * lg[h])
        deck[:, 8 + h] = np.exp((7.0 - (np.arange(128) % 8).astype(f32)) * lg[h])
    c["dmask_d"] = np.ascontiguousarray(dm.reshape(128, -1))
    c["decq_d"] = np.ascontiguousarray(dq.reshape(128, -1))
    c["deck"] = deck
    bm = np.zeros((128, 16), f32)
    bm[np.arange(128), np.arange(128) // 8] = 1.0
    c["bm"] = bm
    bo = np.zeros((128, 128), f32)
    bo[0:64, 0:64] = 1.0
    bo[64:128, 64:128] = 1.0
    c["bones_d"] = bo
    s_ = np.arange(128)[:, None]
    t_ = np.arange(128)[None, :]
    same = (s_ // 8) == (t_ // 8)
    m4 = np.zeros((128, 2, 512), f32)
    for a, sm in ((0, np.ones_like(same)), (1, same)):
        strict = ((t_ > s_) & sm).astype(f32)
        incl = ((t_ >= s_) & sm).astype(f32)
        m4[:, a, 0:128] = strict
        m4[:, a, 128:256] = incl
        m4[:, a, 256:384] = strict
        m4[:, a, 384:512] = incl
    c["maskT4_d"] = np.ascontiguousarray(m4.reshape(128, 1024))
    mn = np.zeros((128, 2, 128), f32)
    mn[:, 0, :] = (t_.T > s_.T).astype(f32) if False else (np.arange(128)[None, :] < np.arange(128)[:, None]).astype(f32)
    mn[:, 1, :] = ((np.arange(128)[None, :] < np.arange(128)[:, None]) & same).astype(f32)
    c["maskN_d"] = np.ascontiguousarray(mn.reshape(128, 256))
    sc = np.ones((128, 2, 128), f32)
    sc[:, 1, :] = (np.arange(128) % 8 != 0).astype(f32)[None, :]
    c["scanm_d"] = np.ascontiguousarray(sc.reshape(128, 256))
    return c


def kernel(x_prompt, x_sample, state_rwkv, state_shift, state_ret, meta_tokens, w_in, w_out, shift_mix,
           w0, w_up, a0, a_up, k_k, k_a, r_k, gn_a_g, gn_a_b, gn_b_g, gn_b_b, ln_g, ln_b):
    f32 = np.float32
    A = lambda v: np.asarray(v, f32)
    x_prompt, x_sample, meta = A(x_prompt), A(x_sample), A(meta_tokens)
    if "nc" not in _NC_CACHE:
        _NC_CACHE["nc"] = build()
    nc = _NC_CACHE["nc"]
    c = _consts()
    pvh = np.zeros((128, NPV), f32)
    fm = lambda v, nb: A(v).reshape(nb, 128).T
    pvh[:, PV_MIX:PV_MIX + 33] = fm(shift_mix[0], 33)
    pvh[:, PV_W0:PV_W0 + 8] = fm(w0[0], 8)
    pvh[:, PV_A0:PV_A0 + 8] = fm(a0[0], 8)
    pvh[:, PV_KK:PV_KK + 8] = fm(k_k[0], 8)
    pvh[:, PV_KA:PV_KA + 8] = fm(k_a[0], 8)
    pvh[:, PV_RK:PV_RK + 8] = fm(A(r_k)[0].reshape(-1), 8)
    pvh[:, PV_GAG:PV_GAG + 8] = fm(gn_a_g[0], 8)
    pvh[:, PV_GAB:PV_GAB + 8] = fm(gn_a_b[0], 8)
    pvh[:, PV_GBG:PV_GBG + 8] = fm(gn_b_g[0], 8)
    pvh[:, PV_GBB:PV_GBB + 8] = fm(gn_b_b[0], 8)
    pvh[:, PV_DECK:PV_DECK + 16] = c["deck"]
    pvh[:, PV_BM:PV_BM + 16] = c["bm"]
    shared = {
        "w_in": np.ascontiguousarray(A(w_in)[0]), "w_out": np.ascontiguousarray(A(w_out)[0]),
        "ident_f": c["ident_f"], "prot_f": c["prot_f"], "cos_d": c["cos_d"], "sin_d": c["sin_d"],
        "dmask_d": c["dmask_d"], "decq_d": c["decq_d"], "pv_d": pvh,
        "lng_d": np.ascontiguousarray(A(ln_g)), "lnb_d": np.ascontiguousarray(A(ln_b)),
        "bones_d": c["bones_d"], "maskT4_d": c["maskT4_d"], "maskN_d": c["maskN_d"], "scanm_d": c["scanm_d"],
        "lora_d": np.ascontiguousarray(np.concatenate([A(w_up)[0], A(a_up)[0]], axis=0)),
    }
    sret = A(state_ret)[0]
    srw = A(state_rwkv)[0]
    ssh = A(state_shift)[0]
    in_maps = []
    for cid in range(8):
        b = cid % 4
        xall = np.concatenate([np.zeros((112, D), f32), meta, x_prompt[b],
                               x_sample[16 * cid:16 * cid + 16].reshape(128, D)], axis=0)
        m = dict(shared)
        m["xall"] = np.ascontiguousarray(xall)
        m["sret_d"] = np.ascontiguousarray(sret[16 * cid:16 * cid + 16])
        m["srwkv_d"] = np.ascontiguousarray(srw[16 * cid:16 * cid + 16])
        m["sshift_d"] = np.ascontiguousarray(ssh[16 * cid:16 * cid + 16])
        in_maps.append(m)
    res = run_bass_kernel_spmd(nc, in_maps, core_ids=list(range(8)))
    R = res.results
    y_prompt = np.stack([R[b]["yp"] for b in range(4)]).astype(f32)
    y_sample = np.concatenate([R[cid]["ys"].reshape(16, 8, D) for cid in range(8)], axis=0).astype(f32)
    rwkv_p = np.stack([R[b]["rwkv_p"] for b in range(4)])[None].astype(f32)
    shift_p = np.stack([R[b]["shift_p"][0] for b in range(4)])[None].astype(f32)
    ret_p = np.stack([R[b]["ret_p"] for b in range(4)])[None].astype(f32)
    rwkv_s = np.concatenate([R[cid]["rwkv_s"] for cid in range(8)], axis=0)[None].astype(f32)
    shift_s = np.concatenate([R[cid]["shift_s"] for cid in range(8)], axis=0)[None].astype(f32)
    ret_s = np.concatenate([R[cid]["ret_s"] for cid in range(8)], axis=0)[None].astype(f32)
    return (y_prompt, y_sample, rwkv_p, shift_p, ret_p, rwkv_s, shift_s, ret_s)
```

```python
import contextlib
import numpy as np
import concourse.bass as bass
import concourse.mybir as mybir
from concourse.bass_utils import run_bass_kernel_spmd

F32 = mybir.dt.float32
BF16 = mybir.dt.bfloat16
AF = mybir.ActivationFunctionType
ALU = mybir.AluOpType

D = 2048
NT = 18
TPS = 3
NST = NT // TPS
STN = TPS * 128
NA = 4224
NB = 4096
NIN = NA + NB
NBLK_A = 33
SAME_ENGINE_SYNC = True
SAME_ENGINE_WINDOW = 3


class Buf:
    def __init__(self, t, name):
        self.t = t
        self.name = name
        self.w = None
        self.r = {}
        self.dsem = None
        self.dcnt = 0
        self.psum = False

    def __getitem__(self, idx):
        return V(self, self.t[idx])


class V:
    def __init__(self, buf, ap):
        self.buf = buf
        self.ap = ap


def _ap(x):
    return x.ap if isinstance(x, V) else x


class Ctx:
    ENG = ["pe", "act", "dve", "pool", "sp"]

    def __init__(self, nc, stack):
        self.nc = nc
        self.stack = stack
        self.prog = {e: [] for e in self.ENG}
        self.cnt = {e: 0 for e in self.ENG}
        self.sem = {e: stack.enter_context(nc.semaphore("sem_" + e)) for e in self.ENG}
        self.seen = {e: {} for e in self.ENG}
        self.nbuf = 0
        self.dma_sems = []

    def sbuf(self, shape, dt, name=None):
        self.nbuf += 1
        name = name or ("sb%d" % self.nbuf)
        t = self.stack.enter_context(self.nc.sbuf_tensor(name, list(shape), dt))
        return Buf(t, name)

    def psum(self, shape, dt, name=None):
        self.nbuf += 1
        name = name or ("ps%d" % self.nbuf)
        t = self.stack.enter_context(self.nc.psum_tensor(name, list(shape), dt))
        bf = Buf(t, name)
        bf.psum = True
        return bf

    def _need(self, e, dep, waits):
        if dep is None:
            return
        kind, key, val, sem = dep
        if kind == "eng" and key == e:
            if not SAME_ENGINE_SYNC or e in ("pe", "sp"):
                return
            if SAME_ENGINE_WINDOW and val <= self.cnt[e] - SAME_ENGINE_WINDOW:
                return
        if self.seen[e].get(key, 0) >= val:
            return
        waits[key] = (sem, max(val, waits.get(key, (None, 0))[1]))

    def _deps(self, e, reads, writes, skip_same_pe=True):
        waits = {}
        for b in reads:
            self._need(e, b.w, waits)
            if b.psum:
                for k_, d in b.r.items():
                    if k_ != e:
                        self._need(e, d, waits)
        for b in writes:
            self._need(e, b.w, waits)
            for d in b.r.values():
                self._need(e, d, waits)
        out = []
        for key, (sem, val) in waits.items():
            self.seen[e][key] = val
            out.append((sem, val))
        return out

    def op(self, e, fn, reads, writes):
        reads = [x.buf for x in reads if isinstance(x, V)]
        writes = [x.buf for x in writes if isinstance(x, V)]
        waits = self._deps(e, reads, writes)
        self.cnt[e] += 1
        n = self.cnt[e]
        sem = self.sem[e]

        def emit(eng):
            for (s, v) in waits:
                eng.wait_ge(s, v)
            fn(eng).then_inc(sem, 1)
        self.prog[e].append(emit)
        dep = ("eng", e, n, sem)
        for b in reads:
            b.r[e] = dep
        for b in writes:
            b.w = dep
            b.r = {}

    def dma(self, q, out, in_, extra_waits=(), **kw):
        reads = [in_.buf] if isinstance(in_, V) else []
        writes = [out.buf] if isinstance(out, V) else []
        waits = self._deps(q, reads, writes) + list(extra_waits)
        tb = (writes + reads)[0]
        if tb.dsem is None:
            tb.dsem = self.stack.enter_context(self.nc.semaphore("dsem_" + tb.name))
            self.dma_sems.append(tb)
        tb.dcnt += 16
        val = tb.dcnt
        sem = tb.dsem
        o, i = _ap(out), _ap(in_)

        def emit(eng):
            for (s, v) in waits:
                eng.wait_ge(s, v)
            eng.dma_start(out=o, in_=i, **kw).then_inc(sem, 16)
        self.prog[q].append(emit)
        dep = ("dma", "d_" + tb.name, val, sem)
        for b in reads:
            b.r["d_" + tb.name] = dep
        for b in writes:
            b.w = dep
            b.r = {}

    def barrier(self):
        targets = [(self.sem[e], self.cnt[e], e) for e in self.ENG if self.cnt[e] > 0]
        dmas = [(b.dsem, b.dcnt, "d_" + b.name) for b in self.dma_sems]
        for e in self.ENG:
            ws = [(s, v) for (s, v, k) in targets if k != e] + [(s, v) for (s, v, k) in dmas]

            def emit(eng, ws=ws):
                for (s, v) in ws:
                    eng.wait_ge(s, v)
            self.prog[e].append(emit)
            for (s, v, k) in targets + dmas:
                if k != e:
                    self.seen[e][k] = max(self.seen[e].get(k, 0), v)

    def mm(self, out, lhsT, rhs, start=True, stop=True):
        o, l, r = out.ap, lhsT.ap, rhs.ap
        self.op("pe", lambda eng: eng.matmul(o, l, r, start=start, stop=stop), [lhsT, rhs], [out])

    def tr(self, out, in_, ident):
        o, i, d = out.ap, in_.ap, ident.ap
        self.op("pe", lambda eng: eng.transpose(o, i, d), [in_, ident], [out])

    def act(self, out, in_, func, bias=None, scale=1.0):
        o, i = out.ap, in_.ap
        rd = [in_]
        kw = {}
        if bias is not None:
            kw["bias"] = _ap(bias)
            if isinstance(bias, V):
                rd.append(bias)
        if isinstance(scale, V):
            rd.append(scale)
        sc = _ap(scale)
        self.op("act", lambda eng: eng.activation(o, i, func, scale=sc, **kw), rd, [out])

    def tt(self, e, out, a, b, op):
        o, x, y = out.ap, a.ap, b.ap
        self.op(e, lambda eng: eng.tensor_tensor(o, x, y, op), [a, b], [out])

    def ts(self, e, out, a, s1, s2, op0, op1=None):
        o, x = out.ap, a.ap
        rd = [a] + [s for s in (s1, s2) if isinstance(s, V)]
        a1, a2 = _ap(s1), _ap(s2)
        if op1 is None:
            self.op(e, lambda eng: eng.tensor_scalar(o, x, a1, None, op0), rd, [out])
        else:
            self.op(e, lambda eng: eng.tensor_scalar(o, x, a1, a2, op0, op1), rd, [out])

    def stt(self, out, in0, scalar, in1, op0, op1):
        o, x, y = out.ap, in0.ap, in1.ap
        rd = [in0, in1] + ([scalar] if isinstance(scalar, V) else [])
        s = _ap(scalar)
        self.op("dve", lambda eng: eng.scalar_tensor_tensor(o, x, s, y, op0, op1), rd, [out])

    def cp(self, e, out, in_):
        o, i = out.ap, in_.ap
        if e == "act":
            self.op(e, lambda eng: eng.copy(o, i), [in_], [out])
        else:
            self.op(e, lambda eng: eng.tensor_copy(o, i), [in_], [out])

    def memset(self, e, out, val):
        o = out.ap
        self.op(e, lambda eng: eng.memset(o, val), [], [out])

    def finish(self):
        finals = [(b.dsem, b.dcnt) for b in self.dma_sems]
        engs = [(e, self.sem[e], self.cnt[e]) for e in self.ENG if e != "sp" and self.cnt[e] > 0]

        def emit(eng):
            for (s, v) in finals:
                eng.wait_ge(s, v)
            for (_, s, v) in engs:
                eng.wait_ge(s, v)
        self.prog["sp"].append(emit)
        prog = self.prog
        with self.nc.Block() as block:
            @block.tensor
            def _(eng):
                for f in prog["pe"]:
                    f(eng)

            @block.scalar
            def _(eng):
                for f in prog["act"]:
                    f(eng)

            @block.vector
            def _(eng):
                for f in prog["dve"]:
                    f(eng)

            @block.gpsimd
            def _(eng):
                for f in prog["pool"]:
                    f(eng)

            @block.sync
            def _(eng):
                for f in prog["sp"]:
                    f(eng)


ALPHA = 2.0 ** 0.25
LN_EPS = 1e-5
GN_EPS_A = 64e-5
GN_EPS_B = 1e-5
LOG_G = [float(np.log1p(-np.exp2(np.float32(-5.0 - h)))) for h in range(8)]
import os
RW_LEVEL = int(os.environ.get("RW_LEVEL", "4"))
SL = int(os.environ.get("SL", "4"))
PIPE = int(os.environ.get("PIPE", "1"))

PV_MIX = 0
PV_W0 = 33
PV_A0 = 41
PV_KK = 49
PV_KA = 57
PV_RK = 65
PV_GAG = 73
PV_GAB = 81
PV_GBG = 89
PV_GBB = 97
PV_DECK = 105
PV_BM = 121
NPV = 137


def build():
    nc = bass.Bass("TRN2", target_bir_lowering=False)

    def din(name, shape, dt=F32):
        return nc.dram_tensor(name, list(shape), dt, kind="ExternalInput").ap()

    def dout(name, shape, dt=F32):
        return nc.dram_tensor(name, list(shape), dt, kind="ExternalOutput").ap()

    xall = din("xall", [NT * 128, D])
    w_in = din("w_in", [D, NIN])
    w_out = din("w_out", [D, D])
    ident_f = din("ident_f", [128, 128])
    prot_f = din("prot_f", [128, 128])
    cos_d = din("cos_d", [128, NT * 128])
    sin_d = din("sin_d", [128, NT * 128])
    dmask_d = din("dmask_d", [128, 2 * 8 * 128])
    decq_d = din("decq_d", [128, 2 * 8 * 128])
    pv_d = din("pv_d", [128, NPV])
    lng_d = din("lng_d", [1, D])
    lnb_d = din("lnb_d", [1, D])
    sret_d = din("sret_d", [16, 8, 128, 128])
    bones_d = din("bones_d", [128, 128])
    maskT4_d = din("maskT4_d", [128, 1024])
    maskN_d = din("maskN_d", [128, 256])
    scanm_d = din("scanm_d", [128, 256])
    lora_d = din("lora_d", [128, 1024])
    rwkv_p = dout("rwkv_p", [16, 64, 64])
    srwkv_d = din("srwkv_d", [16, 16, 64, 64])
    sshift_d = din("sshift_d", [16, NA])
    rwkv_s = dout("rwkv_s", [16, 16, 64, 64])
    wbf = nc.dram_tensor("wbf", [33, 128, 16, 256], BF16, kind="Internal").ap()
    wobf = nc.dram_tensor("wobf", [8, 128, 16, 256], BF16, kind="Internal").ap()
    shift_p = dout("shift_p", [1, NA])
    shift_s = dout("shift_s", [16, NA])
    ret_p = dout("ret_p", [8, 128, 128])
    ret_s = dout("ret_s", [16, 8, 128, 128])
    yp_d = dout("yp", [2048, D])
    ys_d = dout("ys", [128, D])

    with contextlib.ExitStack() as stack:
        cx = Ctx(nc, stack)
        CDEC = float(np.exp(-0.5))
        xf = cx.sbuf([128, D], F32, "xf")
        xb = cx.sbuf([128, D], BF16, "xb")
        xT = cx.sbuf([128, 16, STN], BF16, "xT")
        ZW = 17 * STN
        zA = cx.sbuf([128, ZW], F32, "zA")
        zv = zA.t[:, :].rearrange("p (b t) -> p b t", b=17)
        hv = zA.t[:, 0:3 * D].rearrange("p (j d) -> p j d", j=3)
        wsl = [cx.sbuf([128, 16, 256], BF16, "wsl%d" % i) for i in range(2)]
        oT = cx.sbuf([128, 16, STN], BF16, "oT")
        ps = [cx.psum([128, 512], F32, "ps%d" % i) for i in range(7)]
        ps_tb = cx.psum([128, 1024], BF16, "pstb")

        idf = cx.sbuf([128, 128], F32, "idf")
        idb = cx.sbuf([128, 128], BF16, "idb")
        protb = cx.sbuf([128, 128], BF16, "protb")
        onesf = cx.sbuf([128, 128], F32, "onesf")
        bonesf = cx.sbuf([128, 128], F32, "bonesf")
        bonesb = cx.sbuf([128, 128], BF16, "bonesb")
        pv = cx.sbuf([128, NPV], F32, "pv")
        omm = cx.sbuf([128, 33], F32, "omm")
        dmask = cx.sbuf([128, 2, 4, 128], F32, "dmask")
        decq = cx.sbuf([128, 2, 4, 128], F32, "decq")
        maskT4 = cx.sbuf([128, 2, 512], BF16, "maskT4")
        maskN = cx.sbuf([128, 2, 128], BF16, "maskN")
        scanm = cx.sbuf([128, 2, 128], F32, "scanm")
        lorab = cx.sbuf([128, 1024], BF16, "lorab")
        epsb = cx.sbuf([128, 4], F32, "epsb")
        cx.dma("sp", idf[:, :], ident_f)
        cx.cp("dve", idb[:, :], idf[:, :])
        cx.dma("sp", xf[:, 0:128], prot_f)
        cx.cp("dve", protb[:, :], xf[:, 0:128])
        cx.dma("sp", bonesf[:, :], bones_d)
        cx.cp("dve", bonesb[:, :], bonesf[:, :])
        cx.memset("dve", onesf[:, :], 1.0)
        cx.memset("dve", epsb[:, 0:1], GN_EPS_B)
        cx.memset("dve", epsb[:, 1:2], LN_EPS)
        cx.memset("dve", epsb[:, 2:3], GN_EPS_A)
        cx.memset("dve", epsb[:, 3:4], 1e-18)
        cx.dma("sp", pv[:, :], pv_d)
        cx.ts("dve", omm[:, :], pv[:, PV_MIX:PV_MIX + 33], -1.0, 1.0, ALU.mult, ALU.add)
        cx.dma("sp", xf[:, 0:1024], maskT4_d)
        cx.cp("dve", V(maskT4, maskT4.t[:, :, :].rearrange("p a t -> p (a t)")), xf[:, 0:1024])
        cx.dma("sp", xf[:, 0:256], maskN_d)
        cx.cp("dve", V(maskN, maskN.t[:, :, :].rearrange("p a t -> p (a t)")), xf[:, 0:256])
        cx.dma("sp", V(scanm, scanm.t[:, :, :].rearrange("p a t -> p (a t)")), scanm_d)
        cx.dma("sp", xf[:, 0:1024], lora_d)
        cx.cp("dve", lorab[:, :], xf[:, 0:1024])

        vtB = cx.sbuf([128, 3, 512], BF16, "vtB")
        cosb = cx.sbuf([128, STN], F32, "cosb")
        sinb = cx.sbuf([128, STN], F32, "sinb")
        qr = cx.sbuf([128, 4, STN], BF16, "qr")
        kr = cx.sbuf([128, 4, STN], BF16, "kr")
        zb16 = cx.sbuf([128, STN], BF16, "zb16")
        t2 = cx.sbuf([128, STN], F32, "t2")
        ktok = cx.sbuf([128, 512], BF16, "ktok")
        scm = cx.sbuf([128, 512], BF16, "scm")
        qdec = cx.sbuf([128, 4, 128], BF16, "qdec")
        vdec = cx.sbuf([128, 4, 128], BF16, "vdec")
        Sret = cx.sbuf([128, 8, 128], F32, "Sret")
        Sretb = cx.sbuf([128, 8, 128], BF16, "Sretb")
        Rg = cx.sbuf([128, 5120], F32, "Rg").t

        def rview(a, n, dt, name, pat=None, **kw):
            ap = Rg[:, a:a + n]
            if dt == BF16:
                ap = ap.bitcast(BF16)
            if pat is not None:
                ap = ap.rearrange(pat, **kw)
            return Buf(ap, name)
        SS = rview(0, 1024, F32, "SS", "p (s e) -> p s e", s=8)
        SSn = rview(1024, 1024, F32, "SSn", "p (s e) -> p s e", s=8)
        Nat = rview(2048, 1024, F32, "Nat", "p (s e) -> p s e", s=8)
        SSb = rview(3072, 1024, BF16, "SSb", "p (s e) -> p s e", s=16)
        vexp = rview(4096, 512, BF16, "vexp", "p (s e) -> p s e", s=8)
        Vexp2 = rview(4608, 512, BF16, "Vexp2", "p (s e) -> p s e", s=8)
        Of = cx.sbuf([128, 512], F32, "Of")
        Osq = cx.sbuf([128, 512], F32, "Osq")
        gm = cx.sbuf([128, 512], F32, "gm")
        gq = cx.sbuf([128, 512], F32, "gq")
        gs = cx.sbuf([128, 512], F32, "gs")
        lnst = cx.sbuf([128, 8], F32, "lnst")
        carry = cx.sbuf([128, 33], F32, "carry")
        rawp = cx.sbuf([128, 33], F32, "rawp")
        raws = cx.sbuf([128, 33, 16], F32, "raws")
        zc1 = cx.sbuf([128, STN], F32, "zc1")
        t1 = zc1
        lnbuf = xf
        rowb = Of
        rowp = gm
        twzx = cx.sbuf([128, STN], BF16, "twzx")
        AT = cx.sbuf([128, 4, 128], BF16, "AT")
        BT = cx.sbuf([128, 4, 128], BF16, "BT")
        KT = cx.sbuf([128, 4, 128], BF16, "KT")
        RT = cx.sbuf([128, 4, 128], BF16, "RT")
        vb = cx.sbuf([128, 4, 128], BF16, "vb")
        bonus = cx.sbuf([128, 512], F32, "bonus")
        S1 = cx.sbuf([128, 128], F32, "S1")
        A1 = cx.sbuf([128, 128], F32, "A1")
        CN = cx.sbuf([128, 128], F32, "CN")
        E1 = cx.sbuf([128, 128], F32, "E1")
        E2 = cx.sbuf([128, 128], F32, "E2")
        E3 = cx.sbuf([128, 128], F32, "E3")
        KK = cx.sbuf([128, 128], F32, "KK")
        RN = cx.sbuf([128, 128], F32, "RN")
        T1 = cx.sbuf([128, 128], F32, "T1")
        KP = cx.sbuf([128, 128], F32, "KP")
        KK2b = cx.sbuf([128, 128], BF16, "KK2b")
        RKb = cx.sbuf([128, 128], BF16, "RKb")
        Gam = cx.sbuf([128, 4], F32, "Gam")
        Btok = cx.sbuf([128, 512], BF16, "Btok")
        Ktok = cx.sbuf([128, 512], BF16, "Ktok")
        Vtok2 = cx.sbuf([128, 512], BF16, "Vtok2")
        Vpad = cx.sbuf([128, 8, 128], BF16, "Vpad")
        Xpad = cx.sbuf([128, 8, 128], BF16, "Xpad")
        Upad = cx.sbuf([128, 8, 128], BF16, "Upad")
        Upair = cx.sbuf([128, 4, 128], BF16, "Upair")
        LT = [cx.sbuf([128, 512], BF16, "LT%d" % i) for i in range(8)]
        AA0 = [cx.sbuf([128, 128], BF16, "AA0_%d" % i) for i in range(8)]
        AAM = [[cx.sbuf([128, 384], BF16, "AAMx%d_%d" % (p_, i)) for i in range(8)] for p_ in range(2)]
        XTb = cx.sbuf([128, 128], BF16, "XTb")
        prevS = cx.sbuf([128, 128], F32, "prevS")
        GamS = cx.sbuf([128, 4, 16], F32, "GamS")
        stT = cx.sbuf([128, 33, 16], F32, "stT")
        Hf = cx.sbuf([128, 8, 128], F32, "Hf")
        Hb = cx.sbuf([128, 8, 128], BF16, "Hb")
        bdm = bonesf

        cx.memset("dve", Sret[:, :, :], 0.0)
        cx.memset("dve", Sretb[:, :, :], 0.0)
        cx.memset("pool", oT[:, :, :], 0.0)
        cx.memset("dve", carry[:, :], 0.0)
        cx.memset("pool", Vpad[:, :, :], 0.0)
        cx.memset("pool", Xpad[:, :, :], 0.0)
        cx.memset("pool", Upad[:, :, :], 0.0)
        cx.memset("dve", Hf[:, :, :], 0.0)
        cx.memset("dve", Hb[:, :, :], 0.0)
        SETS = [(AT, BT, KT, RT, vb, bonus, Gam, Btok, Ktok, Vtok2, Vpad)]
        if PIPE:
            SETS.append((
                rview(0, 256, BF16, "AT1", "p (b t) -> p b t", b=4),
                rview(256, 256, BF16, "BT1", "p (b t) -> p b t", b=4),
                rview(512, 256, BF16, "KT1", "p (b t) -> p b t", b=4),
                rview(768, 256, BF16, "RT1", "p (b t) -> p b t", b=4),
                rview(1024, 256, BF16, "vb1", "p (b t) -> p b t", b=4),
                rview(1280, 512, F32, "bonus1"),
                rview(1792, 4, F32, "Gam1"),
                rview(1800, 256, BF16, "Btok1"),
                rview(2056, 256, BF16, "Ktok1"),
                rview(2312, 256, BF16, "Vtok21"),
                rview(2568, 512, BF16, "Vpad1", "p (h c) -> p h c", h=8),
            ))
            cx.memset("pool", SETS[1][10][:, :, :], 0.0)
        else:
            cx.memset("pool", Nat[:, :, :], 0.0)
        for grp in range(9):
            nb = 4 if grp < 8 else 1
            cx.dma("sp", Of[0:16, 0:nb * 128], sshift_d[:, grp * 512:grp * 512 + nb * 128])
            for k in range(nb):
                cx.tr(ps[0][:, k * 16:(k + 1) * 16], Of[0:16, k * 128:(k + 1) * 128], idf[0:16, 0:16])
            cx.cp("dve", V(stT, stT.t[:, grp * 4:grp * 4 + nb, :].rearrange("p b s -> p (b s)")), ps[0][:, 0:nb * 16])

        w_view = w_in.rearrange("(c p) n -> p c n", p=128)
        wo_view = w_out.rearrange("(c p) n -> p c n", p=128)
        dmask_v = dmask_d.rearrange("p (a h t) -> p a h t", a=2, h=8)
        decq_v = decq_d.rearrange("p (a h t) -> p a h t", a=2, h=8)
        state = {"w": 0, "ev": 0, "ps": 0}

        wgran = {}

        def gran_id(which, c0):
            if which == "out":
                return 100 + c0 // 256
            if c0 < 4096:
                return c0 // 256
            if c0 == 4096:
                return 16
            return 17 + (c0 - NA) // 256

        def convert(which, c0, ncol, extra=()):
            gid = gran_id(which, c0)
            pb = Buf(None, "wg%d" % gid)
            if which == "out":
                dst = wobf[c0 // 256]
                src = wo_view[:, :, c0:c0 + ncol]
            else:
                dst = wbf[gid]
                src = w_view[:, :, c0:c0 + ncol]
            cx.dma("pool", V(pb, dst[:, :, 0:ncol]), src, extra_waits=list(extra))
            wgran[gid] = (pb, dst)

        conv_order = []
        for hg in range(2):
            for typ in range(4):
                for q in range(2):
                    conv_order.append(("in", typ * 1024 + hg * 512 + q * 256, 256))
            if hg == 0:
                conv_order.append(("in", 4096, 128))
            for off in (0, 1024, 3072, 2048):
                for q in range(2):
                    conv_order.append(("in", NA + hg * 512 + off + q * 256, 256))
        for g8i in range(8):
            conv_order.append(("out", g8i * 256, 256))
        conv_idx = {gran_id(w_, c_): i for i, (w_, c_, n_) in enumerate(conv_order)}
        conv_state = {"done": 0}

        def ensure_converted(upto):
            upto = min(upto, len(conv_order) - 1)
            while conv_state["done"] <= upto:
                w_, c_, n_ = conv_order[conv_state["done"]]
                extra = []
                if conv_state["done"] >= 4 and cx.cnt["pe"] > 0:
                    extra = [(cx.sem["pe"], cx.cnt["pe"])]
                convert(w_, c_, n_, extra)
                conv_state["done"] += 1

        ensure_converted(3)

        def load_w(which, c0, ncol):
            slot = wsl[state["w"] % 2]
            state["w"] += 1
            ensure_converted(conv_idx[gran_id(which, c0)] + 3)
            pb, src = wgran[gran_id(which, c0)]
            cx.dma("sp", slot[:, :, 0:ncol], V(pb, src[:, :, 0:ncol]))
            return slot

        def evac_eng():
            state["ev"] += 1
            return "act" if state["ev"] % 2 == 0 else "dve"

        def next_ps():
            state["ps"] += 1
            return ps[state["ps"] % 2]

        def proj_blocks(c0, nblk, evac_fn):
            done = 0
            while done < nblk:
                nb = min(2, nblk - done)
                slot = load_w("in", c0 + done * 128, nb * 128)
                for cb in range(nb):
                    p = next_ps()
                    for c in range(16):
                        cx.mm(p[:, 0:STN], slot[:, c, cb * 128:(cb + 1) * 128], xT[:, c, :],
                              start=(c == 0), stop=(c == 15))
                    evac_fn(done + cb, p)
                done += nb

        def mix_evac(st, gblk, dblk, p):
            mixc = pv[:, PV_MIX + gblk:PV_MIX + gblk + 1]
            d = zv[:, dblk, :]
            cx.act(zc1[:, :], p[:, 0:STN], AF.Identity, scale=omm[:, gblk:gblk + 1])
            npr = STN if st < NST - 1 else 256
            cx.stt(V(zA, d[:, 1:npr]), p[:, 0:npr - 1], mixc, zc1[:, 1:npr], ALU.mult, ALU.add)
            cx.stt(V(zA, d[:, 0:1]), carry[:, gblk:gblk + 1], mixc, zc1[:, 0:1], ALU.mult, ALU.add)
            if st == NST - 1 and SL >= 1:
                d3 = d[:, 256:384].rearrange("p (s t) -> p s t", t=8)
                p3 = p.t[:, 256:384].rearrange("p (s t) -> p s t", t=8)
                z3 = zc1.t[:, 256:384].rearrange("p (s t) -> p s t", t=8)
                pS3 = prevS.t[:, :].rearrange("p (s t) -> p s t", t=8)
                cx.cp("act", V(prevS, pS3[:, :, 1:8]), V(p, p3[:, :, 0:7]))
                cx.cp("dve", V(prevS, pS3[:, :, 0]), stT[:, gblk, :])
                cx.stt(V(zA, d[:, 256:384]), prevS[:, :], mixc, zc1[:, 256:384], ALU.mult, ALU.add)
            cx.cp("act", carry[:, gblk:gblk + 1], p[:, STN - 1:STN])
            if st == NST - 1:
                cx.cp("act", rawp[:, gblk:gblk + 1], p[:, 255:256])
                cx.cp("act", raws[:, gblk, :], p[:, 263:384:8])

        def load_H(sh, gb, want_b):
            for e in range(2):
                cx.dma("sp", Nat[e * 64:(e + 1) * 64, :, e * 64:(e + 1) * 64],
                       srwkv_d[sh * 8:(sh + 1) * 8, 2 * gb + e].rearrange("s v k -> v s k"))
            for q in range(2):
                pT = ps[4 + q]
                for i4 in range(4):
                    cx.tr(pT[:, i4 * 128:(i4 + 1) * 128], Nat[:, q * 4 + i4, :], idf[:, :])
                if want_b:
                    o0 = sh * 8 + q * 4
                    cx.cp("act" if q == 0 else "dve",
                          V(SSb, SSb.t[:, o0:o0 + 4, :].rearrange("p s e -> p (s e)")), pT[:, :])
                else:
                    cx.cp("act" if q == 0 else "dve",
                          V(SS, SS.t[:, q * 4:(q + 1) * 4, :].rearrange("p s e -> p (s e)")), pT[:, :])

        def chain_sample(hg, b, cur):
            gb = hg * 4 + b
            he, ho = 2 * b, 2 * b + 1
            bs = slice(b * 128, (b + 1) * 128)
            pXT = ps[2]
            load_H(0, gb, True)
            load_H(1, gb, True)
            cx.mm(pXT[:, 0:128], Vpad[:, he, :], LT[he][:, 256:384], start=True, stop=False)
            cx.mm(pXT[:, 0:128], Vpad[:, ho, :], LT[ho][:, 256:384], start=False, stop=False)
            for i in range(16):
                cx.mm(pXT[:, 8 * i:8 * i + 8], SSb[:, i, :], AT[:, b, 8 * i:8 * i + 8],
                      start=False, stop=(i == 15))
            cx.cp("act", XTb[:, :], pXT[:, 0:128])
            if SL < 4:
                return
            cx.tr(ps_tb[:, 0:128], XTb[:, :], idb[:, :])
            cx.cp("act", Xpad[:, he, 0:64], ps_tb[:, 0:64])
            cx.cp("dve", Xpad[:, ho, 64:128], ps_tb[:, 64:128])
            pU = ps[3]
            cx.mm(pU[:, 0:128], idb[:, :], Xpad[:, he, :], start=True, stop=False)
            cx.mm(pU[:, 0:128], idb[:, :], Xpad[:, ho, :], start=False, stop=False)
            cx.mm(pU[:, 0:128], cur[he][2], Xpad[:, he, :], start=False, stop=False)
            cx.mm(pU[:, 0:128], cur[ho][2], Xpad[:, ho, :], start=False, stop=True)
            cx.cp("act", Upad[:, he, 0:64], pU[:, 0:64])
            cx.cp("dve", Upad[:, ho, 64:128], pU[:, 64:128])
            cx.cp("act", Upair[:, b, :], pU[:, 0:128])
            pO = ps[6][:, bs]
            cx.mm(pO, Upad[:, he, :], LT[he][:, 128:256], start=True, stop=False)
            cx.mm(pO, Upad[:, ho, :], LT[ho][:, 128:256], start=False, stop=False)
            cx.mm(pO, Vpad[:, he, :], LT[he][:, 384:512], start=False, stop=False)
            cx.mm(pO, Vpad[:, ho, :], LT[ho][:, 384:512], start=False, stop=False)
            for i in range(16):
                cx.mm(ps[6][:, b * 128 + 8 * i:b * 128 + 8 * i + 8], SSb[:, i, :], RT[:, b, 8 * i:8 * i + 8],
                      start=False, stop=(i == 15))
            for sh in range(2):
                load_H(sh, gb, False)
                bm = pv.t[:, PV_BM + sh * 8:PV_BM + sh * 8 + 8]
                bmb = V(pv, bm.unsqueeze(2).to_broadcast([128, 8, 128]))
                cx.tt("dve", vexp[:, :, :],
                      V(Upair, Upair.t[:, b, :].unsqueeze(1).to_broadcast([128, 8, 128])), bmb, ALU.mult)
                cx.tt("dve", Vexp2[:, :, :],
                      V(Vtok2, Vtok2.t[:, bs].unsqueeze(1).to_broadcast([128, 8, 128])), bmb, ALU.mult)
                for q in range(2):
                    pp = ps[2 + q]
                    qs = slice(q * 4, (q + 1) * 4)
                    cx.mm(pp[:, :], Btok[:, bs], V(vexp, vexp.t[:, qs, :].rearrange("p s e -> p (s e)")),
                          start=True, stop=False)
                    cx.mm(pp[:, :], Ktok[:, bs], V(Vexp2, Vexp2.t[:, qs, :].rearrange("p s e -> p (s e)")),
                          start=False, stop=True)
                    sn = V(SSn, SSn.t[:, qs, :])
                    cx.tt("dve", sn, V(pp, pp.t[:, :].rearrange("p (s e) -> p s e", s=4)),
                          V(bonesf, bonesf.t[:, :].unsqueeze(1).to_broadcast([128, 4, 128])), ALU.mult)
                    cx.tt("dve", sn, sn, V(SS, SS.t[:, qs, :]), ALU.add)
                    gsl = GamS.t[:, b, sh * 8 + q * 4:sh * 8 + q * 4 + 4]
                    cx.tt("dve", sn, sn, V(GamS, gsl.unsqueeze(2).to_broadcast([128, 4, 128])), ALU.mult)
                for q in range(2):
                    pT = ps[4 + q]
                    for i4 in range(4):
                        cx.tr(pT[:, i4 * 128:(i4 + 1) * 128], SSn[:, q * 4 + i4, :], idf[:, :])
                    cx.cp("act" if q == 0 else "dve",
                          V(SS, SS.t[:, q * 4:(q + 1) * 4, :].rearrange("p s e -> p (s e)")), pT[:, :])
                for e in range(2):
                    cx.dma("sp", rwkv_s[sh * 8:(sh + 1) * 8, 2 * gb + e].rearrange("s v k -> v s k"),
                           SS[e * 64:(e + 1) * 64, :, e * 64:(e + 1) * 64])

        def rw_prep(st, hg, j, s):
            t = st * TPS + j
            samp = 1 if t == NT - 1 else 0
            if RW_LEVEL < 1 or (samp and SL < 2):
                return
            cs = slice(j * 128, (j + 1) * 128)
            AT, BT, KT, RT, vb, bonus, Gam, Btok, Ktok, Vtok2, Vpad = SETS[s]
            for b in range(4):
                gb = hg * 4 + b
                pc = lambda off: pv[:, off + gb:off + gb + 1]
                cx.mm(ps[0][:, 0:128], lorab[0:64, gb * 128:(gb + 1) * 128], twzx[0:64, cs])
                cx.mm(ps[1][:, 256:384], lorab[64:128, gb * 128:(gb + 1) * 128], twzx[64:128, cs])
                cx.act(Of[:, b * 128:(b + 1) * 128], ps[0][:, 0:128], AF.Sigmoid, bias=pc(PV_W0))
                cx.act(Osq[:, b * 128:(b + 1) * 128], ps[1][:, 256:384], AF.Sigmoid, bias=pc(PV_A0))
            yield
            for b in range(4):
                gb = hg * 4 + b
                r_ = V(zA, zv[:, b, cs])
                k_ = V(zA, zv[:, 4 + b, cs])
                v_ = V(zA, zv[:, 8 + b, cs])
                pc = lambda off: pv[:, off + gb:off + gb + 1]
                S1 = V(Of, Of.t[:, b * 128:(b + 1) * 128])
                A1 = V(Osq, Osq.t[:, b * 128:(b + 1) * 128])
                cx.op("dve", lambda eng, o=CN.t[:, :], d0=scanm.t[:, samp, :], d1=S1.ap:
                      eng.tensor_tensor_scan(o, d0, d1, 0.0, ALU.mult, ALU.add),
                      [scanm[:, samp, :], S1], [CN[:, :]])
                cx.act(E1[:, :], CN[:, :], AF.Exp, scale=-CDEC)
                cx.act(E2[:, :], CN[:, :], AF.Exp, scale=CDEC)
                cx.tt("dve", RN[:, :], CN[:, :], S1, ALU.subtract)
                cx.act(E3[:, :], RN[:, :], AF.Exp, scale=-CDEC)
                if samp:
                    cx.cp("act", GamS[:, b, :], E1[:, 7:128:8])
                else:
                    cx.cp("act", Gam[:, b:b + 1], E1[:, 127:128])
                cx.ts("dve", KK[:, :], k_, pc(PV_KK), None, ALU.mult)
                cx.tt("dve", KK2b[:, :], KK[:, :], KK[:, :], ALU.mult)
                cx.mm(ps[1][:, 0:128], bonesb[:, :], KK2b[:, :])
                cx.act(RN[:, :], ps[1][:, 0:128], AF.Ln, bias=epsb[:, 3:4])
                cx.act(RN[:, :], RN[:, :], AF.Exp, scale=-0.5)
                cx.tt("dve", KK[:, :], KK[:, :], RN[:, :], ALU.mult)
                cx.ts("dve", T1[:, :], A1, 1.0, pc(PV_KA), ALU.subtract, ALU.mult)
                cx.stt(KP[:, :], T1[:, :], 1.0, k_, ALU.add, ALU.mult)
                cx.stt(AT[:, b, :], KK[:, :], -1.0, E3[:, :], ALU.mult, ALU.mult)
                cx.tt("dve", T1[:, :], KK[:, :], A1, ALU.mult)
                cx.tt("dve", BT[:, b, :], T1[:, :], E2[:, :], ALU.mult)
                cx.tt("dve", KT[:, b, :], KP[:, :], E2[:, :], ALU.mult)
                cx.tt("dve", RT[:, b, :], r_, E1[:, :], ALU.mult)
                cx.stt(RKb[:, :], r_, pc(PV_RK), KP[:, :], ALU.mult, ALU.mult)
                cx.mm(ps[1][:, 128:256], bonesb[:, :], RKb[:, :])
                cx.cp("act", vb[:, b, :], v_)
                cx.tt("dve", bonus[:, b * 128:(b + 1) * 128], ps[1][:, 128:256], v_, ALU.mult)
                yield
            yield
            for (src, dst) in ((BT, Btok), (KT, Ktok), (vb, Vtok2)):
                for b in range(4):
                    cx.tr(ps_tb[:, b * 128:(b + 1) * 128], src[:, b, :], idb[:, :])
                cx.cp("act", dst[:, :], ps_tb[:, 0:512])
            vt4 = Vtok2.t[:, :].rearrange("p (b c) -> p b c", b=4)
            vp4 = Vpad.t[:, :, :].rearrange("p (b e) c -> p b e c", e=2)
            cx.cp("dve", V(Vpad, vp4[:, :, 0, 0:64]), V(Vtok2, vt4[:, :, 0:64]))
            cx.cp("dve", V(Vpad, vp4[:, :, 1, 64:128]), V(Vtok2, vt4[:, :, 64:128]))
            yield

        def rw_core(st, hg, j, s):
            t = st * TPS + j
            samp = 1 if t == NT - 1 else 0
            if RW_LEVEL < 2 or (samp and SL < 2):
                return
            cs = slice(j * 128, (j + 1) * 128)
            AT, BT, KT, RT, vb, bonus, Gam, Btok, Ktok, Vtok2, Vpad = SETS[s]
            for hl in range(8):
                b = hl // 2
                pr = slice((hl % 2) * 64, (hl % 2) * 64 + 64)
                pL = ps[4 + (hl % 2)]
                cx.mm(pL[:, 0:128], BT[pr, b, :], AT[pr, b, :])
                cx.mm(pL[:, 128:256], BT[pr, b, :], RT[pr, b, :])
                cx.mm(pL[:, 256:384], KT[pr, b, :], AT[pr, b, :])
                cx.mm(pL[:, 384:512], KT[pr, b, :], RT[pr, b, :])
                cx.tt("dve", LT[hl][:, :], pL[:, :], maskT4[:, samp, :], ALU.mult)
                pN = ps[2 + (hl % 2)]
                cx.mm(pN[:, 0:128], AT[pr, b, :], BT[pr, b, :])
                cx.tt("dve", AA0[hl][:, :], pN[:, 0:128], maskN[:, samp, :], ALU.mult)
                yield
            if RW_LEVEL < 3:
                return
            cur = [(AA0[hl][:, :], LT[hl][:, 0:128], LT[hl][:, 0:128]) for hl in range(8)]
            nl = 2 if samp else 6
            for rnd in range(1, nl + 2):
                for hl in range(8):
                    Ac, ATc, Mc = cur[hl]
                    pA = ps[2 + ((hl + 3 * rnd) % 5)]
                    pM = pA
                    do_sq = rnd <= nl
                    do_m = rnd >= 2
                    if do_sq:
                        cx.mm(pA[:, 0:128], ATc, Ac)
                        cx.mm(pA[:, 128:256], Ac, ATc)
                    if do_m:
                        cx.mm(pM[:, 256:384], idb[:, :], ATc, start=True, stop=False)
                        cx.mm(pM[:, 256:384], Ac, Mc, start=False, stop=True)
                    nA, nAT, nM = Ac, ATc, Mc
                    if do_sq:
                        AAn = AAM[rnd % 2][hl]
                        cx.cp("act", AAn[:, 0:256], pA[:, 0:256])
                        nA, nAT = AAn[:, 0:128], AAn[:, 128:256]
                    if do_m:
                        Mn = AAM[rnd % 2][hl]
                        cx.tt("dve", Mn[:, 256:384], pM[:, 256:384], Mc, ALU.add)
                        nM = Mn[:, 256:384]
                    cur[hl] = (nA, nAT, nM)
                    yield
            if RW_LEVEL < 4:
                return
            if samp:
                for b in range(4):
                    chain_sample(hg, b, cur)
            else:
                g4 = slice(hg * 4, hg * 4 + 4)
                pXa, pUa, pHa = ps[2], ps[3], ps[4]
                x4 = Xpad.t[:, :, :].rearrange("p (b e) c -> p b e c", e=2)
                u4 = Upad.t[:, :, :].rearrange("p (b e) c -> p b e c", e=2)
                for b in range(4):
                    gb = hg * 4 + b
                    he, ho = 2 * b, 2 * b + 1
                    o = pXa[:, b * 128:(b + 1) * 128]
                    cx.mm(o, AT[:, b, :], Hb[:, gb, :], start=True, stop=False)
                    cx.mm(o, LT[he][:, 256:384], Vpad[:, he, :], start=False, stop=False)
                    cx.mm(o, LT[ho][:, 256:384], Vpad[:, ho, :], start=False, stop=True)
                yield
                px4 = pXa.t[:, :].rearrange("p (b c) -> p b c", b=4)
                cx.cp("act", V(Xpad, x4[:, :, 0, 0:64]), V(pXa, px4[:, :, 0:64]))
                cx.cp("act", V(Xpad, x4[:, :, 1, 64:128]), V(pXa, px4[:, :, 64:128]))
                for b in range(4):
                    he, ho = 2 * b, 2 * b + 1
                    o = pUa[:, b * 128:(b + 1) * 128]
                    cx.mm(o, idb[:, :], Xpad[:, he, :], start=True, stop=False)
                    cx.mm(o, idb[:, :], Xpad[:, ho, :], start=False, stop=False)
                    cx.mm(o, cur[he][2], Xpad[:, he, :], start=False, stop=False)
                    cx.mm(o, cur[ho][2], Xpad[:, ho, :], start=False, stop=True)
                yield
                pu4 = pUa.t[:, :].rearrange("p (b c) -> p b c", b=4)
                cx.cp("dve", V(Upad, u4[:, :, 0, 0:64]), V(pUa, pu4[:, :, 0:64]))
                cx.cp("dve", V(Upad, u4[:, :, 1, 64:128]), V(pUa, pu4[:, :, 64:128]))
                cx.cp("dve", V(Upair, Upair.t[:, :, :].rearrange("p b c -> p (b c)")), pUa[:, :])
                for b in range(4):
                    gb = hg * 4 + b
                    he, ho = 2 * b, 2 * b + 1
                    pO = ps[6][:, b * 128:(b + 1) * 128]
                    cx.mm(pO, Hb[:, gb, :], RT[:, b, :], start=True, stop=False)
                    cx.mm(pO, Upad[:, he, :], LT[he][:, 128:256], start=False, stop=False)
                    cx.mm(pO, Upad[:, ho, :], LT[ho][:, 128:256], start=False, stop=False)
                    cx.mm(pO, Vpad[:, he, :], LT[he][:, 384:512], start=False, stop=False)
                    cx.mm(pO, Vpad[:, ho, :], LT[ho][:, 384:512], start=False, stop=True)
                    pH = pHa[:, b * 128:(b + 1) * 128]
                    cx.mm(pH, Btok[:, b * 128:(b + 1) * 128], Upair[:, b, :], start=True, stop=False)
                    cx.mm(pH, Ktok[:, b * 128:(b + 1) * 128], Vtok2[:, b * 128:(b + 1) * 128],
                          start=False, stop=True)
                yield
                tq = V(gq, gq.t[:, :].rearrange("p (b c) -> p b c", b=4))
                cx.tt("dve", tq, V(pHa, pHa.t[:, :].rearrange("p (b c) -> p b c", b=4)),
                      V(bonesf, bonesf.t[:, :].unsqueeze(1).to_broadcast([128, 4, 128])), ALU.mult)
                cx.tt("dve", tq, tq, Hf[:, g4, :], ALU.add)
                cx.tt("dve", Hf[:, g4, :], tq,
                      V(Gam, Gam.t[:, 0:4].unsqueeze(2).to_broadcast([128, 4, 128])), ALU.mult)
                cx.cp("act", Hb[:, g4, :], Hf[:, g4, :])
            yield
            cx.cp("act", Of[:, :], ps[6][:, :])
            cx.act(Osq[:, :], ps[6][:, :], AF.Square)
            cx.mm(ps[5][:, :], bonesf[:, :], Of[:, :])
            cx.ts("dve", gm[:, :], ps[5][:, :], 1.0 / 64.0, None, ALU.mult)
            cx.mm(ps[5][:, :], bonesf[:, :], Osq[:, :])
            cx.tt("dve", gq[:, :], gm[:, :], gm[:, :], ALU.mult)
            cx.stt(gq[:, :], ps[5][:, :], 1.0 / 64.0, gq[:, :], ALU.mult, ALU.subtract)
            cx.act(gq[:, :], gq[:, :], AF.Ln, bias=epsb[:, 2:3])
            cx.act(gq[:, :], gq[:, :], AF.Exp, scale=-0.5)
            cx.tt("dve", Of[:, :], Of[:, :], gm[:, :], ALU.subtract)
            cx.tt("dve", Of[:, :], Of[:, :], gq[:, :], ALU.mult)
            for b in range(4):
                gb = hg * 4 + b
                cx.ts("dve", Of[:, b * 128:(b + 1) * 128], Of[:, b * 128:(b + 1) * 128],
                      pv[:, PV_GAG + gb:PV_GAG + gb + 1], pv[:, PV_GAB + gb:PV_GAB + gb + 1],
                      ALU.mult, ALU.add)
            cx.tt("dve", Of[:, :], Of[:, :], bonus[:, :], ALU.add)
            cx.act(V(gs, gs.t[:, :].rearrange("p (h t) -> p h t", h=4)),
                   V(zA, zv[:, 12:16, cs]), AF.Sigmoid)
            cx.tt("dve", V(gs, gs.t[:, :].rearrange("p (h t) -> p h t", h=4)),
                  V(gs, gs.t[:, :].rearrange("p (h t) -> p h t", h=4)), V(zA, zv[:, 12:16, cs]), ALU.mult)
            cx.tt("dve", oT[:, hg * 4:hg * 4 + 4, cs],
                  V(Of, Of.t[:, :].rearrange("p (h t) -> p h t", h=4)),
                  V(gs, gs.t[:, :].rearrange("p (h t) -> p h t", h=4)), ALU.mult)

            yield

        def run_gens(main, side=None, ratio=6):
            k = 0
            while main is not None or side is not None:
                if main is not None:
                    try:
                        next(main)
                    except StopIteration:
                        main = None
                k += 1
                if side is not None and (main is None or k % ratio == 0):
                    try:
                        next(side)
                    except StopIteration:
                        side = None


        for st in range(NST):
            if st == NST - 1 and PIPE:
                cx.barrier()
                cx.memset("dve", Nat[:, :, :], 0.0)
            for j in range(TPS):
                t = st * TPS + j
                cx.dma("sp", xf[:, :], xall[t * 128:(t + 1) * 128, :])
                cx.cp("act", xb[:, :], xf[:, :])
                for half in range(2):
                    for c in range(8):
                        cc = half * 8 + c
                        cx.tr(ps_tb[:, c * 128:(c + 1) * 128], xb[:, cc * 128:(cc + 1) * 128], idb[:, :])
                    cx.cp("dve" if half == 0 else "act",
                          xT[:, half * 8:(half + 1) * 8, j * 128:(j + 1) * 128],
                          V(ps_tb, ps_tb.t[:, :].rearrange("p (c t) -> p c t", c=8)))
            cx.dma("sp", cosb[:, :], cos_d[:, st * STN:(st + 1) * STN])
            cx.dma("sp", sinb[:, :], sin_d[:, st * STN:(st + 1) * STN])

            for hg in range(2):
                for typ in range(4):
                    proj_blocks(typ * 1024 + hg * 512, 4,
                                lambda i, p, typ=typ: mix_evac(st, typ * 8 + hg * 4 + i, typ * 4 + i, p))
                if hg == 0:
                    proj_blocks(4096, 1, lambda i, p: mix_evac(st, 32, 16, p))
                    cx.act(twzx[0:64, :], V(zA, zv[0:64, 16, :]), AF.Tanh)
                    cx.cp("dve", twzx[64:128, :], V(zA, zv[64:128, 16, :]))
                if st < NST - 1 and PIPE:
                    run_gens(rw_prep(st, hg, 0, 0))
                    run_gens(rw_core(st, hg, 0, 0), rw_prep(st, hg, 1, 1))
                    run_gens(rw_core(st, hg, 1, 1), rw_prep(st, hg, 2, 0))
                    run_gens(rw_core(st, hg, 2, 0))
                else:
                    for j in range(TPS):
                        run_gens(rw_prep(st, hg, j, 0))
                        run_gens(rw_core(st, hg, j, 0))

                hs = slice(hg * 4, hg * 4 + 4)
                cx.dma("sp", dmask[:, :, :, :], dmask_v[:, :, hs, :])
                cx.dma("sp", decq[:, :, :, :], decq_v[:, :, hs, :])
                base = NA + hg * 512
                proj_blocks(base, 4, lambda i, p: cx.cp(evac_eng(), V(zA, zv[:, i, :]), p[:, 0:STN]))
                proj_blocks(base + 1024, 4, lambda i, p: cx.act(V(zA, zv[:, 4 + i, :]), p[:, 0:STN], AF.Copy,
                                                                scale=float(128.0 ** -0.5)))
                proj_blocks(base + 3072, 4, lambda i, p: cx.cp(evac_eng(), V(zA, zv[:, 8 + i, :]), p[:, 0:STN]))
                for half in range(2):
                    slot = load_w("in", base + 2048 + half * 256, 256)
                    for j in range(TPS):
                        p = next_ps()
                        for c in range(16):
                            cx.mm(p[:, 0:256], xT[:, c, j * 128:(j + 1) * 128], slot[:, c, :],
                                  start=(c == 0), stop=(c == 15))
                        cx.cp(evac_eng(), vtB[:, j, half * 256:(half + 1) * 256], p[:, 0:256])
                for hh in range(4):
                    for (sb, dst) in ((0, qr), (4, kr)):
                        src = V(zA, zv[:, sb + hh, :])
                        cx.cp("act", zb16[:, :], src)
                        cx.mm(ps[2][:, 0:STN], protb[:, :], zb16[:, :])
                        cx.tt("dve", t1[:, :], src, cosb[:, :], ALU.mult)
                        cx.tt("dve", t2[:, :], ps[2][:, 0:STN], sinb[:, :], ALU.mult)
                        cx.tt("dve", dst[:, hh, :], t1[:, :], t2[:, :], ALU.add)
                for j in range(TPS):
                    t = st * TPS + j
                    samp = 1 if t == NT - 1 else 0
                    cs = slice(j * 128, (j + 1) * 128)
                    for hh in range(4):
                        cx.tr(ps_tb[:, hh * 128:(hh + 1) * 128], kr[:, hh, cs], idb[:, :])
                    cx.cp("act", ktok[:, :], ps_tb[:, 0:512])
                    for hh in range(4):
                        cx.mm(ps[3][:, hh * 128:(hh + 1) * 128], kr[:, hh, cs], qr[:, hh, cs])
                    cx.tt("dve", scm[:, :], ps[3][:, :],
                          V(dmask, dmask.t[:, samp, :, :].rearrange("p h t -> p (h t)")), ALU.mult)
                    cx.tt("dve", qdec[:, :, :], qr[:, :, cs], decq[:, samp, :, :], ALU.mult)
                    dk = pv.t[:, PV_DECK + samp * 8 + hg * 4:PV_DECK + samp * 8 + hg * 4 + 4]
                    cx.tt("dve", vdec[:, :, :],
                          V(vtB, vtB.t[:, j, :].rearrange("p (h e) -> p h e", h=4)),
                          V(pv, dk.unsqueeze(2).to_broadcast([128, 4, 128])), ALU.mult)
                    for hh in range(4):
                        h = hg * 4 + hh
                        o = ps[4][:, hh * 128:(hh + 1) * 128]
                        if not samp:
                            cx.mm(o, vtB[:, j, hh * 128:(hh + 1) * 128], scm[:, hh * 128:(hh + 1) * 128],
                                  start=True, stop=False)
                            cx.mm(o, Sretb[:, h, :], qdec[:, hh, :], start=False, stop=True)
                            cx.mm(ps[5][:, hh * 128:(hh + 1) * 128], ktok[:, hh * 128:(hh + 1) * 128],
                                  vdec[:, hh, :])
                        else:
                            cx.mm(o, vtB[:, j, hh * 128:(hh + 1) * 128], scm[:, hh * 128:(hh + 1) * 128],
                                  start=True, stop=False)
                            g8 = float(np.exp(np.float32(8.0 * LOG_G[h])))
                            for sh in range(2):
                                cx.dma("sp", SS[:, :, :],
                                       sret_d[sh * 8:(sh + 1) * 8, h].rearrange("s d e -> d s e"))
                                cx.cp("act", SSb[:, 0:8, :], SS[:, :, :])
                                for i8 in range(8):
                                    i = sh * 8 + i8
                                    cx.mm(ps[4][:, hh * 128 + 8 * i:hh * 128 + 8 * i + 8], SSb[:, i8, :],
                                          qdec[:, hh, 8 * i:8 * i + 8], start=False, stop=(i == 15))
                                bm = pv.t[:, PV_BM + sh * 8:PV_BM + sh * 8 + 8]
                                cx.tt("dve", vexp[:, :, :],
                                      V(vdec, vdec.t[:, hh, :].unsqueeze(1).to_broadcast([128, 8, 128])),
                                      V(pv, bm.unsqueeze(2).to_broadcast([128, 8, 128])), ALU.mult)
                                for q4 in range(2):
                                    pp = ps[5 + (q4 % 2)]
                                    cx.mm(pp[:, :], ktok[:, hh * 128:(hh + 1) * 128],
                                          V(vexp, vexp.t[:, q4 * 4:(q4 + 1) * 4, :].rearrange("p s e -> p (s e)")))
                                    cx.stt(V(SSn, SSn.t[:, q4 * 4:(q4 + 1) * 4, :].rearrange("p s e -> p (s e)")),
                                           V(SS, SS.t[:, q4 * 4:(q4 + 1) * 4, :].rearrange("p s e -> p (s e)")),
                                           g8, pp[:, :], ALU.mult, ALU.add)
                                cx.dma("sp", ret_s[sh * 8:(sh + 1) * 8, h].rearrange("s d e -> d s e"), SSn[:, :, :])
                    if not samp:
                        for hh in range(4):
                            h = hg * 4 + hh
                            gC = float(np.exp(np.float32(128.0 * LOG_G[h])))
                            cx.stt(Sret[:, h, :], Sret[:, h, :], gC, ps[5][:, hh * 128:(hh + 1) * 128],
                                   ALU.mult, ALU.add)
                        cx.cp("act", Sretb[:, hs, :], Sret[:, hs, :])
                    cx.cp("act", Of[:, :], ps[4][:, :])
                    cx.act(Osq[:, :], ps[4][:, :], AF.Square)
                    cx.mm(ps[6][:, :], onesf[:, :], Of[:, :])
                    cx.ts("dve", gm[:, :], ps[6][:, :], 1.0 / 128.0, None, ALU.mult)
                    cx.mm(ps[6][:, :], onesf[:, :], Osq[:, :])
                    cx.tt("dve", gq[:, :], gm[:, :], gm[:, :], ALU.mult)
                    cx.stt(gq[:, :], ps[6][:, :], 1.0 / 128.0, gq[:, :], ALU.mult, ALU.subtract)
                    cx.act(gq[:, :], gq[:, :], AF.Ln, bias=epsb[:, 0:1])
                    cx.act(gq[:, :], gq[:, :], AF.Exp, scale=-0.5)
                    cx.tt("dve", Of[:, :], Of[:, :], gm[:, :], ALU.subtract)
                    cx.tt("dve", Of[:, :], Of[:, :], gq[:, :], ALU.mult)
                    for hh in range(4):
                        h = hg * 4 + hh
                        cx.ts("dve", Of[:, hh * 128:(hh + 1) * 128], Of[:, hh * 128:(hh + 1) * 128],
                              pv[:, PV_GBG + h:PV_GBG + h + 1], pv[:, PV_GBB + h:PV_GBB + h + 1],
                              ALU.mult, ALU.add)
                    cx.act(V(gs, gs.t[:, :].rearrange("p (h t) -> p h t", h=4)),
                           V(zA, zv[:, 8:12, cs]), AF.Sigmoid)
                    cx.tt("dve", V(gs, gs.t[:, :].rearrange("p (h t) -> p h t", h=4)),
                          V(gs, gs.t[:, :].rearrange("p (h t) -> p h t", h=4)), V(zA, zv[:, 8:12, cs]), ALU.mult)
                    cx.tt("dve", oT[:, 8 + hg * 4:8 + hg * 4 + 4, cs],
                          V(Of, Of.t[:, :].rearrange("p (h t) -> p h t", h=4)),
                          V(gs, gs.t[:, :].rearrange("p (h t) -> p h t", h=4)), ALU.mult)

            for j in range(TPS):
                t = st * TPS + j
                cx.dma("sp", V(zA, hv[:, j, :]), xall[t * 128:(t + 1) * 128, :])
            cx.act(V(zA, zA.t[:, 0:3 * D]), V(zA, zA.t[:, 0:3 * D]), AF.Copy, scale=float(ALPHA))
            for g8i in range(8):
                slot = load_w("out", g8i * 256, 256)
                for j in range(TPS):
                    t = st * TPS + j
                    if t == 0:
                        continue
                    p = next_ps()
                    for c in range(16):
                        cx.mm(p[:, 0:256], oT[:, c, j * 128:(j + 1) * 128], slot[:, c, :],
                              start=(c == 0), stop=(c == 15))
                    hsl = V(zA, hv[:, j, g8i * 256:(g8i + 1) * 256])
                    cx.tt("dve", hsl, hsl, p[:, 0:256], ALU.add)
            tiles_ln = [j for j in range(TPS) if st * TPS + j != 0]
            for j in tiles_ln:
                hj = V(zA, hv[:, j, :])
                cx.op("dve", lambda eng, o=lnst.t[:, 0:1], i=hv[:, j, :]: eng.tensor_reduce(
                    o, i, mybir.AxisListType.X, ALU.add), [hj], [lnst[:, 0:1]])
                cx.act(xf[:, :], hj, AF.Square)
                cx.op("dve", lambda eng, o=lnst.t[:, 1:2], i=xf.t[:, :]: eng.tensor_reduce(
                    o, i, mybir.AxisListType.X, ALU.add), [xf[:, :]], [lnst[:, 1:2]])
                cx.ts("dve", lnst[:, 2:3], lnst[:, 0:1], 1.0 / D, None, ALU.mult)
                cx.tt("dve", lnst[:, 3:4], lnst[:, 2:3], lnst[:, 2:3], ALU.mult)
                cx.stt(lnst[:, 4:5], lnst[:, 1:2], 1.0 / D, lnst[:, 3:4], ALU.mult, ALU.subtract)
                cx.act(lnst[:, 5:6], lnst[:, 4:5], AF.Ln, bias=epsb[:, 1:2])
                cx.act(lnst[:, 5:6], lnst[:, 5:6], AF.Exp, scale=-0.5)
                cx.ts("dve", hj, hj, lnst[:, 2:3], lnst[:, 5:6], ALU.subtract, ALU.mult)
            cx.dma("sp", lnbuf[:, :], lng_d[0, :].partition_broadcast(128))
            for j in tiles_ln:
                hj = V(zA, hv[:, j, :])
                cx.tt("dve", hj, hj, lnbuf[:, :], ALU.mult)
            cx.dma("sp", lnbuf[:, :], lnb_d[0, :].partition_broadcast(128))
            for j in tiles_ln:
                t = st * TPS + j
                hj = V(zA, hv[:, j, :])
                cx.tt("dve", hj, hj, lnbuf[:, :], ALU.add)
                if t == NT - 1:
                    cx.dma("sp", ys_d, hj)
                else:
                    cx.dma("sp", yp_d[(t - 1) * 128:t * 128, :], hj)

        cx.dma("sp", ret_p.rearrange("h d e -> d h e"), Sret[:, :, :])
        for gb in range(8):
            pT = ps[gb % 2]
            cx.tr(pT[:, 0:128], Hf[:, gb, :], idf[:, :])
            cx.cp("act", Of[:, 0:128], pT[:, 0:128])
            for e in range(2):
                cx.dma("sp", rwkv_p[2 * gb + e], Of[e * 64:(e + 1) * 64, e * 64:(e + 1) * 64])
        for grp in range(9):
            nb = 4 if grp < 8 else 1
            pa, pb = ps[0], ps[1]
            for k in range(nb):
                blk = grp * 4 + k
                cx.tr(pa[0:16, k * 128:(k + 1) * 128], raws[:, blk, :], idf[:, :])
                cx.tr(pb[0:1, k * 128:(k + 1) * 128], rawp[:, blk:blk + 1], idf[:, :])
            cx.cp("dve", rowb[0:16, 0:nb * 128], pa[0:16, 0:nb * 128])
            cx.cp("act", rowp[0:1, 0:nb * 128], pb[0:1, 0:nb * 128])
            cx.dma("sp", shift_s[:, grp * 512:grp * 512 + nb * 128], rowb[0:16, 0:nb * 128])
            cx.dma("sp", shift_p[:, grp * 512:grp * 512 + nb * 128], rowp[0:1, 0:nb * 128])
        cx.finish()
    return nc


def dmask_or(decq, samp, hs):
    return V(decq, decq.t[:, samp, hs, :])


_NC_CACHE = {}


def _consts():
    f32 = np.float32
    c = {}
    c["ident_f"] = np.eye(128, dtype=f32)
    P = np.zeros((128, 128), f32)
    for m in range(64):
        P[m + 64, m] = -1.0
    for m in range(64, 128):
        P[m - 64, m] = 1.0
    c["prot_f"] = P
    half = 64
    inv = (np.float32(10000.0) ** (-(np.arange(half, dtype=f32) / np.float32(half)))).astype(f32)
    pos = np.zeros(NT * 128, f32)
    pos[112:128] = np.arange(16)
    pos[128:17 * 128] = 16 + np.arange(2048)
    pos[17 * 128:] = 16384 + (np.arange(128) % 8)
    ang = (pos[None, :] * inv[:, None]).astype(f32)
    cos = np.cos(ang).astype(f32)
    sin = np.sin(ang).astype(f32)
    c["cos_d"] = np.ascontiguousarray(np.concatenate([cos, cos], axis=0))
    c["sin_d"] = np.ascontiguousarray(np.concatenate([sin, sin], axis=0))
    lg = np.array(LOG_G, f32)
    j = np.arange(128)[:, None]
    i = np.arange(128)[None, :]
    dm = np.zeros((128, 2, 8, 128), f32)
    dq = np.zeros((128, 2, 8, 128), f32)
    deck = np.zeros((128, 16), f32)
    for h in range(8):
        diff = (i - j).astype(f32)
        dm[:, 0, h, :] = np.where(diff >= 0, np.exp(np.maximum(diff, 0) * lg[h]), 0.0)
        same = (i // 8) == (j // 8)
        dm[:, 1, h, :] = np.where((diff >= 0) & same, np.exp(np.maximum(diff, 0) * lg[h]), 0.0)
        dq[:, 0, h, :] = np.exp((np.arange(128, dtype=f32) + 1.0) * lg[h])[None, :]
        dq[:, 1, h, :] = np.exp(((np.arange(128) % 8).astype(f32) + 1.0) * lg[h])[None, :]
        deck[:, h] = np.exp((127.0 - np.arange(128, dtype=f32)) * lg[h])
        deck[:, 8 + h] = np.exp((7.0 - (np.arange(128) % 8).astype(f32)) * lg[h])
    c["dmask_d"] = np.ascontiguousarray(dm.reshape(128, -1))
    c["decq_d"] = np.ascontiguousarray(dq.reshape(128, -1))
    c["deck"] = deck
    bm = np.zeros((128, 16), f32)
    bm[np.arange(128), np.arange(128) // 8] = 1.0
    c["bm"] = bm
    bo = np.zeros((128, 128), f32)
    bo[0:64, 0:64] = 1.0
    bo[64:128, 64:128] = 1.0
    c["bones_d"] = bo
    s_ = np.arange(128)[:, None]
    t_ = np.arange(128)[None, :]
    same = (s_ // 8) == (t_ // 8)
    m4 = np.zeros((128, 2, 512), f32)
    for a, sm in ((0, np.ones_like(same)), (1, same)):
        strict = ((t_ > s_) & sm).astype(f32)
        incl = ((t_ >= s_) & sm).astype(f32)
        m4[:, a, 0:128] = strict
        m4[:, a, 128:256] = incl
        m4[:, a, 256:384] = strict
        m4[:, a, 384:512] = incl
    c["maskT4_d"] = np.ascontiguousarray(m4.reshape(128, 1024))
    mn = np.zeros((128, 2, 128), f32)
    mn[:, 0, :] = (t_.T > s_.T).astype(f32) if False else (np.arange(128)[None, :] < np.arange(128)[:, None]).astype(f32)
    mn[:, 1, :] = ((np.arange(128)[None, :] < np.arange(128)[:, None]) & same).astype(f32)
    c["maskN_d"] = np.ascontiguousarray(mn.reshape(128, 256))
    sc = np.ones((128, 2, 128), f32)
    sc[:, 1, :] = (np.arange(128) % 8 != 0).astype(f32)[None, :]
    c["scanm_d"] = np.ascontiguousarray(sc.reshape(128, 256))
    return c


def kernel(x_prompt, x_sample, state_rwkv, state_shift, state_ret, meta_tokens, w_in, w_out, shift_mix,
           w0, w_up, a0, a_up, k_k, k_a, r_k, gn_a_g, gn_a_b, gn_b_g, gn_b_b, ln_g, ln_b):
    f32 = np.float32
    A = lambda v: np.asarray(v, f32)
    x_prompt, x_sample, meta = A(x_prompt), A(x_sample), A(meta_tokens)
    if "nc" not in _NC_CACHE:
        _NC_CACHE["nc"] = build()
    nc = _NC_CACHE["nc"]
    c = _consts()
    pvh = np.zeros((128, NPV), f32)
    fm = lambda v, nb: A(v).reshape(nb, 128).T
    pvh[:, PV_MIX:PV_MIX + 33] = fm(shift_mix[0], 33)
    pvh[:, PV_W0:PV_W0 + 8] = fm(w0[0], 8)
    pvh[:, PV_A0:PV_A0 + 8] = fm(a0[0], 8)
    pvh[:, PV_KK:PV_KK + 8] = fm(k_k[0], 8)
    pvh[:, PV_KA:PV_KA + 8] = fm(k_a[0], 8)
    pvh[:, PV_RK:PV_RK + 8] = fm(A(r_k)[0].reshape(-1), 8)
    pvh[:, PV_GAG:PV_GAG + 8] = fm(gn_a_g[0], 8)
    pvh[:, PV_GAB:PV_GAB + 8] = fm(gn_a_b[0], 8)
    pvh[:, PV_GBG:PV_GBG + 8] = fm(gn_b_g[0], 8)
    pvh[:, PV_GBB:PV_GBB + 8] = fm(gn_b_b[0], 8)
    pvh[:, PV_DECK:PV_DECK + 16] = c["deck"]
    pvh[:, PV_BM:PV_BM + 16] = c["bm"]
    shared = {
        "w_in": np.ascontiguousarray(A(w_in)[0]), "w_out": np.ascontiguousarray(A(w_out)[0]),
        "ident_f": c["ident_f"], "prot_f": c["prot_f"], "cos_d": c["cos_d"], "sin_d": c["sin_d"],
        "dmask_d": c["dmask_d"], "decq_d": c["decq_d"], "pv_d": pvh,
        "lng_d": np.ascontiguousarray(A(ln_g)), "lnb_d": np.ascontiguousarray(A(ln_b)),
        "bones_d": c["bones_d"], "maskT4_d": c["maskT4_d"], "maskN_d": c["maskN_d"], "scanm_d": c["scanm_d"],
        "lora_d": np.ascontiguousarray(np.concatenate([A(w_up)[0], A(a_up)[0]], axis=0)),
    }
    sret = A(state_ret)[0]
    srw = A(state_rwkv)[0]
    ssh = A(state_shift)[0]
    in_maps = []
    for cid in range(8):
        b = cid % 4
        xall = np.concatenate([np.zeros((112, D), f32), meta, x_prompt[b],
                               x_sample[16 * cid:16 * cid + 16].reshape(128, D)], axis=0)
        m = dict(shared)
        m["xall"] = np.ascontiguousarray(xall)
        m["sret_d"] = np.ascontiguousarray(sret[16 * cid:16 * cid + 16])
        m["srwkv_d"] = np.ascontiguousarray(srw[16 * cid:16 * cid + 16])
        m["sshift_d"] = np.ascontiguousarray(ssh[16 * cid:16 * cid + 16])
        in_maps.append(m)
    res = run_bass_kernel_spmd(nc, in_maps, core_ids=list(range(8)))
    R = res.results
    y_prompt = np.stack([R[b]["yp"] for b in range(4)]).astype(f32)
    y_sample = np.concatenate([R[cid]["ys"].reshape(16, 8, D) for cid in range(8)], axis=0).astype(f32)
    rwkv_p = np.stack([R[b]["rwkv_p"] for b in range(4)])[None].astype(f32)
    shift_p = np.stack([R[b]["shift_p"][0] for b in range(4)])[None].astype(f32)
    ret_p = np.stack([R[b]["ret_p"] for b in range(4)])[None].astype(f32)
    rwkv_s = np.concatenate([R[cid]["rwkv_s"] for cid in range(8)], axis=0)[None].astype(f32)
    shift_s = np.concatenate([R[cid]["shift_s"] for cid in range(8)], axis=0)[None].astype(f32)
    ret_s = np.concatenate([R[cid]["ret_s"] for cid in range(8)], axis=0)[None].astype(f32)
    return (y_prompt, y_sample, rwkv_p, shift_p, ret_p, rwkv_s, shift_s, ret_s)
```

```python
import contextlib
import numpy as np
import concourse.bass as bass
import concourse.mybir as mybir
from concourse.bass_utils import run_bass_kernel_spmd

F32 = mybir.dt.float32
BF16 = mybir.dt.bfloat16
AF = mybir.ActivationFunctionType
ALU = mybir.AluOpType

D = 2048
NT = 18
TPS = 3
NST = NT // TPS
STN = TPS * 128
NA = 4224
NB = 4096
NIN = NA + NB
NBLK_A = 33
SAME_ENGINE_SYNC = True
SAME_ENGINE_WINDOW = 3


class Buf:
    def __init__(self, t, name):
        self.t = t
        self.name = name
        self.w = None
        self.r = {}
        self.dsem = None
        self.dcnt = 0
        self.psum = False

    def __getitem__(self, idx):
        return V(self, self.t[idx])


class V:
    def __init__(self, buf, ap):
        self.buf = buf
        self.ap = ap


def _ap(x):
    return x.ap if isinstance(x, V) else x


class Ctx:
    ENG = ["pe", "act", "dve", "pool", "sp"]

    def __init__(self, nc, stack):
        self.nc = nc
        self.stack = stack
        self.prog = {e: [] for e in self.ENG}
        self.cnt = {e: 0 for e in self.ENG}
        self.sem = {e: stack.enter_context(nc.semaphore("sem_" + e)) for e in self.ENG}
        self.seen = {e: {} for e in self.ENG}
        self.nbuf = 0
        self.dma_sems = []

    def sbuf(self, shape, dt, name=None):
        self.nbuf += 1
        name = name or ("sb%d" % self.nbuf)
        t = self.stack.enter_context(self.nc.sbuf_tensor(name, list(shape), dt))
        return Buf(t, name)

    def psum(self, shape, dt, name=None):
        self.nbuf += 1
        name = name or ("ps%d" % self.nbuf)
        t = self.stack.enter_context(self.nc.psum_tensor(name, list(shape), dt))
        bf = Buf(t, name)
        bf.psum = True
        return bf

    def _need(self, e, dep, waits):
        if dep is None:
            return
        kind, key, val, sem = dep
        if kind == "eng" and key == e:
            if not SAME_ENGINE_SYNC or e in ("pe", "sp"):
                return
            if SAME_ENGINE_WINDOW and val <= self.cnt[e] - SAME_ENGINE_WINDOW:
                return
        if self.seen[e].get(key, 0) >= val:
            return
        waits[key] = (sem, max(val, waits.get(key, (None, 0))[1]))

    def _deps(self, e, reads, writes, skip_same_pe=True):
        waits = {}
        for b in reads:
            self._need(e, b.w, waits)
            if b.psum:
                for k_, d in b.r.items():
                    if k_ != e:
                        self._need(e, d, waits)
        for b in writes:
            self._need(e, b.w, waits)
            for d in b.r.values():
                self._need(e, d, waits)
        out = []
        for key, (sem, val) in waits.items():
            self.seen[e][key] = val
            out.append((sem, val))
        return out

    def op(self, e, fn, reads, writes):
        reads = [x.buf for x in reads if isinstance(x, V)]
        writes = [x.buf for x in writes if isinstance(x, V)]
        waits = self._deps(e, reads, writes)
        self.cnt[e] += 1
        n = self.cnt[e]
        sem = self.sem[e]

        def emit(eng):
            for (s, v) in waits:
                eng.wait_ge(s, v)
            fn(eng).then_inc(sem, 1)
        self.prog[e].append(emit)
        dep = ("eng", e, n, sem)
        for b in reads:
            b.r[e] = dep
        for b in writes:
            b.w = dep
            b.r = {}

    def dma(self, q, out, in_, extra_waits=(), **kw):
        reads = [in_.buf] if isinstance(in_, V) else []
        writes = [out.buf] if isinstance(out, V) else []
        waits = self._deps(q, reads, writes) + list(extra_waits)
        tb = (writes + reads)[0]
        if tb.dsem is None:
            tb.dsem = self.stack.enter_context(self.nc.semaphore("dsem_" + tb.name))
            self.dma_sems.append(tb)
        tb.dcnt += 16
        val = tb.dcnt
        sem = tb.dsem
        o, i = _ap(out), _ap(in_)

        def emit(eng):
            for (s, v) in waits:
                eng.wait_ge(s, v)
            eng.dma_start(out=o, in_=i, **kw).then_inc(sem, 16)
        self.prog[q].append(emit)
        dep = ("dma", "d_" + tb.name, val, sem)
        for b in reads:
            b.r["d_" + tb.name] = dep
        for b in writes:
            b.w = dep
            b.r = {}

    def barrier(self):
        targets = [(self.sem[e], self.cnt[e], e) for e in self.ENG if self.cnt[e] > 0]
        dmas = [(b.dsem, b.dcnt, "d_" + b.name) for b in self.dma_sems]
        for e in self.ENG:
            ws = [(s, v) for (s, v, k) in targets if k != e] + [(s, v) for (s, v, k) in dmas]

            def emit(eng, ws=ws):
                for (s, v) in ws:
                    eng.wait_ge(s, v)
            self.prog[e].append(emit)
            for (s, v, k) in targets + dmas:
                if k != e:
                    self.seen[e][k] = max(self.seen[e].get(k, 0), v)

    def mm(self, out, lhsT, rhs, start=True, stop=True):
        o, l, r = out.ap, lhsT.ap, rhs.ap
        self.op("pe", lambda eng: eng.matmul(o, l, r, start=start, stop=stop), [lhsT, rhs], [out])

    def tr(self, out, in_, ident):
        o, i, d = out.ap, in_.ap, ident.ap
        self.op("pe", lambda eng: eng.transpose(o, i, d), [in_, ident], [out])

    def act(self, out, in_, func, bias=None, scale=1.0):
        o, i = out.ap, in_.ap
        rd = [in_]
        kw = {}
        if bias is not None:
            kw["bias"] = _ap(bias)
            if isinstance(bias, V):
                rd.append(bias)
        if isinstance(scale, V):
            rd.append(scale)
        sc = _ap(scale)
        self.op("act", lambda eng: eng.activation(o, i, func, scale=sc, **kw), rd, [out])

    def tt(self, e, out, a, b, op):
        o, x, y = out.ap, a.ap, b.ap
        self.op(e, lambda eng: eng.tensor_tensor(o, x, y, op), [a, b], [out])

    def ts(self, e, out, a, s1, s2, op0, op1=None):
        o, x = out.ap, a.ap
        rd = [a] + [s for s in (s1, s2) if isinstance(s, V)]
        a1, a2 = _ap(s1), _ap(s2)
        if op1 is None:
            self.op(e, lambda eng: eng.tensor_scalar(o, x, a1, None, op0), rd, [out])
        else:
            self.op(e, lambda eng: eng.tensor_scalar(o, x, a1, a2, op0, op1), rd, [out])

    def stt(self, out, in0, scalar, in1, op0, op1):
        o, x, y = out.ap, in0.ap, in1.ap
        rd = [in0, in1] + ([scalar] if isinstance(scalar, V) else [])
        s = _ap(scalar)
        self.op("dve", lambda eng: eng.scalar_tensor_tensor(o, x, s, y, op0, op1), rd, [out])

    def cp(self, e, out, in_):
        o, i = out.ap, in_.ap
        if e == "act":
            self.op(e, lambda eng: eng.copy(o, i), [in_], [out])
        else:
            self.op(e, lambda eng: eng.tensor_copy(o, i), [in_], [out])

    def memset(self, e, out, val):
        o = out.ap
        self.op(e, lambda eng: eng.memset(o, val), [], [out])

    def finish(self):
        finals = [(b.dsem, b.dcnt) for b in self.dma_sems]
        engs = [(e, self.sem[e], self.cnt[e]) for e in self.ENG if e != "sp" and self.cnt[e] > 0]

        def emit(eng):
            for (s, v) in finals:
                eng.wait_ge(s, v)
            for (_, s, v) in engs:
                eng.wait_ge(s, v)
        self.prog["sp"].append(emit)
        prog = self.prog
        with self.nc.Block() as block:
            @block.tensor
            def _(eng):
                for f in prog["pe"]:
                    f(eng)

            @block.scalar
            def _(eng):
                for f in prog["act"]:
                    f(eng)

            @block.vector
            def _(eng):
                for f in prog["dve"]:
                    f(eng)

            @block.gpsimd
            def _(eng):
                for f in prog["pool"]:
                    f(eng)

            @block.sync
            def _(eng):
                for f in prog["sp"]:
                    f(eng)


ALPHA = 2.0 ** 0.25
LN_EPS = 1e-5
GN_EPS_A = 64e-5
GN_EPS_B = 1e-5
LOG_G = [float(np.log1p(-np.exp2(np.float32(-5.0 - h)))) for h in range(8)]
import os
RW_LEVEL = int(os.environ.get("RW_LEVEL", "4"))
SL = int(os.environ.get("SL", "4"))
PIPE = int(os.environ.get("PIPE", "1"))

PV_MIX = 0
PV_W0 = 33
PV_A0 = 41
PV_KK = 49
PV_KA = 57
PV_RK = 65
PV_GAG = 73
PV_GAB = 81
PV_GBG = 89
PV_GBB = 97
PV_DECK = 105
PV_BM = 121
NPV = 137


def build():
    nc = bass.Bass("TRN2", target_bir_lowering=False)

    def din(name, shape, dt=F32):
        return nc.dram_tensor(name, list(shape), dt, kind="ExternalInput").ap()

    def dout(name, shape, dt=F32):
        return nc.dram_tensor(name, list(shape), dt, kind="ExternalOutput").ap()

    xall = din("xall", [NT * 128, D])
    w_in = din("w_in", [D, NIN])
    w_out = din("w_out", [D, D])
    ident_f = din("ident_f", [128, 128])
    prot_f = din("prot_f", [128, 128])
    cos_d = din("cos_d", [128, NT * 128])
    sin_d = din("sin_d", [128, NT * 128])
    dmask_d = din("dmask_d", [128, 2 * 8 * 128])
    decq_d = din("decq_d", [128, 2 * 8 * 128])
    pv_d = din("pv_d", [128, NPV])
    lng_d = din("lng_d", [1, D])
    lnb_d = din("lnb_d", [1, D])
    sret_d = din("sret_d", [16, 8, 128, 128])
    bones_d = din("bones_d", [128, 128])
    maskT4_d = din("maskT4_d", [128, 1024])
    maskN_d = din("maskN_d", [128, 256])
    scanm_d = din("scanm_d", [128, 256])
    lora_d = din("lora_d", [128, 1024])
    rwkv_p = dout("rwkv_p", [16, 64, 64])
    srwkv_d = din("srwkv_d", [16, 16, 64, 64])
    sshift_d = din("sshift_d", [16, NA])
    rwkv_s = dout("rwkv_s", [16, 16, 64, 64])
    wbf = nc.dram_tensor("wbf", [33, 128, 16, 256], BF16, kind="Internal").ap()
    wobf = nc.dram_tensor("wobf", [8, 128, 16, 256], BF16, kind="Internal").ap()
    shift_p = dout("shift_p", [1, NA])
    shift_s = dout("shift_s", [16, NA])
    ret_p = dout("ret_p", [8, 128, 128])
    ret_s = dout("ret_s", [16, 8, 128, 128])
    yp_d = dout("yp", [2048, D])
    ys_d = dout("ys", [128, D])

    with contextlib.ExitStack() as stack:
        cx = Ctx(nc, stack)
        CDEC = float(np.exp(-0.5))
        xf = cx.sbuf([128, D], F32, "xf")
        xb = cx.sbuf([128, D], BF16, "xb")
        xT = cx.sbuf([128, 16, STN], BF16, "xT")
        ZW = 17 * STN
        zA = cx.sbuf([128, ZW], F32, "zA")
        zv = zA.t[:, :].rearrange("p (b t) -> p b t", b=17)
        hv = zA.t[:, 0:3 * D].rearrange("p (j d) -> p j d", j=3)
        wsl = [cx.sbuf([128, 16, 256], BF16, "wsl%d" % i) for i in range(2)]
        oT = cx.sbuf([128, 16, STN], BF16, "oT")
        ps = [cx.psum([128, 512], F32, "ps%d" % i) for i in range(7)]
        ps_tb = cx.psum([128, 1024], BF16, "pstb")

        idf = cx.sbuf([128, 128], F32, "idf")
        idb = cx.sbuf([128, 128], BF16, "idb")
        protb = cx.sbuf([128, 128], BF16, "protb")
        onesf = cx.sbuf([128, 128], F32, "onesf")
        bonesf = cx.sbuf([128, 128], F32, "bonesf")
        bonesb = cx.sbuf([128, 128], BF16, "bonesb")
        pv = cx.sbuf([128, NPV], F32, "pv")
        omm = cx.sbuf([128, 33], F32, "omm")
        dmask = cx.sbuf([128, 2, 4, 128], F32, "dmask")
        decq = cx.sbuf([128, 2, 4, 128], F32, "decq")
        maskT4 = cx.sbuf([128, 2, 512], BF16, "maskT4")
        maskN = cx.sbuf([128, 2, 128], BF16, "maskN")
        scanm = cx.sbuf([128, 2, 128], F32, "scanm")
        lorab = cx.sbuf([128, 1024], BF16, "lorab")
        epsb = cx.sbuf([128, 4], F32, "epsb")
        cx.dma("sp", idf[:, :], ident_f)
        cx.cp("dve", idb[:, :], idf[:, :])
        cx.dma("sp", xf[:, 0:128], prot_f)
        cx.cp("dve", protb[:, :], xf[:, 0:128])
        cx.dma("sp", bonesf[:, :], bones_d)
        cx.cp("dve", bonesb[:, :], bonesf[:, :])
        cx.memset("dve", onesf[:, :], 1.0)
        cx.memset("dve", epsb[:, 0:1], GN_EPS_B)
        cx.memset("dve", epsb[:, 1:2], LN_EPS)
        cx.memset("dve", epsb[:, 2:3], GN_EPS_A)
        cx.memset("dve", epsb[:, 3:4], 1e-18)
        cx.dma("sp", pv[:, :], pv_d)
        cx.ts("dve", omm[:, :], pv[:, PV_MIX:PV_MIX + 33], -1.0, 1.0, ALU.mult, ALU.add)
        cx.dma("sp", xf[:, 0:1024], maskT4_d)
        cx.cp("dve", V(maskT4, maskT4.t[:, :, :].rearrange("p a t -> p (a t)")), xf[:, 0:1024])
        cx.dma("sp", xf[:, 0:256], maskN_d)
        cx.cp("dve", V(maskN, maskN.t[:, :, :].rearrange("p a t -> p (a t)")), xf[:, 0:256])
        cx.dma("sp", V(scanm, scanm.t[:, :, :].rearrange("p a t -> p (a t)")), scanm_d)
        cx.dma("sp", xf[:, 0:1024], lora_d)
        cx.cp("dve", lorab[:, :], xf[:, 0:1024])

        vtB = cx.sbuf([128, 3, 512], BF16, "vtB")
        cosb = cx.sbuf([128, STN], F32, "cosb")
        sinb = cx.sbuf([128, STN], F32, "sinb")
        qr = cx.sbuf([128, 4, STN], BF16, "qr")
        kr = cx.sbuf([128, 4, STN], BF16, "kr")
        zb16 = cx.sbuf([128, STN], BF16, "zb16")
        t2 = cx.sbuf([128, STN], F32, "t2")
        ktok = cx.sbuf([128, 512], BF16, "ktok")
        scm = cx.sbuf([128, 512], BF16, "scm")
        qdec = cx.sbuf([128, 4, 128], BF16, "qdec")
        vdec = cx.sbuf([128, 4, 128], BF16, "vdec")
        Sret = cx.sbuf([128, 8, 128], F32, "Sret")
        Sretb = cx.sbuf([128, 8, 128], BF16, "Sretb")
        Rg = cx.sbuf([128, 5120], F32, "Rg").t

        def rview(a, n, dt, name, pat=None, **kw):
            ap = Rg[:, a:a + n]
            if dt == BF16:
                ap = ap.bitcast(BF16)
            if pat is not None:
                ap = ap.rearrange(pat, **kw)
            return Buf(ap, name)
        SS = rview(0, 1024, F32, "SS", "p (s e) -> p s e", s=8)
        SSn = rview(1024, 1024, F32, "SSn", "p (s e) -> p s e", s=8)
        Nat = rview(2048, 1024, F32, "Nat", "p (s e) -> p s e", s=8)
        SSb = rview(3072, 1024, BF16, "SSb", "p (s e) -> p s e", s=16)
        vexp = rview(4096, 512, BF16, "vexp", "p (s e) -> p s e", s=8)
        Vexp2 = rview(4608, 512, BF16, "Vexp2", "p (s e) -> p s e", s=8)
        Of = cx.sbuf([128, 512], F32, "Of")
        Osq = cx.sbuf([128, 512], F32, "Osq")
        gm = cx.sbuf([128, 512], F32, "gm")
        gq = cx.sbuf([128, 512], F32, "gq")
        gs = cx.sbuf([128, 512], F32, "gs")
        lnst = cx.sbuf([128, 8], F32, "lnst")
        carry = cx.sbuf([128, 33], F32, "carry")
        rawp = cx.sbuf([128, 33], F32, "rawp")
        raws = cx.sbuf([128, 33, 16], F32, "raws")
        zc1 = cx.sbuf([128, STN], F32, "zc1")
        t1 = zc1
        lnbuf = xf
        rowb = Of
        rowp = gm
        twzx = cx.sbuf([128, STN], BF16, "twzx")
        AT = cx.sbuf([128, 4, 128], BF16, "AT")
        BT = cx.sbuf([128, 4, 128], BF16, "BT")
        KT = cx.sbuf([128, 4, 128], BF16, "KT")
        RT = cx.sbuf([128, 4, 128], BF16, "RT")
        vb = cx.sbuf([128, 4, 128], BF16, "vb")
        bonus = cx.sbuf([128, 512], F32, "bonus")
        S1 = cx.sbuf([128, 128], F32, "S1")
        A1 = cx.sbuf([128, 128], F32, "A1")
        CN = cx.sbuf([128, 128], F32, "CN")
        E1 = cx.sbuf([128, 128], F32, "E1")
        E2 = cx.sbuf([128, 128], F32, "E2")
        E3 = cx.sbuf([128, 128], F32, "E3")
        KK = cx.sbuf([128, 128], F32, "KK")
        RN = cx.sbuf([128, 128], F32, "RN")
        T1 = cx.sbuf([128, 128], F32, "T1")
        KP = cx.sbuf([128, 128], F32, "KP")
        KK2b = cx.sbuf([128, 128], BF16, "KK2b")
        RKb = cx.sbuf([128, 128], BF16, "RKb")
        Gam = cx.sbuf([128, 4], F32, "Gam")
        Btok = cx.sbuf([128, 512], BF16, "Btok")
        Ktok = cx.sbuf([128, 512], BF16, "Ktok")
        Vtok2 = cx.sbuf([128, 512], BF16, "Vtok2")
        Vpad = cx.sbuf([128, 8, 128], BF16, "Vpad")
        Xpad = cx.sbuf([128, 8, 128], BF16, "Xpad")
        Upad = cx.sbuf([128, 8, 128], BF16, "Upad")
        Upair = cx.sbuf([128, 4, 128], BF16, "Upair")
        LT = [cx.sbuf([128, 512], BF16, "LT%d" % i) for i in range(8)]
        AA0 = [cx.sbuf([128, 128], BF16, "AA0_%d" % i) for i in range(8)]
        AAM = [[cx.sbuf([128, 384], BF16, "AAMx%d_%d" % (p_, i)) for i in range(8)] for p_ in range(2)]
        XTb = cx.sbuf([128, 128], BF16, "XTb")
        prevS = cx.sbuf([128, 128], F32, "prevS")
        GamS = cx.sbuf([128, 4, 16], F32, "GamS")
        stT = cx.sbuf([128, 33, 16], F32, "stT")
        Hf = cx.sbuf([128, 8, 128], F32, "Hf")
        Hb = cx.sbuf([128, 8, 128], BF16, "Hb")
        bdm = bonesf

        cx.memset("dve", Sret[:, :, :], 0.0)
        cx.memset("dve", Sretb[:, :, :], 0.0)
        cx.memset("pool", oT[:, :, :], 0.0)
        cx.memset("dve", carry[:, :], 0.0)
        cx.memset("pool", Vpad[:, :, :], 0.0)
        cx.memset("pool", Xpad[:, :, :], 0.0)
        cx.memset("pool", Upad[:, :, :], 0.0)
        cx.memset("dve", Hf[:, :, :], 0.0)
        cx.memset("dve", Hb[:, :, :], 0.0)
        SETS = [(AT, BT, KT, RT, vb, bonus, Gam, Btok, Ktok, Vtok2, Vpad)]
        if PIPE:
            SETS.append((
                rview(0, 256, BF16, "AT1", "p (b t) -> p b t", b=4),
                rview(256, 256, BF16, "BT1", "p (b t) -> p b t", b=4),
                rview(512, 256, BF16, "KT1", "p (b t) -> p b t", b=4),
                rview(768, 256, BF16, "RT1", "p (b t) -> p b t", b=4),
                rview(1024, 256, BF16, "vb1", "p (b t) -> p b t", b=4),
                rview(1280, 512, F32, "bonus1"),
                rview(1792, 4, F32, "Gam1"),
                rview(1800, 256, BF16, "Btok1"),
                rview(2056, 256, BF16, "Ktok1"),
                rview(2312, 256, BF16, "Vtok21"),
                rview(2568, 512, BF16, "Vpad1", "p (h c) -> p h c", h=8),
            ))
            cx.memset("pool", SETS[1][10][:, :, :], 0.0)
        else:
            cx.memset("pool", Nat[:, :, :], 0.0)
        for grp in range(9):
            nb = 4 if grp < 8 else 1
            cx.dma("sp", Of[0:16, 0:nb * 128], sshift_d[:, grp * 512:grp * 512 + nb * 128])
            for k in range(nb):
                cx.tr(ps[0][:, k * 16:(k + 1) * 16], Of[0:16, k * 128:(k + 1) * 128], idf[0:16, 0:16])
            cx.cp("dve", V(stT, stT.t[:, grp * 4:grp * 4 + nb, :].rearrange("p b s -> p (b s)")), ps[0][:, 0:nb * 16])

        w_view = w_in.rearrange("(c p) n -> p c n", p=128)
        wo_view = w_out.rearrange("(c p) n -> p c n", p=128)
        dmask_v = dmask_d.rearrange("p (a h t) -> p a h t", a=2, h=8)
        decq_v = decq_d.rearrange("p (a h t) -> p a h t", a=2, h=8)
        state = {"w": 0, "ev": 0, "ps": 0}

        wgran = {}

        def gran_id(which, c0):
            if which == "out":
                return 100 + c0 // 256
            if c0 < 4096:
                return c0 // 256
            if c0 == 4096:
                return 16
            return 17 + (c0 - NA) // 256

        def convert(which, c0, ncol, extra=()):
            gid = gran_id(which, c0)
            pb = Buf(None, "wg%d" % gid)
            if which == "out":
                dst = wobf[c0 // 256]
                src = wo_view[:, :, c0:c0 + ncol]
            else:
                dst = wbf[gid]
                src = w_view[:, :, c0:c0 + ncol]
            cx.dma("pool", V(pb, dst[:, :, 0:ncol]), src, extra_waits=list(extra))
            wgran[gid] = (pb, dst)

        conv_order = []
        for hg in range(2):
            for typ in range(4):
                for q in range(2):
                    conv_order.append(("in", typ * 1024 + hg * 512 + q * 256, 256))
            if hg == 0:
                conv_order.append(("in", 4096, 128))
            for off in (0, 1024, 3072, 2048):
                for q in range(2):
                    conv_order.append(("in", NA + hg * 512 + off + q * 256, 256))
        for g8i in range(8):
            conv_order.append(("out", g8i * 256, 256))
        conv_idx = {gran_id(w_, c_): i for i, (w_, c_, n_) in enumerate(conv_order)}
        conv_state = {"done": 0}

        def ensure_converted(upto):
            upto = min(upto, len(conv_order) - 1)
            while conv_state["done"] <= upto:
                w_, c_, n_ = conv_order[conv_state["done"]]
                extra = []
                if conv_state["done"] >= 4 and cx.cnt["pe"] > 0:
                    extra = [(cx.sem["pe"], cx.cnt["pe"])]
                convert(w_, c_, n_, extra)
                conv_state["done"] += 1

        ensure_converted(3)

        def load_w(which, c0, ncol):
            slot = wsl[state["w"] % 2]
            state["w"] += 1
            ensure_converted(conv_idx[gran_id(which, c0)] + 3)
            pb, src = wgran[gran_id(which, c0)]
            cx.dma("sp", slot[:, :, 0:ncol], V(pb, src[:, :, 0:ncol]))
            return slot

        def evac_eng():
            state["ev"] += 1
            return "act" if state["ev"] % 2 == 0 else "dve"

        def next_ps():
            state["ps"] += 1
            return ps[state["ps"] % 2]

        def proj_blocks(c0, nblk, evac_fn):
            done = 0
            while done < nblk:
                nb = min(2, nblk - done)
                slot = load_w("in", c0 + done * 128, nb * 128)
                for cb in range(nb):
                    p = next_ps()
                    for c in range(16):
                        cx.mm(p[:, 0:STN], slot[:, c, cb * 128:(cb + 1) * 128], xT[:, c, :],
                              start=(c == 0), stop=(c == 15))
                    evac_fn(done + cb, p)
                done += nb

        def mix_evac(st, gblk, dblk, p):
            mixc = pv[:, PV_MIX + gblk:PV_MIX + gblk + 1]
            d = zv[:, dblk, :]
            cx.act(zc1[:, :], p[:, 0:STN], AF.Identity, scale=omm[:, gblk:gblk + 1])
            npr = STN if st < NST - 1 else 256
            cx.stt(V(zA, d[:, 1:npr]), p[:, 0:npr - 1], mixc, zc1[:, 1:npr], ALU.mult, ALU.add)
            cx.stt(V(zA, d[:, 0:1]), carry[:, gblk:gblk + 1], mixc, zc1[:, 0:1], ALU.mult, ALU.add)
            if st == NST - 1 and SL >= 1:
                d3 = d[:, 256:384].rearrange("p (s t) -> p s t", t=8)
                p3 = p.t[:, 256:384].rearrange("p (s t) -> p s t", t=8)
                z3 = zc1.t[:, 256:384].rearrange("p (s t) -> p s t", t=8)
                pS3 = prevS.t[:, :].rearrange("p (s t) -> p s t", t=8)
                cx.cp("act", V(prevS, pS3[:, :, 1:8]), V(p, p3[:, :, 0:7]))
                cx.cp("dve", V(prevS, pS3[:, :, 0]), stT[:, gblk, :])
                cx.stt(V(zA, d[:, 256:384]), prevS[:, :], mixc, zc1[:, 256:384], ALU.mult, ALU.add)
            cx.cp("act", carry[:, gblk:gblk + 1], p[:, STN - 1:STN])
            if st == NST - 1:
                cx.cp("act", rawp[:, gblk:gblk + 1], p[:, 255:256])
                cx.cp("act", raws[:, gblk, :], p[:, 263:384:8])

        def load_H(sh, gb, want_b):
            for e in range(2):
                cx.dma("sp", Nat[e * 64:(e + 1) * 64, :, e * 64:(e + 1) * 64],
                       srwkv_d[sh * 8:(sh + 1) * 8, 2 * gb + e].rearrange("s v k -> v s k"))
            for q in range(2):
                pT = ps[4 + q]
                for i4 in range(4):
                    cx.tr(pT[:, i4 * 128:(i4 + 1) * 128], Nat[:, q * 4 + i4, :], idf[:, :])
                if want_b:
                    o0 = sh * 8 + q * 4
                    cx.cp("act" if q == 0 else "dve",
                          V(SSb, SSb.t[:, o0:o0 + 4, :].rearrange("p s e -> p (s e)")), pT[:, :])
                else:
                    cx.cp("act" if q == 0 else "dve",
                          V(SS, SS.t[:, q * 4:(q + 1) * 4, :].rearrange("p s e -> p (s e)")), pT[:, :])

        def chain_sample(hg, b, cur):
            gb = hg * 4 + b
            he, ho = 2 * b, 2 * b + 1
            bs = slice(b * 128, (b + 1) * 128)
            pXT = ps[2]
            load_H(0, gb, True)
            load_H(1, gb, True)
            cx.mm(pXT[:, 0:128], Vpad[:, he, :], LT[he][:, 256:384], start=True, stop=False)
            cx.mm(pXT[:, 0:128], Vpad[:, ho, :], LT[ho][:, 256:384], start=False, stop=False)
            for i in range(16):
                cx.mm(pXT[:, 8 * i:8 * i + 8], SSb[:, i, :], AT[:, b, 8 * i:8 * i + 8],
                      start=False, stop=(i == 15))
            cx.cp("act", XTb[:, :], pXT[:, 0:128])
            if SL < 4:
                return
            cx.tr(ps_tb[:, 0:128], XTb[:, :], idb[:, :])
            cx.cp("act", Xpad[:, he, 0:64], ps_tb[:, 0:64])
            cx.cp("dve", Xpad[:, ho, 64:128], ps_tb[:, 64:128])
            pU = ps[3]
            cx.mm(pU[:, 0:128], idb[:, :], Xpad[:, he, :], start=True, stop=False)
            cx.mm(pU[:, 0:128], idb[:, :], Xpad[:, ho, :], start=False, stop=False)
            cx.mm(pU[:, 0:128], cur[he][2], Xpad[:, he, :], start=False, stop=False)
            cx.mm(pU[:, 0:128], cur[ho][2], Xpad[:, ho, :], start=False, stop=True)
            cx.cp("act", Upad[:, he, 0:64], pU[:, 0:64])
            cx.cp("dve", Upad[:, ho, 64:128], pU[:, 64:128])
            cx.cp("act", Upair[:, b, :], pU[:, 0:128])
            pO = ps[6][:, bs]
            cx.mm(pO, Upad[:, he, :], LT[he][:, 128:256], start=True, stop=False)
            cx.mm(pO, Upad[:, ho, :], LT[ho][:, 128:256], start=False, stop=False)
            cx.mm(pO, Vpad[:, he, :], LT[he][:, 384:512], start=False, stop=False)
            cx.mm(pO, Vpad[:, ho, :], LT[ho][:, 384:512], start=False, stop=False)
            for i in range(16):
                cx.mm(ps[6][:, b * 128 + 8 * i:b * 128 + 8 * i + 8], SSb[:, i, :], RT[:, b, 8 * i:8 * i + 8],
                      start=False, stop=(i == 15))
            for sh in range(2):
                load_H(sh, gb, False)
                bm = pv.t[:, PV_BM + sh * 8:PV_BM + sh * 8 + 8]
                bmb = V(pv, bm.unsqueeze(2).to_broadcast([128, 8, 128]))
                cx.tt("dve", vexp[:, :, :],
                      V(Upair, Upair.t[:, b, :].unsqueeze(1).to_broadcast([128, 8, 128])), bmb, ALU.mult)
                cx.tt("dve", Vexp2[:, :, :],
                      V(Vtok2, Vtok2.t[:, bs].unsqueeze(1).to_broadcast([128, 8, 128])), bmb, ALU.mult)
                for q in range(2):
                    pp = ps[2 + q]
                    qs = slice(q * 4, (q + 1) * 4)
                    cx.mm(pp[:, :], Btok[:, bs], V(vexp, vexp.t[:, qs, :].rearrange("p s e -> p (s e)")),
                          start=True, stop=False)
                    cx.mm(pp[:, :], Ktok[:, bs], V(Vexp2, Vexp2.t[:, qs, :].rearrange("p s e -> p (s e)")),
                          start=False, stop=True)
                    sn = V(SSn, SSn.t[:, qs, :])
                    cx.tt("dve", sn, V(pp, pp.t[:, :].rearrange("p (s e) -> p s e", s=4)),
                          V(bonesf, bonesf.t[:, :].unsqueeze(1).to_broadcast([128, 4, 128])), ALU.mult)
                    cx.tt("dve", sn, sn, V(SS, SS.t[:, qs, :]), ALU.add)
                    gsl = GamS.t[:, b, sh * 8 + q * 4:sh * 8 + q * 4 + 4]
                    cx.tt("dve", sn, sn, V(GamS, gsl.unsqueeze(2).to_broadcast([128, 4, 128])), ALU.mult)
                for q in range(2):
                    pT = ps[4 + q]
                    for i4 in range(4):
                        cx.tr(pT[:, i4 * 128:(i4 + 1) * 128], SSn[:, q * 4 + i4, :], idf[:, :])
                    cx.cp("act" if q == 0 else "dve",
                          V(SS, SS.t[:, q * 4:(q + 1) * 4, :].rearrange("p s e -> p (s e)")), pT[:, :])
                for e in range(2):
                    cx.dma("sp", rwkv_s[sh * 8:(sh + 1) * 8, 2 * gb + e].rearrange("s v k -> v s k"),
                           SS[e * 64:(e + 1) * 64, :, e * 64:(e + 1) * 64])

        def rw_prep(st, hg, j, s):
            t = st * TPS + j
            samp = 1 if t == NT - 1 else 0
            if RW_LEVEL < 1 or (samp and SL < 2):
                return
            cs = slice(j * 128, (j + 1) * 128)
            AT, BT, KT, RT, vb, bonus, Gam, Btok, Ktok, Vtok2, Vpad = SETS[s]
            for b in range(4):
                gb = hg * 4 + b
                pc = lambda off: pv[:, off + gb:off + gb + 1]
                cx.mm(ps[0][:, 0:128], lorab[0:64, gb * 128:(gb + 1) * 128], twzx[0:64, cs])
                cx.mm(ps[1][:, 256:384], lorab[64:128, gb * 128:(gb + 1) * 128], twzx[64:128, cs])
                cx.act(Of[:, b * 128:(b + 1) * 128], ps[0][:, 0:128], AF.Sigmoid, bias=pc(PV_W0))
                cx.act(Osq[:, b * 128:(b + 1) * 128], ps[1][:, 256:384], AF.Sigmoid, bias=pc(PV_A0))
            yield
            for b in range(4):
                gb = hg * 4 + b
                r_ = V(zA, zv[:, b, cs])
                k_ = V(zA, zv[:, 4 + b, cs])
                v_ = V(zA, zv[:, 8 + b, cs])
                pc = lambda off: pv[:, off + gb:off + gb + 1]
                S1 = V(Of, Of.t[:, b * 128:(b + 1) * 128])
                A1 = V(Osq, Osq.t[:, b * 128:(b + 1) * 128])
                cx.op("dve", lambda eng, o=CN.t[:, :], d0=scanm.t[:, samp, :], d1=S1.ap:
                      eng.tensor_tensor_scan(o, d0, d1, 0.0, ALU.mult, ALU.add),
                      [scanm[:, samp, :], S1], [CN[:, :]])
                cx.act(E1[:, :], CN[:, :], AF.Exp, scale=-CDEC)
                cx.act(E2[:, :], CN[:, :], AF.Exp, scale=CDEC)
                cx.tt("pool", RN[:, :], CN[:, :], S1, ALU.subtract)
                cx.act(E3[:, :], RN[:, :], AF.Exp, scale=-CDEC)
                if samp:
                    cx.cp("act", GamS[:, b, :], E1[:, 7:128:8])
                else:
                    cx.cp("act", Gam[:, b:b + 1], E1[:, 127:128])
                cx.ts("dve", KK[:, :], k_, pc(PV_KK), None, ALU.mult)
                cx.tt("pool", KK2b[:, :], KK[:, :], KK[:, :], ALU.mult)
                cx.mm(ps[1][:, 0:128], bonesb[:, :], KK2b[:, :])
                cx.act(RN[:, :], ps[1][:, 0:128], AF.Ln, bias=epsb[:, 3:4])
                cx.act(RN[:, :], RN[:, :], AF.Exp, scale=-0.5)
                cx.tt("dve", KK[:, :], KK[:, :], RN[:, :], ALU.mult)
                cx.ts("dve", T1[:, :], A1, 1.0, pc(PV_KA), ALU.subtract, ALU.mult)
                cx.stt(KP[:, :], T1[:, :], 1.0, k_, ALU.add, ALU.mult)
                cx.stt(AT[:, b, :], KK[:, :], -1.0, E3[:, :], ALU.mult, ALU.mult)
                cx.tt("pool", T1[:, :], KK[:, :], A1, ALU.mult)
                cx.tt("pool", BT[:, b, :], T1[:, :], E2[:, :], ALU.mult)
                cx.tt("pool", KT[:, b, :], KP[:, :], E2[:, :], ALU.mult)
                cx.tt("dve", RT[:, b, :], r_, E1[:, :], ALU.mult)
                cx.stt(RKb[:, :], r_, pc(PV_RK), KP[:, :], ALU.mult, ALU.mult)
                cx.mm(ps[1][:, 128:256], bonesb[:, :], RKb[:, :])
                cx.cp("act", vb[:, b, :], v_)
                cx.tt("dve", bonus[:, b * 128:(b + 1) * 128], ps[1][:, 128:256], v_, ALU.mult)
                yield
            yield
            for (src, dst) in ((BT, Btok), (KT, Ktok), (vb, Vtok2)):
                for b in range(4):
                    cx.tr(ps_tb[:, b * 128:(b + 1) * 128], src[:, b, :], idb[:, :])
                cx.cp("act", dst[:, :], ps_tb[:, 0:512])
            vt4 = Vtok2.t[:, :].rearrange("p (b c) -> p b c", b=4)
            vp4 = Vpad.t[:, :, :].rearrange("p (b e) c -> p b e c", e=2)
            cx.cp("dve", V(Vpad, vp4[:, :, 0, 0:64]), V(Vtok2, vt4[:, :, 0:64]))
            cx.cp("dve", V(Vpad, vp4[:, :, 1, 64:128]), V(Vtok2, vt4[:, :, 64:128]))
            yield

        def rw_core(st, hg, j, s):
            t = st * TPS + j
            samp = 1 if t == NT - 1 else 0
            if RW_LEVEL < 2 or (samp and SL < 2):
                return
            cs = slice(j * 128, (j + 1) * 128)
            AT, BT, KT, RT, vb, bonus, Gam, Btok, Ktok, Vtok2, Vpad = SETS[s]
            for hl in range(8):
                b = hl // 2
                pr = slice((hl % 2) * 64, (hl % 2) * 64 + 64)
                pL = ps[4 + (hl % 2)]
                cx.mm(pL[:, 0:128], BT[pr, b, :], AT[pr, b, :])
                cx.mm(pL[:, 128:256], BT[pr, b, :], RT[pr, b, :])
                cx.mm(pL[:, 256:384], KT[pr, b, :], AT[pr, b, :])
                cx.mm(pL[:, 384:512], KT[pr, b, :], RT[pr, b, :])
                cx.tt("dve", LT[hl][:, :], pL[:, :], maskT4[:, samp, :], ALU.mult)
                pN = ps[2 + (hl % 2)]
                cx.mm(pN[:, 0:128], AT[pr, b, :], BT[pr, b, :])
                cx.tt("dve", AA0[hl][:, :], pN[:, 0:128], maskN[:, samp, :], ALU.mult)
                yield
            if RW_LEVEL < 3:
                return
            cur = [(AA0[hl][:, :], LT[hl][:, 0:128], LT[hl][:, 0:128]) for hl in range(8)]
            nl = 2 if samp else 6
            for rnd in range(1, nl + 2):
                for hl in range(8):
                    Ac, ATc, Mc = cur[hl]
                    pA = ps[2 + ((hl + 3 * rnd) % 5)]
                    pM = pA
                    do_sq = rnd <= nl
                    do_m = rnd >= 2
                    if do_sq:
                        cx.mm(pA[:, 0:128], ATc, Ac)
                        cx.mm(pA[:, 128:256], Ac, ATc)
                    if do_m:
                        cx.mm(pM[:, 256:384], idb[:, :], ATc, start=True, stop=False)
                        cx.mm(pM[:, 256:384], Ac, Mc, start=False, stop=True)
                    nA, nAT, nM = Ac, ATc, Mc
                    if do_sq:
                        AAn = AAM[rnd % 2][hl]
                        cx.cp("act", AAn[:, 0:256], pA[:, 0:256])
                        nA, nAT = AAn[:, 0:128], AAn[:, 128:256]
                    if do_m:
                        Mn = AAM[rnd % 2][hl]
                        cx.tt("dve", Mn[:, 256:384], pM[:, 256:384], Mc, ALU.add)
                        nM = Mn[:, 256:384]
                    cur[hl] = (nA, nAT, nM)
                    yield
            if RW_LEVEL < 4:
                return
            if samp:
                for b in range(4):
                    chain_sample(hg, b, cur)
            else:
                g4 = slice(hg * 4, hg * 4 + 4)
                pXa, pUa, pHa = ps[2], ps[3], ps[4]
                x4 = Xpad.t[:, :, :].rearrange("p (b e) c -> p b e c", e=2)
                u4 = Upad.t[:, :, :].rearrange("p (b e) c -> p b e c", e=2)
                for b in range(4):
                    gb = hg * 4 + b
                    he, ho = 2 * b, 2 * b + 1
                    o = pXa[:, b * 128:(b + 1) * 128]
                    cx.mm(o, AT[:, b, :], Hb[:, gb, :], start=True, stop=False)
                    cx.mm(o, LT[he][:, 256:384], Vpad[:, he, :], start=False, stop=False)
                    cx.mm(o, LT[ho][:, 256:384], Vpad[:, ho, :], start=False, stop=True)
                yield
                px4 = pXa.t[:, :].rearrange("p (b c) -> p b c", b=4)
                cx.cp("act", V(Xpad, x4[:, :, 0, 0:64]), V(pXa, px4[:, :, 0:64]))
                cx.cp("act", V(Xpad, x4[:, :, 1, 64:128]), V(pXa, px4[:, :, 64:128]))
                for b in range(4):
                    he, ho = 2 * b, 2 * b + 1
                    o = pUa[:, b * 128:(b + 1) * 128]
                    cx.mm(o, idb[:, :], Xpad[:, he, :], start=True, stop=False)
                    cx.mm(o, idb[:, :], Xpad[:, ho, :], start=False, stop=False)
                    cx.mm(o, cur[he][2], Xpad[:, he, :], start=False, stop=False)
                    cx.mm(o, cur[ho][2], Xpad[:, ho, :], start=False, stop=True)
                yield
                pu4 = pUa.t[:, :].rearrange("p (b c) -> p b c", b=4)
                cx.cp("dve", V(Upad, u4[:, :, 0, 0:64]), V(pUa, pu4[:, :, 0:64]))
                cx.cp("dve", V(Upad, u4[:, :, 1, 64:128]), V(pUa, pu4[:, :, 64:128]))
                cx.cp("dve", V(Upair, Upair.t[:, :, :].rearrange("p b c -> p (b c)")), pUa[:, :])
                for b in range(4):
                    gb = hg * 4 + b
                    he, ho = 2 * b, 2 * b + 1
                    pO = ps[6][:, b * 128:(b + 1) * 128]
                    cx.mm(pO, Hb[:, gb, :], RT[:, b, :], start=True, stop=False)
                    cx.mm(pO, Upad[:, he, :], LT[he][:, 128:256], start=False, stop=False)
                    cx.mm(pO, Upad[:, ho, :], LT[ho][:, 128:256], start=False, stop=False)
                    cx.mm(pO, Vpad[:, he, :], LT[he][:, 384:512], start=False, stop=False)
                    cx.mm(pO, Vpad[:, ho, :], LT[ho][:, 384:512], start=False, stop=True)
                    pH = pHa[:, b * 128:(b + 1) * 128]
                    cx.mm(pH, Btok[:, b * 128:(b + 1) * 128], Upair[:, b, :], start=True, stop=False)
                    cx.mm(pH, Ktok[:, b * 128:(b + 1) * 128], Vtok2[:, b * 128:(b + 1) * 128],
                          start=False, stop=True)
                yield
                tq = V(gq, gq.t[:, :].rearrange("p (b c) -> p b c", b=4))
                cx.tt("dve", tq, V(pHa, pHa.t[:, :].rearrange("p (b c) -> p b c", b=4)),
                      V(bonesf, bonesf.t[:, :].unsqueeze(1).to_broadcast([128, 4, 128])), ALU.mult)
                cx.tt("dve", tq, tq, Hf[:, g4, :], ALU.add)
                cx.tt("dve", Hf[:, g4, :], tq,
                      V(Gam, Gam.t[:, 0:4].unsqueeze(2).to_broadcast([128, 4, 128])), ALU.mult)
                cx.cp("act", Hb[:, g4, :], Hf[:, g4, :])
            yield
            cx.cp("act", Of[:, :], ps[6][:, :])
            cx.act(Osq[:, :], ps[6][:, :], AF.Square)
            cx.mm(ps[5][:, :], bonesf[:, :], Of[:, :])
            cx.ts("dve", gm[:, :], ps[5][:, :], 1.0 / 64.0, None, ALU.mult)
            cx.mm(ps[5][:, :], bonesf[:, :], Osq[:, :])
            cx.tt("dve", gq[:, :], gm[:, :], gm[:, :], ALU.mult)
            cx.stt(gq[:, :], ps[5][:, :], 1.0 / 64.0, gq[:, :], ALU.mult, ALU.subtract)
            cx.act(gq[:, :], gq[:, :], AF.Ln, bias=epsb[:, 2:3])
            cx.act(gq[:, :], gq[:, :], AF.Exp, scale=-0.5)
            cx.tt("dve", Of[:, :], Of[:, :], gm[:, :], ALU.subtract)
            cx.tt("dve", Of[:, :], Of[:, :], gq[:, :], ALU.mult)
            for b in range(4):
                gb = hg * 4 + b
                cx.ts("dve", Of[:, b * 128:(b + 1) * 128], Of[:, b * 128:(b + 1) * 128],
                      pv[:, PV_GAG + gb:PV_GAG + gb + 1], pv[:, PV_GAB + gb:PV_GAB + gb + 1],
                      ALU.mult, ALU.add)
            cx.tt("dve", Of[:, :], Of[:, :], bonus[:, :], ALU.add)
            cx.act(V(gs, gs.t[:, :].rearrange("p (h t) -> p h t", h=4)),
                   V(zA, zv[:, 12:16, cs]), AF.Sigmoid)
            cx.tt("dve", V(gs, gs.t[:, :].rearrange("p (h t) -> p h t", h=4)),
                  V(gs, gs.t[:, :].rearrange("p (h t) -> p h t", h=4)), V(zA, zv[:, 12:16, cs]), ALU.mult)
            cx.tt("dve", oT[:, hg * 4:hg * 4 + 4, cs],
                  V(Of, Of.t[:, :].rearrange("p (h t) -> p h t", h=4)),
                  V(gs, gs.t[:, :].rearrange("p (h t) -> p h t", h=4)), ALU.mult)

            yield

        def run_gens(main, side=None, ratio=6):
            k = 0
            while main is not None or side is not None:
                if main is not None:
                    try:
                        next(main)
                    except StopIteration:
                        main = None
                k += 1
                if side is not None and (main is None or k % ratio == 0):
                    try:
                        next(side)
                    except StopIteration:
                        side = None


        for st in range(NST):
            if st == NST - 1 and PIPE:
                cx.barrier()
                cx.memset("dve", Nat[:, :, :], 0.0)
            for j in range(TPS):
                t = st * TPS + j
                cx.dma("sp", xf[:, :], xall[t * 128:(t + 1) * 128, :])
                cx.cp("act", xb[:, :], xf[:, :])
                for half in range(2):
                    for c in range(8):
                        cc = half * 8 + c
                        cx.tr(ps_tb[:, c * 128:(c + 1) * 128], xb[:, cc * 128:(cc + 1) * 128], idb[:, :])
                    cx.cp("dve" if half == 0 else "act",
                          xT[:, half * 8:(half + 1) * 8, j * 128:(j + 1) * 128],
                          V(ps_tb, ps_tb.t[:, :].rearrange("p (c t) -> p c t", c=8)))
            cx.dma("sp", cosb[:, :], cos_d[:, st * STN:(st + 1) * STN])
            cx.dma("sp", sinb[:, :], sin_d[:, st * STN:(st + 1) * STN])

            for hg in range(2):
                for typ in range(4):
                    proj_blocks(typ * 1024 + hg * 512, 4,
                                lambda i, p, typ=typ: mix_evac(st, typ * 8 + hg * 4 + i, typ * 4 + i, p))
                if hg == 0:
                    proj_blocks(4096, 1, lambda i, p: mix_evac(st, 32, 16, p))
                    cx.act(twzx[0:64, :], V(zA, zv[0:64, 16, :]), AF.Tanh)
                    cx.cp("dve", twzx[64:128, :], V(zA, zv[64:128, 16, :]))
                if st < NST - 1 and PIPE:
                    run_gens(rw_prep(st, hg, 0, 0))
                    run_gens(rw_core(st, hg, 0, 0), rw_prep(st, hg, 1, 1))
                    run_gens(rw_core(st, hg, 1, 1), rw_prep(st, hg, 2, 0))
                    run_gens(rw_core(st, hg, 2, 0))
                else:
                    for j in range(TPS):
                        run_gens(rw_prep(st, hg, j, 0))
                        run_gens(rw_core(st, hg, j, 0))

                hs = slice(hg * 4, hg * 4 + 4)
                cx.dma("sp", dmask[:, :, :, :], dmask_v[:, :, hs, :])
                cx.dma("sp", decq[:, :, :, :], decq_v[:, :, hs, :])
                base = NA + hg * 512
                proj_blocks(base, 4, lambda i, p: cx.cp(evac_eng(), V(zA, zv[:, i, :]), p[:, 0:STN]))
                proj_blocks(base + 1024, 4, lambda i, p: cx.act(V(zA, zv[:, 4 + i, :]), p[:, 0:STN], AF.Copy,
                                                                scale=float(128.0 ** -0.5)))
                proj_blocks(base + 3072, 4, lambda i, p: cx.cp(evac_eng(), V(zA, zv[:, 8 + i, :]), p[:, 0:STN]))
                for half in range(2):
                    slot = load_w("in", base + 2048 + half * 256, 256)
                    for j in range(TPS):
                        p = next_ps()
                        for c in range(16):
                            cx.mm(p[:, 0:256], xT[:, c, j * 128:(j + 1) * 128], slot[:, c, :],
                                  start=(c == 0), stop=(c == 15))
                        cx.cp(evac_eng(), vtB[:, j, half * 256:(half + 1) * 256], p[:, 0:256])
                for hh in range(4):
                    for (sb, dst) in ((0, qr), (4, kr)):
                        src = V(zA, zv[:, sb + hh, :])
                        cx.cp("act", zb16[:, :], src)
                        cx.mm(ps[2][:, 0:STN], protb[:, :], zb16[:, :])
                        cx.tt("dve", t1[:, :], src, cosb[:, :], ALU.mult)
                        cx.tt("dve", t2[:, :], ps[2][:, 0:STN], sinb[:, :], ALU.mult)
                        cx.tt("dve", dst[:, hh, :], t1[:, :], t2[:, :], ALU.add)
                for j in range(TPS):
                    t = st * TPS + j
                    samp = 1 if t == NT - 1 else 0
                    cs = slice(j * 128, (j + 1) * 128)
                    for hh in range(4):
                        cx.tr(ps_tb[:, hh * 128:(hh + 1) * 128], kr[:, hh, cs], idb[:, :])
                    cx.cp("act", ktok[:, :], ps_tb[:, 0:512])
                    for hh in range(4):
                        cx.mm(ps[3][:, hh * 128:(hh + 1) * 128], kr[:, hh, cs], qr[:, hh, cs])
                    cx.tt("dve", scm[:, :], ps[3][:, :],
                          V(dmask, dmask.t[:, samp, :, :].rearrange("p h t -> p (h t)")), ALU.mult)
                    cx.tt("dve", qdec[:, :, :], qr[:, :, cs], decq[:, samp, :, :], ALU.mult)
                    dk = pv.t[:, PV_DECK + samp * 8 + hg * 4:PV_DECK + samp * 8 + hg * 4 + 4]
                    cx.tt("dve", vdec[:, :, :],
                          V(vtB, vtB.t[:, j, :].rearrange("p (h e) -> p h e", h=4)),
                          V(pv, dk.unsqueeze(2).to_broadcast([128, 4, 128])), ALU.mult)
                    for hh in range(4):
                        h = hg * 4 + hh
                        o = ps[4][:, hh * 128:(hh + 1) * 128]
                        if not samp:
                            cx.mm(o, vtB[:, j, hh * 128:(hh + 1) * 128], scm[:, hh * 128:(hh + 1) * 128],
                                  start=True, stop=False)
                            cx.mm(o, Sretb[:, h, :], qdec[:, hh, :], start=False, stop=True)
                            cx.mm(ps[5][:, hh * 128:(hh + 1) * 128], ktok[:, hh * 128:(hh + 1) * 128],
                                  vdec[:, hh, :])
                        else:
                            cx.mm(o, vtB[:, j, hh * 128:(hh + 1) * 128], scm[:, hh * 128:(hh + 1) * 128],
                                  start=True, stop=False)
                            g8 = float(np.exp(np.float32(8.0 * LOG_G[h])))
                            for sh in range(2):
                                cx.dma("sp", SS[:, :, :],
                                       sret_d[sh * 8:(sh + 1) * 8, h].rearrange("s d e -> d s e"))
                                cx.cp("act", SSb[:, 0:8, :], SS[:, :, :])
                                for i8 in range(8):
                                    i = sh * 8 + i8
                                    cx.mm(ps[4][:, hh * 128 + 8 * i:hh * 128 + 8 * i + 8], SSb[:, i8, :],
                                          qdec[:, hh, 8 * i:8 * i + 8], start=False, stop=(i == 15))
                                bm = pv.t[:, PV_BM + sh * 8:PV_BM + sh * 8 + 8]
                                cx.tt("dve", vexp[:, :, :],
                                      V(vdec, vdec.t[:, hh, :].unsqueeze(1).to_broadcast([128, 8, 128])),
                                      V(pv, bm.unsqueeze(2).to_broadcast([128, 8, 128])), ALU.mult)
                                for q4 in range(2):
                                    pp = ps[5 + (q4 % 2)]
                                    cx.mm(pp[:, :], ktok[:, hh * 128:(hh + 1) * 128],
                                          V(vexp, vexp.t[:, q4 * 4:(q4 + 1) * 4, :].rearrange("p s e -> p (s e)")))
                                    cx.stt(V(SSn, SSn.t[:, q4 * 4:(q4 + 1) * 4, :].rearrange("p s e -> p (s e)")),
                                           V(SS, SS.t[:, q4 * 4:(q4 + 1) * 4, :].rearrange("p s e -> p (s e)")),
                                           g8, pp[:, :], ALU.mult, ALU.add)
                                cx.dma("sp", ret_s[sh * 8:(sh + 1) * 8, h].rearrange("s d e -> d s e"), SSn[:, :, :])
                    if not samp:
                        for hh in range(4):
                            h = hg * 4 + hh
                            gC = float(np.exp(np.float32(128.0 * LOG_G[h])))
                            cx.stt(Sret[:, h, :], Sret[:, h, :], gC, ps[5][:, hh * 128:(hh + 1) * 128],
                                   ALU.mult, ALU.add)
                        cx.cp("act", Sretb[:, hs, :], Sret[:, hs, :])
                    cx.cp("act", Of[:, :], ps[4][:, :])
                    cx.act(Osq[:, :], ps[4][:, :], AF.Square)
                    cx.mm(ps[6][:, :], onesf[:, :], Of[:, :])
                    cx.ts("dve", gm[:, :], ps[6][:, :], 1.0 / 128.0, None, ALU.mult)
                    cx.mm(ps[6][:, :], onesf[:, :], Osq[:, :])
                    cx.tt("dve", gq[:, :], gm[:, :], gm[:, :], ALU.mult)
                    cx.stt(gq[:, :], ps[6][:, :], 1.0 / 128.0, gq[:, :], ALU.mult, ALU.subtract)
                    cx.act(gq[:, :], gq[:, :], AF.Ln, bias=epsb[:, 0:1])
                    cx.act(gq[:, :], gq[:, :], AF.Exp, scale=-0.5)
                    cx.tt("dve", Of[:, :], Of[:, :], gm[:, :], ALU.subtract)
                    cx.tt("dve", Of[:, :], Of[:, :], gq[:, :], ALU.mult)
                    for hh in range(4):
                        h = hg * 4 + hh
                        cx.ts("dve", Of[:, hh * 128:(hh + 1) * 128], Of[:, hh * 128:(hh + 1) * 128],
                              pv[:, PV_GBG + h:PV_GBG + h + 1], pv[:, PV_GBB + h:PV_GBB + h + 1],
                              ALU.mult, ALU.add)
                    cx.act(V(gs, gs.t[:, :].rearrange("p (h t) -> p h t", h=4)),
                           V(zA, zv[:, 8:12, cs]), AF.Sigmoid)
                    cx.tt("dve", V(gs, gs.t[:, :].rearrange("p (h t) -> p h t", h=4)),
                          V(gs, gs.t[:, :].rearrange("p (h t) -> p h t", h=4)), V(zA, zv[:, 8:12, cs]), ALU.mult)
                    cx.tt("dve", oT[:, 8 + hg * 4:8 + hg * 4 + 4, cs],
                          V(Of, Of.t[:, :].rearrange("p (h t) -> p h t", h=4)),
                          V(gs, gs.t[:, :].rearrange("p (h t) -> p h t", h=4)), ALU.mult)

            for j in range(TPS):
                t = st * TPS + j
                cx.dma("sp", V(zA, hv[:, j, :]), xall[t * 128:(t + 1) * 128, :])
            cx.act(V(zA, zA.t[:, 0:3 * D]), V(zA, zA.t[:, 0:3 * D]), AF.Copy, scale=float(ALPHA))
            for g8i in range(8):
                slot = load_w("out", g8i * 256, 256)
                for j in range(TPS):
                    t = st * TPS + j
                    if t == 0:
                        continue
                    p = next_ps()
                    for c in range(16):
                        cx.mm(p[:, 0:256], oT[:, c, j * 128:(j + 1) * 128], slot[:, c, :],
                              start=(c == 0), stop=(c == 15))
                    hsl = V(zA, hv[:, j, g8i * 256:(g8i + 1) * 256])
                    cx.tt("dve", hsl, hsl, p[:, 0:256], ALU.add)
            tiles_ln = [j for j in range(TPS) if st * TPS + j != 0]
            for j in tiles_ln:
                hj = V(zA, hv[:, j, :])
                cx.op("dve", lambda eng, o=lnst.t[:, 0:1], i=hv[:, j, :]: eng.tensor_reduce(
                    o, i, mybir.AxisListType.X, ALU.add), [hj], [lnst[:, 0:1]])
                cx.act(xf[:, :], hj, AF.Square)
                cx.op("dve", lambda eng, o=lnst.t[:, 1:2], i=xf.t[:, :]: eng.tensor_reduce(
                    o, i, mybir.AxisListType.X, ALU.add), [xf[:, :]], [lnst[:, 1:2]])
                cx.ts("dve", lnst[:, 2:3], lnst[:, 0:1], 1.0 / D, None, ALU.mult)
                cx.tt("dve", lnst[:, 3:4], lnst[:, 2:3], lnst[:, 2:3], ALU.mult)
                cx.stt(lnst[:, 4:5], lnst[:, 1:2], 1.0 / D, lnst[:, 3:4], ALU.mult, ALU.subtract)
                cx.act(lnst[:, 5:6], lnst[:, 4:5], AF.Ln, bias=epsb[:, 1:2])
                cx.act(lnst[:, 5:6], lnst[:, 5:6], AF.Exp, scale=-0.5)
                cx.ts("dve", hj, hj, lnst[:, 2:3], lnst[:, 5:6], ALU.subtract, ALU.mult)
            cx.dma("sp", lnbuf[:, :], lng_d[0, :].partition_broadcast(128))
            for j in tiles_ln:
                hj = V(zA, hv[:, j, :])
                cx.tt("dve", hj, hj, lnbuf[:, :], ALU.mult)
            cx.dma("sp", lnbuf[:, :], lnb_d[0, :].partition_broadcast(128))
            for j in tiles_ln:
                t = st * TPS + j
                hj = V(zA, hv[:, j, :])
                cx.tt("dve", hj, hj, lnbuf[:, :], ALU.add)
                if t == NT - 1:
                    cx.dma("sp", ys_d, hj)
                else:
                    cx.dma("sp", yp_d[(t - 1) * 128:t * 128, :], hj)

        cx.dma("sp", ret_p.rearrange("h d e -> d h e"), Sret[:, :, :])
        for gb in range(8):
            pT = ps[gb % 2]
            cx.tr(pT[:, 0:128], Hf[:, gb, :], idf[:, :])
            cx.cp("act", Of[:, 0:128], pT[:, 0:128])
            for e in range(2):
                cx.dma("sp", rwkv_p[2 * gb + e], Of[e * 64:(e + 1) * 64, e * 64:(e + 1) * 64])
        for grp in range(9):
            nb = 4 if grp < 8 else 1
            pa, pb = ps[0], ps[1]
            for k in range(nb):
                blk = grp * 4 + k
                cx.tr(pa[0:16, k * 128:(k + 1) * 128], raws[:, blk, :], idf[:, :])
                cx.tr(pb[0:1, k * 128:(k + 1) * 128], rawp[:, blk:blk + 1], idf[:, :])
            cx.cp("dve", rowb[0:16, 0:nb * 128], pa[0:16, 0:nb * 128])
            cx.cp("act", rowp[0:1, 0:nb * 128], pb[0:1, 0:nb * 128])
            cx.dma("sp", shift_s[:, grp * 512:grp * 512 + nb * 128], rowb[0:16, 0:nb * 128])
            cx.dma("sp", shift_p[:, grp * 512:grp * 512 + nb * 128], rowp[0:1, 0:nb * 128])
        cx.finish()
    return nc


def dmask_or(decq, samp, hs):
    return V(decq, decq.t[:, samp, hs, :])


_NC_CACHE = {}


def _consts():
    f32 = np.float32
    c = {}
    c["ident_f"] = np.eye(128, dtype=f32)
    P = np.zeros((128, 128), f32)
    for m in range(64):
        P[m + 64, m] = -1.0
    for m in range(64, 128):
        P[m - 64, m] = 1.0
    c["prot_f"] = P
    half = 64
    inv = (np.float32(10000.0) ** (-(np.arange(half, dtype=f32) / np.float32(half)))).astype(f32)
    pos = np.zeros(NT * 128, f32)
    pos[112:128] = np.arange(16)
    pos[128:17 * 128] = 16 + np.arange(2048)
    pos[17 * 128:] = 16384 + (np.arange(128) % 8)
    ang = (pos[None, :] * inv[:, None]).astype(f32)
    cos = np.cos(ang).astype(f32)
    sin = np.sin(ang).astype(f32)
    c["cos_d"] = np.ascontiguousarray(np.concatenate([cos, cos], axis=0))
    c["sin_d"] = np.ascontiguousarray(np.concatenate([sin, sin], axis=0))
    lg = np.array(LOG_G, f32)
    j = np.arange(128)[:, None]
    i = np.arange(128)[None, :]
    dm = np.zeros((128, 2, 8, 128), f32)
    dq = np.zeros((128, 2, 8, 128), f32)
    deck = np.zeros((128, 16), f32)
    for h in range(8):
        diff = (i - j).astype(f32)
        dm[:, 0, h, :] = np.where(diff >= 0, np.exp(np.maximum(diff, 0) * lg[h]), 0.0)
        same = (i // 8) == (j // 8)
        dm[:, 1, h, :] = np.where((diff >= 0) & same, np.exp(np.maximum(diff, 0) * lg[h]), 0.0)
        dq[:, 0, h, :] = np.exp((np.arange(128, dtype=f32) + 1.0) * lg[h])[None, :]
        dq[:, 1, h, :] = np.exp(((np.arange(128) % 8).astype(f32) + 1.0) * lg[h])[None, :]
        deck[:, h] = np.exp((127.0 - np.arange(128, dtype=f32)) * lg[h])
        deck[:, 8 + h] = np.exp((7.0 - (np.arange(128) % 8).astype(f32)) * lg[h])
    c["dmask_d"] = np.ascontiguousarray(dm.reshape(128, -1))
    c["decq_d"] = np.ascontiguousarray(dq.reshape(128, -1))
    c["deck"] = deck
    bm = np.zeros((128, 16), f32)
    bm[np.arange(128), np.arange(128) // 8] = 1.0
    c["bm"] = bm
    bo = np.zeros((128, 128), f32)
    bo[0:64, 0:64] = 1.0
    bo[64:128, 64:128] = 1.0
    c["bones_d"] = bo
    s_ = np.arange(128)[:, None]
    t_ = np.arange(128)[None, :]
    same = (s_ // 8) == (t_ // 8)
    m4 = np.zeros((128, 2, 512), f32)
    for a, sm in ((0, np.ones_like(same)), (1, same)):
        strict = ((t_ > s_) & sm).astype(f32)
        incl = ((t_ >= s_) & sm).astype(f32)
        m4[:, a, 0:128] = strict
        m4[:, a, 128:256] = incl
        m4[:, a, 256:384] = strict
        m4[:, a, 384:512] = incl
    c["maskT4_d"] = np.ascontiguousarray(m4.reshape(128, 1024))
    mn = np.zeros((128, 2, 128), f32)
    mn[:, 0, :] = (t_.T > s_.T).astype(f32) if False else (np.arange(128)[None, :] < np.arange(128)[:, None]).astype(f32)
    mn[:, 1, :] = ((np.arange(128)[None, :] < np.arange(128)[:, None]) & same).astype(f32)
    c["maskN_d"] = np.ascontiguousarray(mn.reshape(128, 256))
    sc = np.ones((128, 2, 128), f32)
    sc[:, 1, :] = (np.arange(128) % 8 != 0).astype(f32)[None, :]
    c["scanm_d"] = np.ascontiguousarray(sc.reshape(128, 256))
    return c


def kernel(x_prompt, x_sample, state_rwkv, state_shift, state_ret, meta_tokens, w_in, w_out, shift_mix,
           w0, w_up, a0, a_up, k_k, k_a, r_k, gn_a_g, gn_a_b, gn_b_g, gn_b_b, ln_g, ln_b):
    f32 = np.float32
    A = lambda v: np.asarray(v, f32)
    x_prompt, x_sample, meta = A(x_prompt), A(x_sample), A(meta_tokens)
    if "nc" not in _NC_CACHE:
        _NC_CACHE["nc"] = build()
    nc = _NC_CACHE["nc"]
    c = _consts()
    pvh = np.zeros((128, NPV), f32)
    fm = lambda v, nb: A(v).reshape(nb, 128).T
    pvh[:, PV_MIX:PV_MIX + 33] = fm(shift_mix[0], 33)
    pvh[:, PV_W0:PV_W0 + 8] = fm(w0[0], 8)
    pvh[:, PV_A0:PV_A0 + 8] = fm(a0[0], 8)
    pvh[:, PV_KK:PV_KK + 8] = fm(k_k[0], 8)
    pvh[:, PV_KA:PV_KA + 8] = fm(k_a[0], 8)
    pvh[:, PV_RK:PV_RK + 8] = fm(A(r_k)[0].reshape(-1), 8)
    pvh[:, PV_GAG:PV_GAG + 8] = fm(gn_a_g[0], 8)
    pvh[:, PV_GAB:PV_GAB + 8] = fm(gn_a_b[0], 8)
    pvh[:, PV_GBG:PV_GBG + 8] = fm(gn_b_g[0], 8)
    pvh[:, PV_GBB:PV_GBB + 8] = fm(gn_b_b[0], 8)
    pvh[:, PV_DECK:PV_DECK + 16] = c["deck"]
    pvh[:, PV_BM:PV_BM + 16] = c["bm"]
    shared = {
        "w_in": np.ascontiguousarray(A(w_in)[0]), "w_out": np.ascontiguousarray(A(w_out)[0]),
        "ident_f": c["ident_f"], "prot_f": c["prot_f"], "cos_d": c["cos_d"], "sin_d": c["sin_d"],
        "dmask_d": c["dmask_d"], "decq_d": c["decq_d"], "pv_d": pvh,
        "lng_d": np.ascontiguousarray(A(ln_g)), "lnb_d": np.ascontiguousarray(A(ln_b)),
        "bones_d": c["bones_d"], "maskT4_d": c["maskT4_d"], "maskN_d": c["maskN_d"], "scanm_d": c["scanm_d"],
        "lora_d": np.ascontiguousarray(np.concatenate([A(w_up)[0], A(a_up)[0]], axis=0)),
    }
    sret = A(state_ret)[0]
    srw = A(state_rwkv)[0]
    ssh = A(state_shift)[0]
    in_maps = []
    for cid in range(8):
        b = cid % 4
        xall = np.concatenate([np.zeros((112, D), f32), meta, x_prompt[b],
                               x_sample[16 * cid:16 * cid + 16].reshape(128, D)], axis=0)
        m = dict(shared)
        m["xall"] = np.ascontiguousarray(xall)
        m["sret_d"] = np.ascontiguousarray(sret[16 * cid:16 * cid + 16])
        m["srwkv_d"] = np.ascontiguousarray(srw[16 * cid:16 * cid + 16])
        m["sshift_d"] = np.ascontiguousarray(ssh[16 * cid:16 * cid + 16])
        in_maps.append(m)
    res = run_bass_kernel_spmd(nc, in_maps, core_ids=list(range(8)))
    R = res.results
    y_prompt = np.stack([R[b]["yp"] for b in range(4)]).astype(f32)
    y_sample = np.concatenate([R[cid]["ys"].reshape(16, 8, D) for cid in range(8)], axis=0).astype(f32)
    rwkv_p = np.stack([R[b]["rwkv_p"] for b in range(4)])[None].astype(f32)
    shift_p = np.stack([R[b]["shift_p"][0] for b in range(4)])[None].astype(f32)
    ret_p = np.stack([R[b]["ret_p"] for b in range(4)])[None].astype(f32)
    rwkv_s = np.concatenate([R[cid]["rwkv_s"] for cid in range(8)], axis=0)[None].astype(f32)
    shift_s = np.concatenate([R[cid]["shift_s"] for cid in range(8)], axis=0)[None].astype(f32)
    ret_s = np.concatenate([R[cid]["ret_s"] for cid in range(8)], axis=0)[None].astype(f32)
    return (y_prompt, y_sample, rwkv_p, shift_p, ret_p, rwkv_s, shift_s, ret_s)
```
